# Optimizing a Trainium2 kernel written in Bass

```python
import math
import jax
import jax.numpy as jnp
from jax import lax
import numpy as np

D_MODEL = 2048
BATCH = 4
SEQ = 2048
DEPTH = 2
DEC_BATCH = 8
DEC_SEQ = 32
PAST_LEN = 2048

CHUNK = 64
MIX_WIDTH = D_MODEL // 2
SB_HEADS = 8
SB_HEAD_DIM = MIX_WIDTH // SB_HEADS
SB_BLOCK = 128
LRU_WIDTH = MIX_WIDTH
LRU_BLOCKS = 8
LRU_BLOCK_DIM = LRU_WIDTH // LRU_BLOCKS
LRU_CONV = 4
LRU_C = 8.0
SSD_INNER = MIX_WIDTH
SSD_HEAD_DIM = 64
SSD_HEADS = SSD_INNER // SSD_HEAD_DIM
SSD_GROUPS = 2
SSD_STATE = 128
SSD_CONV = 4
SSD_CONV_DIM = SSD_INNER + 2 * SSD_GROUPS * SSD_STATE
N_BRANCH = 3
N_MEM = 256
XA_HEADS = 4
XA_HEAD_DIM = D_MODEL // XA_HEADS
D_FF = 5632
FFN_CONV = 3
NORM_EPS = 1e-6

IN_SIZES = (MIX_WIDTH, MIX_WIDTH, MIX_WIDTH, LRU_WIDTH, LRU_WIDTH, SSD_INNER, SSD_CONV_DIM, SSD_HEADS, N_BRANCH * D_MODEL)
IN_COLS = sum(IN_SIZES)
IN_SPLITS = tuple(int(s) for s in np.cumsum(IN_SIZES)[:-1])

kernel_name = 'hybrid_stickbreak_rglru_ssd_streaming_step'


def rmsnorm(x, g):
    xf = x.astype(jnp.float32)
    y = xf * lax.rsqrt(jnp.mean(xf * xf, axis=-1, keepdims=True) + NORM_EPS)
    return (y * g.astype(jnp.float32)).astype(x.dtype)


def causal_dwconv(x, buf, w, b):
    K = w.shape[0]
    L = x.shape[1]
    xp = jnp.concatenate([buf.astype(x.dtype), x], axis=1)
    y = b
    for k in range(K):
        y = y + w[k] * xp[:, k:k + L]
    return y, xp[:, L:]


def sb_attend_block(qb, qpos, k, v, kpos):
    z = jnp.einsum('bqhd,bshd->bhqs', qb, k).astype(jnp.float32) * (SB_HEAD_DIM ** -0.5)
    valid = kpos[None, :] < qpos[:, None]
    lneg = jnp.where(valid, jax.nn.log_sigmoid(-z), 0.0)
    after = lax.cumsum(lneg, axis=3, reverse=True) - lneg
    w = jnp.where(valid, jnp.exp(jax.nn.log_sigmoid(z) + after), 0.0)
    return jnp.einsum('bhqs,bshd->bqhd', w.astype(v.dtype), v)


def stick_breaking(q, k, v, q_offset):
    B, L, H, D = q.shape
    kpos = jnp.arange(k.shape[1])
    qpos = q_offset + jnp.arange(L)
    if L <= SB_BLOCK or L % SB_BLOCK:
        return sb_attend_block(q, qpos, k, v, kpos)
    nb = L // SB_BLOCK
    qb = jnp.moveaxis(q.reshape(B, nb, SB_BLOCK, H, D), 1, 0)
    pb = qpos.reshape(nb, SB_BLOCK)
    out = lax.map(lambda a: sb_attend_block(a[0], a[1], k, v, kpos), (qb, pb))
    return jnp.moveaxis(out, 0, 1).reshape(B, L, H, D)


def rg_lru(xc, h0, wa, ba, wx, bx, lam):
    B, L, W = xc.shape
    f32 = jnp.float32
    xf = xc.astype(f32)
    xblk = xf.reshape(B, L, LRU_BLOCKS, LRU_BLOCK_DIM)
    r = jax.nn.sigmoid(jnp.einsum('blnk,nkj->blnj', xblk, wa.astype(f32)).reshape(B, L, W) + ba.astype(f32))
    i = jax.nn.sigmoid(jnp.einsum('blnk,nkj->blnj', xblk, wx.astype(f32)).reshape(B, L, W) + bx.astype(f32))
    log_a = LRU_C * r * jax.nn.log_sigmoid(lam.astype(f32))
    a = jnp.exp(log_a)
    u = jnp.sqrt(-jnp.expm1(2.0 * log_a)) * (i * xf)
    u = u.at[:, 0].add(a[:, 0] * h0.astype(f32))

    def comb(e1, e2):
        a1, b1 = e1
        a2, b2 = e2
        return a1 * a2, a2 * b1 + b2

    _, h = lax.associative_scan(comb, (a, u), axis=1)
    return h.astype(xc.dtype), h[:, -1]


def ssd_chunked(x, dt, a, bm, cm, h0):
    b, L, H, P = x.shape
    G, N = bm.shape[2], bm.shape[3]
    E = H // G
    Q = math.gcd(L, CHUNK)
    nc = L // Q
    f32 = jnp.float32
    xc = x.astype(f32).reshape(b, nc, Q, G, E, P)
    dtc = dt.astype(f32).reshape(b, nc, Q, G, E)
    bc = bm.astype(f32).reshape(b, nc, Q, G, N)
    cc = cm.astype(f32).reshape(b, nc, Q, G, N)
    cum = jnp.cumsum(dtc * a.reshape(G, E), axis=2)
    seg = cum[:, :, :, None] - cum[:, :, None, :]
    causal = jnp.tril(jnp.ones((Q, Q), dtype=bool))[:, :, None, None]
    decay = jnp.exp(jnp.where(causal, seg, -jnp.inf))
    cb = jnp.einsum('bctgn,bcsgn->bctsg', cc, bc)
    y_intra = jnp.einsum('bctsge,bcsgep->bctgep', cb[..., None] * decay * dtc[:, :, None], xc)
    decay_end = jnp.exp(cum[:, :, -1:] - cum)
    s_chunk = jnp.einsum('bcsgn,bcsgep->bcgepn', bc, (decay_end * dtc)[..., None] * xc)
    chunk_decay = jnp.exp(cum[:, :, -1])

    def step(h, inp):
        s_c, d_c = inp
        return h * d_c[..., None, None] + s_c, h

    h_last, h_in = lax.scan(step, h0.astype(f32).reshape(b, G, E, P, N),
                            (jnp.moveaxis(s_chunk, 1, 0), jnp.moveaxis(chunk_decay, 1, 0)))
    h_in = jnp.moveaxis(h_in, 0, 1)
    y_inter = jnp.einsum('bctgn,bcgepn->bctgep', cc, h_in) * jnp.exp(cum)[..., None]
    y = (y_intra + y_inter).reshape(b, L, H, P)
    return y.astype(x.dtype), h_last.reshape(b, H, P, N)


def hybrid_mixer(xn, past_k, past_v, lru_buf, lru_h0, ssd_buf, ssd_h0, lp):
    B, L, _ = xn.shape
    proj = xn @ lp['w_in']
    q, k, v, lx, lg, z, xbc, dt, gate_logits = jnp.split(proj, IN_SPLITS, axis=-1)
    q = q.reshape(B, L, SB_HEADS, SB_HEAD_DIM)
    k = k.reshape(B, L, SB_HEADS, SB_HEAD_DIM)
    v = v.reshape(B, L, SB_HEADS, SB_HEAD_DIM)
    P = past_k.shape[1]
    k_all = jnp.concatenate([past_k.astype(k.dtype), k], axis=1)
    v_all = jnp.concatenate([past_v.astype(v.dtype), v], axis=1)
    o_a = stick_breaking(q, k_all, v_all, P).reshape(B, L, MIX_WIDTH)
    lx_c, lru_buf_new = causal_dwconv(lx, lru_buf, lp['lru_conv_w'], lp['lru_conv_b'])
    h, lru_h = rg_lru(lx_c, lru_h0, lp['lru_wa'], lp['lru_ba'], lp['lru_wx'], lp['lru_bx'], lp['lru_lam'])
    o_b = h * jax.nn.gelu(lg)
    xbc_c, ssd_buf_new = causal_dwconv(xbc, ssd_buf, lp['ssd_conv_w'], lp['ssd_conv_b'])
    xbc_c = jax.nn.silu(xbc_c)
    xs, bm, cm = jnp.split(xbc_c, (SSD_INNER, SSD_INNER + SSD_GROUPS * SSD_STATE), axis=-1)
    dts = jax.nn.softplus(dt.astype(jnp.float32) + lp['ssd_dt_bias'].astype(jnp.float32))
    a = -jnp.exp(lp['ssd_a_log'].astype(jnp.float32))
    xh = xs.reshape(B, L, SSD_HEADS, SSD_HEAD_DIM)
    y, ssd_h = ssd_chunked(xh, dts, a, bm.reshape(B, L, SSD_GROUPS, SSD_STATE),
                           cm.reshape(B, L, SSD_GROUPS, SSD_STATE), ssd_h0)
    y = y + lp['ssd_d'][:, None] * xh
    yg = (y.reshape(B, L, SSD_INNER) * jax.nn.silu(z)).reshape(B, L, SSD_GROUPS, SSD_INNER // SSD_GROUPS)
    o_c = rmsnorm(yg, lp['ssd_norm'].reshape(SSD_GROUPS, SSD_INNER // SSD_GROUPS)).reshape(B, L, SSD_INNER)
    branches = jnp.stack([o_a, o_b, o_c], axis=2)
    gates = jax.nn.sigmoid(gate_logits.reshape(B, L, N_BRANCH, D_MODEL))
    merged = jnp.sum(jnp.einsum('blnk,nkd->blnd', branches, lp['w_branch']) * gates, axis=2)
    return merged @ lp['w_out'], (k, v, lru_buf_new, lru_h, ssd_buf_new, ssd_h)


def mem_kv(mem, g, wk, wv):
    B, M, _ = mem.shape
    mn = rmsnorm(mem, g)
    return ((mn @ wk).reshape(B, M, XA_HEADS, XA_HEAD_DIM), (mn @ wv).reshape(B, M, XA_HEADS, XA_HEAD_DIM))


def cross_attn(xn, mk, mv, wq, wo):
    B, L, _ = xn.shape
    q = (xn @ wq).reshape(B, L, XA_HEADS, XA_HEAD_DIM)
    s = jnp.einsum('blhd,bmhd->bhlm', q, mk.astype(q.dtype)).astype(jnp.float32) * (XA_HEAD_DIM ** -0.5)
    p = jax.nn.softmax(s, axis=-1).astype(q.dtype)
    o = jnp.einsum('bhlm,bmhd->blhd', p, mv.astype(q.dtype)).reshape(B, L, D_MODEL)
    return o @ wo


def conv_ffn(xn, buf, w_up, cw, cb, w_down):
    u, buf_new = causal_dwconv(xn @ w_up, buf, cw, cb)
    g, v = jnp.split(u, 2, axis=-1)
    return (jax.nn.silu(g) * v) @ w_down, buf_new


def trunk_layer(x, mk, mv, past_k, past_v, lru_buf, lru_h, ssd_buf, ssd_h, ffn_buf, lp):
    m, st = hybrid_mixer(rmsnorm(x, lp['norm_mix']), past_k, past_v, lru_buf, lru_h, ssd_buf, ssd_h, lp)
    h = x + m
    h = h + cross_attn(rmsnorm(h, lp['norm_xattn']), mk, mv, lp['w_xq'], lp['w_xo'])
    f, ffn_new = conv_ffn(rmsnorm(h, lp['norm_ffn']), ffn_buf, lp['w_up'], lp['ffn_conv_w'], lp['ffn_conv_b'], lp['w_down'])
    return h + f, st + (ffn_new,)


def setup_inputs(seed: int = 0) -> dict:
    key = jax.random.key(seed)
    ks = iter(jax.random.split(key, 64))

    def nrm(shape, scale):
        return jax.random.normal(next(ks), shape, jnp.float32) * scale

    def gain(shape):
        return 1.0 + nrm(shape, 0.02)

    def unif(shape, lo, hi):
        return jax.random.uniform(next(ks), shape, jnp.float32, lo, hi)

    dt0 = jnp.exp(unif((DEPTH, SSD_HEADS), math.log(1e-3), math.log(1e-1)))
    a_base = unif((DEPTH, LRU_WIDTH), 0.9, 0.999) ** (1.0 / LRU_C)
    a_log = jnp.log(unif((DEPTH, SSD_HEADS), 1.0, 16.0))
    return {
        'x_prompt': nrm((BATCH, SEQ, D_MODEL), 1.0),
        'x_sample': nrm((DEC_BATCH, DEC_SEQ, D_MODEL), 1.0),
        'mem_prompt': nrm((BATCH, N_MEM, D_MODEL), 1.0),
        'cache_sb_k': nrm((DEPTH, DEC_BATCH, PAST_LEN, SB_HEADS, SB_HEAD_DIM), 1.0),
        'cache_sb_v': nrm((DEPTH, DEC_BATCH, PAST_LEN, SB_HEADS, SB_HEAD_DIM), 1.0),
        'cache_mem_k': nrm((DEPTH, DEC_BATCH, N_MEM, XA_HEADS, XA_HEAD_DIM), 1.0),
        'cache_mem_v': nrm((DEPTH, DEC_BATCH, N_MEM, XA_HEADS, XA_HEAD_DIM), 1.0),
        'state_lru_conv': nrm((DEPTH, DEC_BATCH, LRU_CONV - 1, LRU_WIDTH), 1.0),
        'state_lru_h': nrm((DEPTH, DEC_BATCH, LRU_WIDTH), 0.5),
        'state_ssd_conv': nrm((DEPTH, DEC_BATCH, SSD_CONV - 1, SSD_CONV_DIM), 1.0),
        'state_ssd': nrm((DEPTH, DEC_BATCH, SSD_HEADS, SSD_HEAD_DIM, SSD_STATE), 0.1),
        'state_ffn_conv': nrm((DEPTH, DEC_BATCH, FFN_CONV - 1, 2 * D_FF), 1.0),
        'norm_mix': gain((DEPTH, D_MODEL)),
        'w_in': nrm((DEPTH, D_MODEL, IN_COLS), D_MODEL ** -0.5),
        'lru_conv_w': nrm((DEPTH, LRU_CONV, LRU_WIDTH), LRU_CONV ** -0.5),
        'lru_conv_b': nrm((DEPTH, LRU_WIDTH), 0.01),
        'lru_wa': nrm((DEPTH, LRU_BLOCKS, LRU_BLOCK_DIM, LRU_BLOCK_DIM), LRU_BLOCK_DIM ** -0.5),
        'lru_ba': nrm((DEPTH, LRU_WIDTH), 0.01),
        'lru_wx': nrm((DEPTH, LRU_BLOCKS, LRU_BLOCK_DIM, LRU_BLOCK_DIM), LRU_BLOCK_DIM ** -0.5),
        'lru_bx': nrm((DEPTH, LRU_WIDTH), 0.01),
        'lru_lam': jnp.log(a_base) - jnp.log1p(-a_base),
        'ssd_conv_w': nrm((DEPTH, SSD_CONV, SSD_CONV_DIM), SSD_CONV ** -0.5),
        'ssd_conv_b': nrm((DEPTH, SSD_CONV_DIM), 0.01),
        'ssd_dt_bias': dt0 + jnp.log(-jnp.expm1(-dt0)),
        'ssd_a_log': a_log,
        'ssd_d': gain((DEPTH, SSD_HEADS)),
        'ssd_norm': gain((DEPTH, SSD_INNER)),
        'w_branch': nrm((DEPTH, N_BRANCH, MIX_WIDTH, D_MODEL), MIX_WIDTH ** -0.5),
        'w_out': nrm((DEPTH, D_MODEL, D_MODEL), D_MODEL ** -0.5),
        'norm_xattn': gain((DEPTH, D_MODEL)),
        'norm_mem': gain((DEPTH, D_MODEL)),
        'w_xq': nrm((DEPTH, D_MODEL, D_MODEL), D_MODEL ** -0.5),
        'w_xk': nrm((DEPTH, D_MODEL, D_MODEL), D_MODEL ** -0.5),
        'w_xv': nrm((DEPTH, D_MODEL, D_MODEL), D_MODEL ** -0.5),
        'w_xo': nrm((DEPTH, D_MODEL, D_MODEL), D_MODEL ** -0.5),
        'norm_ffn': gain((DEPTH, D_MODEL)),
        'w_up': nrm((DEPTH, D_MODEL, 2 * D_FF), D_MODEL ** -0.5),
        'ffn_conv_w': nrm((DEPTH, FFN_CONV, 2 * D_FF), FFN_CONV ** -0.5),
        'ffn_conv_b': nrm((DEPTH, 2 * D_FF), 0.01),
        'w_down': nrm((DEPTH, D_FF, D_MODEL), D_FF ** -0.5),
        'norm_final': gain((D_MODEL,)),
    }


def reference(x_prompt, x_sample, mem_prompt, cache_sb_k, cache_sb_v, cache_mem_k, cache_mem_v,
              state_lru_conv, state_lru_h, state_ssd_conv, state_ssd, state_ffn_conv,
              norm_mix, w_in, lru_conv_w, lru_conv_b, lru_wa, lru_ba, lru_wx, lru_bx, lru_lam,
              ssd_conv_w, ssd_conv_b, ssd_dt_bias, ssd_a_log, ssd_d, ssd_norm, w_branch, w_out,
              norm_xattn, norm_mem, w_xq, w_xk, w_xv, w_xo, norm_ffn, w_up, ffn_conv_w, ffn_conv_b,
              w_down, norm_final):
    bp = x_prompt.shape[0]
    dtp = x_prompt.dtype
    empty_kv = jnp.zeros((bp, 0, SB_HEADS, SB_HEAD_DIM), dtp)
    zero_lru_conv = jnp.zeros((bp, LRU_CONV - 1, LRU_WIDTH), dtp)
    zero_lru_h = jnp.zeros((bp, LRU_WIDTH), jnp.float32)
    zero_ssd_conv = jnp.zeros((bp, SSD_CONV - 1, SSD_CONV_DIM), dtp)
    zero_ssd = jnp.zeros((bp, SSD_HEADS, SSD_HEAD_DIM, SSD_STATE), jnp.float32)
    zero_ffn_conv = jnp.zeros((bp, FFN_CONV - 1, 2 * D_FF), dtp)

    yp, ys = x_prompt, x_sample
    st_p, st_s, mk_list, mv_list = [], [], [], []
    for l in range(DEPTH):
        lp = dict(norm_mix=norm_mix[l], w_in=w_in[l], lru_conv_w=lru_conv_w[l], lru_conv_b=lru_conv_b[l],
                  lru_wa=lru_wa[l], lru_ba=lru_ba[l], lru_wx=lru_wx[l], lru_bx=lru_bx[l], lru_lam=lru_lam[l],
                  ssd_conv_w=ssd_conv_w[l], ssd_conv_b=ssd_conv_b[l], ssd_dt_bias=ssd_dt_bias[l],
                  ssd_a_log=ssd_a_log[l], ssd_d=ssd_d[l], ssd_norm=ssd_norm[l], w_branch=w_branch[l],
                  w_out=w_out[l], norm_xattn=norm_xattn[l], w_xq=w_xq[l], w_xo=w_xo[l], norm_ffn=norm_ffn[l],
                  w_up=w_up[l], ffn_conv_w=ffn_conv_w[l], ffn_conv_b=ffn_conv_b[l], w_down=w_down[l])
        mk_p, mv_p = mem_kv(mem_prompt, norm_mem[l], w_xk[l], w_xv[l])
        yp, sp = trunk_layer(yp, mk_p, mv_p, empty_kv, empty_kv, zero_lru_conv, zero_lru_h,
                             zero_ssd_conv, zero_ssd, zero_ffn_conv, lp)
        ys, ss = trunk_layer(ys, cache_mem_k[l], cache_mem_v[l], cache_sb_k[l], cache_sb_v[l],
                             state_lru_conv[l], state_lru_h[l], state_ssd_conv[l], state_ssd[l],
                             state_ffn_conv[l], lp)
        st_p.append(sp)
        st_s.append(ss)
        mk_list.append(mk_p)
        mv_list.append(mv_p)
    y_prompt = rmsnorm(yp, norm_final)
    y_sample = rmsnorm(ys, norm_final)

    def stk(lst, i):
        return jnp.stack([s[i] for s in lst], axis=0)

    return (y_prompt, y_sample,
            stk(st_p, 0), stk(st_p, 1), stk(st_p, 2), stk(st_p, 3), stk(st_p, 4), stk(st_p, 5), stk(st_p, 6),
            jnp.stack(mk_list, axis=0), jnp.stack(mv_list, axis=0),
            stk(st_s, 0), stk(st_s, 1), stk(st_s, 2), stk(st_s, 3), stk(st_s, 4), stk(st_s, 5), stk(st_s, 6))
```

```python
import contextlib
import numpy as np
import concourse.bass as bass
import concourse.mybir as mybir
from concourse.bass_utils import run_bass_kernel_spmd

F32 = mybir.dt.float32
BF16 = mybir.dt.bfloat16
AF = mybir.ActivationFunctionType
ALU = mybir.AluOpType
AX = mybir.AxisListType

D = 2048
KC = D // 128
MIX = 1024
NH = 8
DFF = 5632
IN_COLS = 13840
NMEM = 256
EPS = 1e-6
DEPTH = 2


class Buf:
    __slots__ = ("name", "w", "r", "serial")

    def __init__(self, name):
        self.name = name
        self.serial = name.startswith("ps")
        self.w = None
        self.r = []


class Prog:
    ENG = ("pe", "act", "dve", "pool", "sp")

    def __init__(self, nc, es, n_dma_sems=60):
        self.nc = nc
        self.es = es
        self.lists = {e: [] for e in self.ENG}
        self.sems = []
        self.tot = []
        self.isdma = []
        self.eng_sem = {}
        for e in ("pe", "act", "dve", "pool"):
            self.eng_sem[e] = self._new_sem("e_" + e, False)
        self.dma_pool = [self._new_sem("d%d" % i, True) for i in range(n_dma_sems)]
        self.dma_next = 0
        self.sw_pool = [self._new_sem("w%d" % i, True) for i in range(20)]
        self.wsems = [self._new_sem("ws%d" % i, True) for i in range(4)]
        self.sw_next = 0
        self.waited = {e: {} for e in self.ENG}
        self.bufs = []
        self.same_engine_sync = True

    def _new_sem(self, name, isdma):
        s = self.es.enter_context(self.nc.semaphore(name))
        self.sems.append(s)
        self.tot.append(0)
        self.isdma.append(isdma)
        return len(self.sems) - 1

    def buf(self, name):
        b = Buf(name)
        self.bufs.append(b)
        return b

    def bufs_n(self, name, n):
        return [self.buf("%s%d" % (name, i)) for i in range(n)]

    def dma_sem(self, q="sp"):
        if q == "pool":
            s = self.sw_pool[self.sw_next % len(self.sw_pool)]
            self.sw_next += 1
            return s
        s = self.dma_pool[self.dma_next % len(self.dma_pool)]
        self.dma_next += 1
        return s

    def _deps(self, eng, reads, writes):
        deps = []
        for b in reads:
            if b.w is not None:
                deps.append(b.w)
            if b.serial:
                deps.extend(b.r)
        for b in writes:
            if b.w is not None:
                deps.append(b.w)
            deps.extend(b.r)
        waits = []
        wd = self.waited[eng]
        for (s, v) in deps:
            if self.isdma[s]:
                v = self.tot[s]
            elif (not self.same_engine_sync or eng == "pe") and self.eng_sem.get(eng) == s:
                continue
            if wd.get(s, 0) >= v:
                continue
            wd[s] = v
            waits.append((s, v))
        return waits

    def _commit(self, ev, reads, writes):
        for b in reads:
            b.r.append(ev)
        for b in writes:
            b.w = ev
            b.r = []

    def op(self, eng, reads, writes, fn):
        waits = self._deps(eng, reads, writes)
        s = self.eng_sem[eng]
        self.tot[s] += 1
        ev = (s, self.tot[s])
        self.lists[eng].append((waits, fn, s, 1))
        self._commit(ev, reads, writes)

    def dma(self, q, out, in_, reads, writes, sem=None, **kw):
        waits = self._deps(q, reads, writes)
        s = self.dma_sem(q) if sem is None else sem
        self.tot[s] += 16
        ev = (s, self.tot[s])

        def fn(e, out=out, in_=in_, kw=kw):
            return e.dma_start(out=out, in_=in_, **kw)
        self.lists[q].append((waits, fn, s, 16))
        self._commit(ev, reads, writes)

    def barrier(self):
        for e in self.ENG:
            wd = self.waited[e]
            waits = []
            for s in range(len(self.sems)):
                v = self.tot[s]
                if v > 0 and wd.get(s, 0) < v:
                    wd[s] = v
                    waits.append((s, v))
            if waits:
                self.lists[e].append((waits, None, None, 0))
        for b in self.bufs:
            b.w = None
            b.r = []

    def emit(self, final_wait_eng="sp"):
        nc = self.nc
        self.barrier()
        engmap = {"pe": "tensor", "act": "scalar", "dve": "vector", "pool": "gpsimd", "sp": "sync"}
        with nc.Block() as block:
            for e in self.ENG:
                lst = self.lists[e]

                def body(eng, lst=lst):
                    for (waits, fn, s, amt) in lst:
                        for (ws, wv) in waits:
                            eng.wait_ge(self.sems[ws], wv)
                        if fn is not None:
                            ins = fn(eng)
                            ins.then_inc(self.sems[s], amt)
                getattr(block, engmap[e])(body)


class Arena:
    def __init__(self, nc, es, name, words):
        self.t = es.enter_context(nc.sbuf_tensor(name, [128, words], F32))
        self.words = words
        self.off = 0
        self.base = 0

    def reset(self):
        self.off = self.base

    def pin(self):
        self.base = self.off

    def f32(self, n):
        assert self.off + n <= self.words, ("arena overflow", self.off, n, self.words)
        ap = self.t[:, self.off:self.off + n]
        self.off += n
        return ap

    def bf16(self, n):
        w = (n + 1) // 2
        ap = self.f32(w).bitcast(BF16)
        return ap[:, 0:n]


def build(cfg):
    S = cfg["S"]
    R = cfg["R"]
    PAST = cfg["PAST"]
    debug = cfg.get("debug", False)
    T = S + R
    NB = S // 128
    tblocks = [(i * 128, 128) for i in range(NB)] + [(S, R)]
    tchunks = [(i * 512, min(512, S - i * 512)) for i in range((S + 511) // 512)] + [(S, R)]

    nc = bass.Bass("TRN2", target_bir_lowering=False)
    es = contextlib.ExitStack()
    P = Prog(nc, es)

    def din(name, shape, dt=F32):
        return nc.dram_tensor(name, list(shape), dt, kind="ExternalInput").ap()

    def dout(name, shape, dt=F32):
        return nc.dram_tensor(name, list(shape), dt, kind="ExternalOutput").ap()

    def dscr(name, shape, dt=BF16):
        return nc.dram_tensor(name, list(shape), dt, kind=("ExternalOutput" if debug else "Internal")).ap()

    I = {}
    I["x_prompt"] = din("x_prompt", [S, D])
    I["x_sample"] = din("x_sample", [R, D])
    I["norm_mix"] = din("norm_mix", [DEPTH, D])
    I["w_in"] = din("w_in", [DEPTH, D, IN_COLS])
    I["ident"] = din("ident", [128, 128])
    I["maskL"] = din("maskL", [128, 128])
    for nm, shp in [("w_branch", [DEPTH, 3, MIX, D]), ("w_out", [DEPTH, D, D]), ("norm_xattn", [DEPTH, D]), ("norm_mem", [DEPTH, D]),
                    ("w_xq", [DEPTH, D, D]), ("w_xk", [DEPTH, D, D]), ("w_xv", [DEPTH, D, D]), ("w_xo", [DEPTH, D, D]),
                    ("norm_ffn", [DEPTH, D]), ("w_up", [DEPTH, D, 2 * DFF]), ("ffn_conv_w", [DEPTH, 3, 2 * DFF]),
                    ("ffn_conv_b", [DEPTH, 2 * DFF]), ("w_down", [DEPTH, DFF, D]), ("norm_final", [1, D]),
                    ("mem_prompt", [NMEM, D]), ("cache_mem_k", [DEPTH, NMEM, D]), ("cache_mem_v", [DEPTH, NMEM, D]),
                    ("state_ffn_conv", [DEPTH, 2, 2 * DFF])]:
        I[nm] = din(nm, shp)
    for nm, shp in [("ssd_conv_w", [DEPTH, 4, 1536]), ("ssd_conv_b", [DEPTH, 1536]), ("ssd_dt_bias", [DEPTH, 16]),
                    ("ssd_a_log", [DEPTH, 16]), ("ssd_d", [DEPTH, 16]), ("ssd_norm", [DEPTH, MIX]),
                    ("state_ssd_conv", [DEPTH, 3, 1536]), ("state_ssd", [DEPTH, 16, 64, 128]),
                    ("negmask", [128, 128]), ("selh", [16, 16 * 128]), ("sel128", [128, 128]), ("selR", [128, 128])]:
        I[nm] = din(nm, shp)
    for nm, shp in [("lru_conv_w", [DEPTH, 4, MIX]), ("lru_conv_b", [DEPTH, MIX]), ("lru_wa", [DEPTH, 8, 128, 128]),
                    ("lru_ba", [DEPTH, MIX]), ("lru_wx", [DEPTH, 8, 128, 128]), ("lru_bx", [DEPTH, MIX]),
                    ("lru_lam", [DEPTH, MIX]), ("state_lru_conv", [DEPTH, 3, MIX]), ("state_lru_h", [DEPTH, MIX])]:
        I[nm] = din(nm, shp)
    I["cache_sb_k"] = din("cache_sb_k", [DEPTH, PAST, NH, 128])
    I["cache_sb_v"] = din("cache_sb_v", [DEPTH, PAST, NH, 128])

    O = {}
    O["sb_k_prompt"] = dout("sb_k_prompt", [DEPTH, S, MIX])
    O["sb_v_prompt"] = dout("sb_v_prompt", [DEPTH, S, MIX])
    O["sb_k_sample"] = dout("sb_k_sample", [DEPTH, R, MIX])
    O["sb_v_sample"] = dout("sb_v_sample", [DEPTH, R, MIX])

    for nm, shp in [("lru_conv_prompt", [DEPTH, 3, MIX]), ("lru_h_prompt", [DEPTH, MIX]),
                    ("lru_conv_sample", [DEPTH, 3, MIX]), ("lru_h_sample", [DEPTH, MIX])]:
        O[nm] = dout(nm, shp)
    for nm, shp in [("ssd_conv_prompt", [DEPTH, 3, 1536]), ("ssd_state_prompt", [DEPTH, 16, 64, 128]),
                    ("ssd_conv_sample", [DEPTH, 3, 1536]), ("ssd_state_sample", [DEPTH, 16, 64, 128])]:
        O[nm] = dout(nm, shp)
    for nm, shp in [("y_prompt", [S, D]), ("y_sample", [R, D]), ("ffn_conv_prompt", [DEPTH, 2, 2 * DFF]),
                    ("ffn_conv_sample", [DEPTH, 2, 2 * DFF]), ("mem_k_prompt", [DEPTH, NMEM, D]), ("mem_v_prompt", [DEPTH, NMEM, D])]:
        O[nm] = dout(nm, shp)
    actd = dscr("actd", [DFF, T])
    pj = dscr("pj", [IN_COLS + 112, T])
    vtok = dscr("vtok", [T, MIX])
    ztok = dscr("ztok", [T, MIX])
    brT = dscr("brT", [3, MIX, T])

    ar = Arena(nc, es, "arena", 46 * 1024)
    psum = es.enter_context(nc.psum_tensor("psum", [128, 4096], F32))
    pbank = [psum[:, i * 512:(i + 1) * 512] for i in range(8)]
    pbuf = P.bufs_n("ps", 8)
    P.pin_bufs = list(pbuf)

    ident_bf = ar.bf16(128)
    ident_f = ar.f32(128)
    b_const = P.buf("const")
    P.dma("pool", ident_bf, I["ident"], [], [b_const])
    P.dma("sp", ident_f, I["ident"], [], [b_const])
    maskL = ar.f32(128)
    P.dma("sp", maskL, I["maskL"], [], [b_const])
    ar.pin()
    P.barrier()
    pbuf = P.bufs_n("ps", 8)

    state = {"pbi": 0}
    res = dscr("res", [T, D], F32)

    def res_src(l, t0, n, c0=0, ncol=D):
        if l == 0 and not state.get("res_live"):
            if t0 < S:
                return I["x_prompt"][t0:t0 + n, c0:c0 + ncol]
            return I["x_sample"][t0 - S:t0 - S + n, c0:c0 + ncol]
        return res[t0:t0 + n, c0:c0 + ncol]

    def split_range(g0, gn, step):
        out = []
        t = g0
        while t < g0 + gn:
            lim = S if t < S else g0 + gn
            n = min(step, lim - t, g0 + gn - t)
            out.append((t, n))
            t += n
        return out

    def gemm_fm2(w_ap, w_buf, ncols, xT, x_bufs, nk, g0, chunks, evac):
        nsub = (ncols + 127) // 128
        for (t0, n) in chunks:
            for sub in range(nsub):
                m = min(128, ncols - sub * 128)
                pb = next_bank()

                def mm(e, sub=sub, m=m, t0=t0, n=n, pb=pb):
                    ins = None
                    for k in range(nk):
                        ins = e.matmul(pbank[pb][0:m, 0:n], w_ap[:, k, sub * 128:sub * 128 + m],
                                       xT[:, k, t0 - g0:t0 - g0 + n], start=(k == 0), stop=(k == nk - 1))
                    return ins
                P.op("pe", [w_buf] + x_bufs, [pbuf[pb]], mm)
                evac(pbank[pb], pbuf[pb], sub, t0, n, m)

    def gemm_tm2(w_ap, w_buf, ncols, xT, x_bufs, nk, g0, blocks, evac):
        for (t0, n) in blocks:
            pb = next_bank()

            def mm(e, t0=t0, n=n, pb=pb):
                ins = None
                for k in range(nk):
                    ins = e.matmul(pbank[pb][0:n, 0:ncols], xT[:, k, t0 - g0:t0 - g0 + n], w_ap[:, k, 0:ncols],
                                   start=(k == 0), stop=(k == nk - 1))
                return ins
            P.op("pe", [w_buf] + x_bufs, [pbuf[pb]], mm)
            evac(pbank[pb], pbuf[pb], t0, n)

    def next_bank():
        i = state["pbi"] % 8
        state["pbi"] += 1
        return i

    def phase_norm(blocks, gain_ap, xnT, xnT_bufs, gB, gB_buf, work, final_dst=None):
        P.dma("sp", gB, gain_ap.partition_broadcast(128), [], [gB_buf])
        xt, xb, sq, st = work
        for bi, blk in enumerate(blocks):
            t0, n, src = blk[0], blk[1], blk[2]
            j = bi % 2
            P.dma("sp", xt["ap"][j][0:n, :], src, [], [xt["b"][j]])
            P.op("act", [xt["b"][j]], [sq["b"][j], st["b"][j]],
                 lambda e, j=j, n=n: e.activation(out=sq["ap"][j][0:n, :], in_=xt["ap"][j][0:n, :],
                                                  func=AF.Square, accum_out=st["ap"][j][0:n, 0:1]))
            P.op("dve", [st["b"][j]], [st["b"][j]],
                 lambda e, j=j, n=n: e.tensor_scalar(st["ap"][j][0:n, 1:2], st["ap"][j][0:n, 0:1],
                                                     1.0 / D, EPS, ALU.mult, ALU.add))
            P.op("act", [st["b"][j]], [st["b"][j]],
                 lambda e, j=j, n=n: e.activation(out=st["ap"][j][0:n, 2:3], in_=st["ap"][j][0:n, 1:2],
                                                  func=AF.Sqrt))
            P.op("dve", [st["b"][j]], [st["b"][j]],
                 lambda e, j=j, n=n: e.reciprocal(st["ap"][j][0:n, 3:4], st["ap"][j][0:n, 2:3]))
            if len(blk) > 3:
                P.op("dve", [st["b"][j], xt["b"][j], gB_buf], [sq["b"][j]],
                     lambda e, j=j, n=n: e.scalar_tensor_tensor(sq["ap"][j][0:n, :], xt["ap"][j][0:n, :],
                                                                st["ap"][j][0:n, 3:4], gB[0:n, :],
                                                                ALU.mult, ALU.mult))
                P.dma("sp", blk[3], sq["ap"][j][0:n, :], [sq["b"][j]], [])
                continue
            P.op("dve", [st["b"][j], xt["b"][j], gB_buf], [xb["b"][j]],
                 lambda e, j=j, n=n: e.scalar_tensor_tensor(xb["ap"][j][0:n, :], xt["ap"][j][0:n, :],
                                                            st["ap"][j][0:n, 3:4], gB[0:n, :],
                                                            ALU.mult, ALU.mult))
            for half in range(2):
                pb = next_bank()
                pv = pbank[pb].bitcast(BF16)

                def tr(e, j=j, n=n, half=half, pv=pv):
                    ins = None
                    for c in range(8):
                        k = half * 8 + c
                        ins = e.transpose(pv[:, c * 128:c * 128 + n], xb["ap"][j][0:n, k * 128:(k + 1) * 128],
                                          ident_bf[0:n, 0:n])
                    return ins
                P.op("pe", [xb["b"][j], b_const], [pbuf[pb]], tr)
                eng = "act" if half == 0 else "dve"

                def ev(e, n=n, half=half, pv=pv, t0=t0, eng=eng):
                    src = pv.rearrange("p (c t) -> p c t", c=8)[:, :, 0:n]
                    dst = xnT[:, half * 8:(half + 1) * 8, t0:t0 + n]
                    if eng == "act":
                        return e.copy(out=dst, in_=src)
                    return e.tensor_copy(dst, src)
                P.op(eng, [pbuf[pb]], [xnT_bufs[bi]], ev)

    wrot = {"n": 0}

    def load_w2(wsrc, c0, ncols, nkc, wbf):
        nb = len(wbf["ap"])
        j = wrot["n"] % nb
        wrot["n"] += 1
        P.dma("pool", wbf["ap"][j][:, 0:nkc, 0:ncols], wsrc.rearrange("(k p) n -> p k n", p=128)[:, :, c0:c0 + ncols],
              [], [wbf["b"][j]], sem=P.wsems[j])
        return wbf["ap"][j], wbf["b"][j]

    def AN(fn, nm, n):
        return {"ap": [fn() for _ in range(n)], "b": P.bufs_n(nm, n)}

    def gemm_fm(wbf_ap, wbf_buf, ncols, xT, x_bufs, nk, evac):
        nsub = (ncols + 127) // 128
        for ti, (t0, n) in enumerate(tchunks):
            for sub in range(nsub):
                m = min(128, ncols - sub * 128)
                pb = next_bank()

                def mm(e, sub=sub, m=m, t0=t0, n=n, pb=pb):
                    ins = None
                    for k in range(nk):
                        ins = e.matmul(pbank[pb][0:m, 0:n], wbf_ap[:, k, sub * 128:sub * 128 + m],
                                       xT[:, k, t0:t0 + n], start=(k == 0), stop=(k == nk - 1))
                    return ins
                P.op("pe", [wbf_buf] + x_bufs, [pbuf[pb]], mm)
                evac(pbank[pb], pbuf[pb], sub, ti, t0, n, m)

    def gemm_tm(wbf_ap, wbf_buf, ncols, xT, x_bufs, nk, evac):
        for bi, (t0, n) in enumerate(tblocks):
            pb = next_bank()

            def mm(e, t0=t0, n=n, pb=pb):
                ins = None
                for k in range(nk):
                    ins = e.matmul(pbank[pb][0:n, 0:ncols], xT[:, k, t0:t0 + n], wbf_ap[:, k, 0:ncols],
                                   start=(k == 0), stop=(k == nk - 1))
                return ins
            P.op("pe", [wbf_buf, x_bufs[bi]], [pbuf[pb]], mm)
            evac(pbank[pb], pbuf[pb], bi, t0, n)

    def dbl(alloc, name):
        aps = [alloc(), alloc()]
        return {"ap": aps, "b": P.bufs_n(name, 2)}

    for l in range(cfg["depth"]):
        ar.reset()
        xnT = ar.bf16(KC * T).rearrange("p (k t) -> p k t", k=KC)
        xnT_bufs = P.bufs_n("xnT", len(tblocks))
        mark = ar.off
        gB = ar.f32(D)
        gB_buf = P.buf("gB")
        work = (dbl(lambda: ar.f32(D), "xt"), dbl(lambda: ar.bf16(D), "xb"),
                dbl(lambda: ar.f32(D), "sq"), dbl(lambda: ar.f32(4), "st"))
        phase_norm([(t0, n, res_src(l, t0, n)) for (t0, n) in tblocks], I["norm_mix"][l:l + 1, :], xnT, xnT_bufs, gB, gB_buf, work)
        P.barrier()
        pbuf = P.bufs_n("ps", 8)
        xnT_bufs = P.bufs_n("xnT", len(tblocks))
        ar.off = mark
        wbf = AN(lambda: ar.bf16(KC * 512).rearrange("p (k n) -> p k n", k=KC), "wbf", 3)
        ofm = dbl(lambda: ar.bf16(4 * 512).rearrange("p (s n) -> p s n", s=4), "ofm")
        otm = dbl(lambda: ar.f32(512), "otm")
        otb = dbl(lambda: ar.bf16(512), "otb")
        cnt = {"fm": 0, "tm": 0, "dt": 0}
        dtr = dbl(lambda: ar.bf16(512), "dtr")
        ncb = (IN_COLS + 511) // 512
        for cb in range(cfg.get("ncb", ncb)):
            c0 = cb * 512
            ncols = min(512, IN_COLS - c0)
            wap, wbuf = load_w2(I["w_in"][l], c0, ncols, KC, wbf)
            is_gate = c0 >= 7680
            tm_kind = None
            if 1024 <= c0 < 2048:
                tm_kind = "k"
            elif 2048 <= c0 < 3072:
                tm_kind = "v"
            elif 5120 <= c0 < 6144:
                tm_kind = "z"
            if tm_kind != "v" and tm_kind != "z":
                nsub = (ncols + 127) // 128

                def evac(ps_ap, ps_buf, sub, ti, t0, n, m, c0=c0, nsub=nsub, is_gate=is_gate):
                    jj = cnt["fm"] % 2
                    if c0 == 7680 and sub == 0:
                        jd = cnt["dt"] % 2
                        cnt["dt"] += 1
                        P.op("dve", [ps_buf], [dtr["b"][jd]],
                             lambda e: e.tensor_copy(dtr["ap"][jd][0:16, 0:n], ps_ap[0:16, 0:n]))
                        P.dma("sp", pj[IN_COLS:IN_COLS + 16, t0:t0 + n], dtr["ap"][jd][0:16, 0:n], [dtr["b"][jd]], [])
                    eng = "act" if (sub % 2 == 0 or is_gate) else "dve"
                    dst = ofm["ap"][jj][0:m, sub, 0:n]
                    wr = [ofm["b"][jj]]
                    if eng == "act":
                        P.op("act", [ps_buf], wr,
                             lambda e: e.activation(out=dst, in_=ps_ap[0:m, 0:n],
                                                    func=(AF.Sigmoid if is_gate else AF.Copy)))
                    else:
                        P.op("dve", [ps_buf], wr, lambda e: e.tensor_copy(dst, ps_ap[0:m, 0:n]))
                    if sub == nsub - 1:
                        if m == 128:
                            P.dma("sp", pj[c0:c0 + nsub * 128, t0:t0 + n].rearrange("(s p) t -> p s t", p=128),
                                  ofm["ap"][jj][:, 0:nsub, 0:n], [ofm["b"][jj]], [])
                        else:
                            if nsub > 1:
                                P.dma("sp", pj[c0:c0 + (nsub - 1) * 128, t0:t0 + n].rearrange("(s p) t -> p s t", p=128),
                                      ofm["ap"][jj][:, 0:nsub - 1, 0:n], [ofm["b"][jj]], [])
                            P.dma("sp", pj[c0 + (nsub - 1) * 128:c0 + (nsub - 1) * 128 + m, t0:t0 + n],
                                  ofm["ap"][jj][0:m, nsub - 1, 0:n], [ofm["b"][jj]], [])
                        cnt["fm"] += 1
                gemm_fm(wap, wbuf, ncols, xnT, xnT_bufs, KC, evac)
            if tm_kind is not None:
                def evac_t(ps_ap, ps_buf, bi, t0, n, c0=c0, tm_kind=tm_kind):
                    jj = cnt["tm"] % 2
                    cnt["tm"] += 1
                    import os
                    kd = os.environ.get("KDBG", "")
                    if tm_kind in ("k", "v") and "noact" not in kd:
                        P.op("act", [ps_buf], [otm["b"][jj]],
                             lambda e: e.copy(out=otm["ap"][jj][0:n, :], in_=ps_ap[0:n, 0:512]))
                        cc = c0 - (1024 if tm_kind == "k" else 2048)
                        if t0 < S:
                            dst = O["sb_%s_prompt" % tm_kind][l, t0:t0 + n, cc:cc + 512]
                        else:
                            dst = O["sb_%s_sample" % tm_kind][l, 0:n, cc:cc + 512]
                        P.dma("sp", dst, otm["ap"][jj][0:n, :], [otm["b"][jj]], [])
                    if tm_kind in ("v", "z") and "nodve" not in kd:
                        if tm_kind == "v":
                            P.op("dve", [otm["b"][jj]], [otb["b"][jj]],
                                 lambda e: e.tensor_copy(otb["ap"][jj][0:n, :], otm["ap"][jj][0:n, :]))
                        else:
                            P.op("dve", [ps_buf], [otb["b"][jj]],
                                 lambda e: e.tensor_copy(otb["ap"][jj][0:n, :], ps_ap[0:n, 0:512]))
                        cc = c0 - (2048 if tm_kind == "v" else 5120)
                        dd = vtok if tm_kind == "v" else ztok
                        P.dma("sp", dd[t0:t0 + n, cc:cc + 512], otb["ap"][jj][0:n, :], [otb["b"][jj]], [])
                gemm_tm(wap, wbuf, 512, xnT, xnT_bufs, KC, evac_t)
        P.barrier()

        if cfg.get("do_A", True):
            ar.reset()
            PB = PAST // 128
            NK = max(S, PAST + R)
            NKB = (NK + 127) // 128
            ones = ar.f32(NK)
            b_ones = P.buf("ones")
            P.op("dve", [], [b_ones], lambda e: e.memset(ones, 1.0))
            hd = []
            for j in range(2):
                hd.append(dict(
                    qT=ar.bf16(T), kT=ar.bf16(S), kTs=ar.bf16(PAST + R),
                    vh=ar.bf16(NB * 128).rearrange("p (b d) -> p b d", b=NB),
                    vs=ar.bf16((PB + 1) * 128).rearrange("p (b d) -> p b d", b=PB + 1),
                    ck=ar.bf16(PB * 128).rearrange("p (b d) -> p b d", b=PB),
                    oT=ar.bf16(T),
                    b_q=P.buf("hq"), b_k=P.buf("hk"), b_ks=P.buf("hks"), b_v=P.buf("hv"), b_vs=P.buf("hvs"),
                    b_ck=P.buf("hck"), b_o=P.buf("ho")))
            wk = []
            NWK = 3
            for j in range(NWK):
                d_ = dict(E2=ar.f32(NK), c=ar.f32(NK), rec=ar.f32(NK),
                          A=ar.bf16(NK), AT=ar.bf16(NKB * 128).rearrange("p (b q) -> p b q", b=NKB),
                          nt=ar.f32(2),
                          b_E2=P.buf("wE2"), b_c=P.buf("wc"),
                          b_rec=P.buf("wrec"), b_A=P.buf("wA"), b_AT=P.buf("wAT"), b_nt=P.buf("wnt"))
                d_["W"] = d_["E2"]
                d_["b_W"] = d_["b_E2"]
                wk.append(d_)
            acnt = {"n": 0}
            scale = 128 ** -0.5

            def attend(qT_ap, b_q, nq, kT_ap, b_k, nk, vblocks, b_v, oT_ap, b_o):
                w = wk[acnt["n"] % NWK]
                acnt["n"] += 1
                nd = nq
                for c0 in range(0, nk, 512):
                    n = min(512, nk - c0)
                    pb = next_bank()
                    P.op("pe", [b_q, b_k], [pbuf[pb]],
                         lambda e, pb=pb, c0=c0, n=n: e.matmul(pbank[pb][0:nq, 0:n], qT_ap, kT_ap[:, c0:c0 + n],
                                                               start=True, stop=True))
                    P.op("act", [pbuf[pb]], [w["b_E2"]],
                         lambda e, pb=pb, c0=c0, n=n: e.activation(out=w["E2"][0:nq, c0:c0 + n], in_=pbank[pb][0:nq, 0:n],
                                                                   func=AF.Exp, scale=scale))
                    P.op("act", [pbuf[pb]], [w["b_rec"]],
                         lambda e, pb=pb, c0=c0, n=n: e.activation(out=w["rec"][0:nq, c0:c0 + n], in_=pbank[pb][0:nq, 0:n],
                                                                   func=AF.Exp, scale=-scale))
                P.op("act", [w["b_E2"]], [w["b_W"]],
                     lambda e: e.activation(out=w["W"][0:nq, 0:nk], in_=w["E2"][0:nq, 0:nk], func=AF.Ln, bias=1.0))
                P.op("act", [w["b_rec"]], [w["b_rec"]],
                     lambda e: e.activation(out=w["rec"][0:nq, 0:nk], in_=w["rec"][0:nq, 0:nk], func=AF.Ln, bias=1.0))
                fns = [lambda: attend_s1b(w, nq, nk, nd), lambda: attend_s2(w, nq, vblocks, b_v, oT_ap, b_o)]
                for item in reversed(list(inflight)):
                    item.pop(0)()
                    if not item:
                        inflight.remove(item)
                inflight.append(fns)

            inflight = []

            def attend_flush():
                while inflight:
                    for item in list(inflight):
                        item.pop(0)()
                        if not item:
                            inflight.remove(item)

            def attend_s1b(w, nq, nk, nd):
                P.op("dve", [w["b_W"], b_const], [w["b_W"]],
                     lambda e: e.tensor_tensor(w["W"][0:nq, nk - nd:nk], w["W"][0:nq, nk - nd:nk],
                                               maskL[0:nq, 0:nd], ALU.mult))
                P.op("dve", [w["b_W"], b_ones], [w["b_c"]],
                     lambda e: e.tensor_tensor_scan(w["c"][0:nq, 0:nk], ones[0:nq, 0:nk], w["W"][0:nq, 0:nk],
                                                    0.0, ALU.mult, ALU.add))
                P.op("dve", [w["b_c"]], [w["b_nt"]],
                     lambda e: e.tensor_scalar_mul(w["nt"][0:nq, 0:1], w["c"][0:nq, nk - 1:nk], -1.0))
                P.op("dve", [w["b_c"], w["b_rec"]], [w["b_c"]],
                     lambda e: e.tensor_tensor(w["c"][0:nq, 0:nk], w["c"][0:nq, 0:nk], w["rec"][0:nq, 0:nk], ALU.subtract))
                P.op("act", [w["b_c"], w["b_nt"]], [w["b_A"]],
                     lambda e: e.activation(out=w["A"][0:nq, 0:nk], in_=w["c"][0:nq, 0:nk], func=AF.Exp,
                                            bias=w["nt"][0:nq, 0:1]))
                P.op("dve", [w["b_A"], b_const], [w["b_A"]],
                     lambda e: e.tensor_tensor(w["A"][0:nq, nk - nd:nk], w["A"][0:nq, nk - nd:nk],
                                               maskL[0:nq, 0:nd], ALU.mult))

            def attend_s2(w, nq, vblocks, b_v, oT_ap, b_o):
                nblk = len(vblocks)
                groups = []
                for i in range(nblk):
                    if groups and len(groups[-1]) < 8 and vblocks[groups[-1][0]][2] == vblocks[i][2]:
                        groups[-1].append(i)
                    else:
                        groups.append([i])
                for gi, grp in enumerate(groups):
                    g0, gn = grp[0], len(grp)
                    pb = next_bank()
                    pv = pbank[pb].bitcast(BF16)

                    def tr(e, g0=g0, gn=gn, pv=pv):
                        ins = None
                        for i in range(gn):
                            k0, kn = vblocks[g0 + i][1], vblocks[g0 + i][2]
                            ins = e.transpose(pv[0:kn, i * 128:i * 128 + nq], w["A"][0:nq, k0:k0 + kn],
                                              ident_bf[0:nq, 0:nq])
                        return ins
                    P.op("pe", [w["b_A"], b_const], [pbuf[pb]], tr)
                    kmax = max(vblocks[g0 + i][2] for i in range(gn))
                    eng = "act" if gi % 2 == 0 else "dve"

                    def ev(e, g0=g0, gn=gn, pv=pv, eng=eng, kmax=kmax):
                        src = pv.rearrange("p (b q) -> p b q", b=8)[0:kmax, 0:gn, 0:nq]
                        dst = w["AT"][0:kmax, g0:g0 + gn, 0:nq]
                        if eng == "act":
                            return e.copy(out=dst, in_=src)
                        return e.tensor_copy(dst, src)
                    P.op(eng, [pbuf[pb]], [w["b_AT"]], ev)
                pb = next_bank()

                def av(e, pb=pb):
                    ins = None
                    for i, (vap, k0, kn) in enumerate(vblocks):
                        ins = e.matmul(pbank[pb][:, 0:nq], vap, w["AT"][0:kn, i, 0:nq],
                                       start=(i == 0), stop=(i == nblk - 1))
                    return ins
                P.op("pe", [b_v, w["b_AT"]], [pbuf[pb]], av)
                P.op("act", [pbuf[pb]], [b_o], lambda e, pb=pb: e.copy(out=oT_ap, in_=pbank[pb][:, 0:nq]))

            for h in range(NH):
                H = hd[h % 2]
                P.dma("sp", H["qT"], pj[h * 128:(h + 1) * 128, :], [], [H["b_q"]])
                P.dma("sp", H["kT"], pj[1024 + h * 128:1024 + (h + 1) * 128, 0:S], [], [H["b_k"]])
                P.dma("sp", H["vh"], vtok[0:S, h * 128:(h + 1) * 128].rearrange("(b p) d -> p b d", p=128),
                      [], [H["b_v"]])
                P.dma("pool", H["ck"], I["cache_sb_k"][l, :, h, :].rearrange("(b p) d -> p b d", p=128),
                      [], [H["b_ck"]])
                P.dma("pool", H["vs"][:, 0:PB, :], I["cache_sb_v"][l, :, h, :].rearrange("(b p) d -> p b d", p=128),
                      [], [H["b_vs"]])
                P.dma("sp", H["vs"][0:R, PB, :], vtok[S:S + R, h * 128:(h + 1) * 128], [], [H["b_vs"]])
                P.dma("sp", H["kTs"][:, PAST:PAST + R], pj[1024 + h * 128:1024 + (h + 1) * 128, S:S + R],
                      [], [H["b_ks"]])
                for g0 in range(0, PB, 8):
                    gn = min(8, PB - g0)
                    pb = next_bank()
                    pv = pbank[pb].bitcast(BF16)

                    def trk(e, g0=g0, gn=gn, pv=pv, H=H):
                        ins = None
                        for i in range(gn):
                            ins = e.transpose(pv[:, i * 128:(i + 1) * 128], H["ck"][:, g0 + i, :], ident_bf)
                        return ins
                    P.op("pe", [H["b_ck"], b_const], [pbuf[pb]], trk)
                    P.op("dve", [pbuf[pb]], [H["b_ks"]],
                         lambda e, g0=g0, gn=gn, pv=pv, H=H: e.tensor_copy(H["kTs"][:, g0 * 128:(g0 + gn) * 128],
                                                                           pv[:, 0:gn * 128]))
                for qb in range(NB):
                    vbl = [(H["vh"][:, i, :], i * 128, 128) for i in range(qb + 1)]
                    attend(H["qT"][:, qb * 128:(qb + 1) * 128], H["b_q"], 128, H["kT"], H["b_k"], (qb + 1) * 128,
                           vbl, H["b_v"], H["oT"][:, qb * 128:(qb + 1) * 128], H["b_o"])
                vbl = [(H["vs"][:, i, :], i * 128, 128) for i in range(PB)] + [(H["vs"][0:R, PB, :], PAST, R)]
                attend(H["qT"][:, S:S + R], H["b_q"], R, H["kTs"], H["b_ks"], PAST + R,
                       vbl, H["b_vs"], H["oT"][:, S:S + R], H["b_o"])
                attend_flush()
                P.dma("sp", brT[0, h * 128:(h + 1) * 128, :], H["oT"], [H["b_o"]], [])
            P.barrier()

        if cfg.get("do_L", True):
            ar.reset()
            NC_ = {"allow_slow_non_contiguous": True}
            pp = ar.f32(8 * 16).rearrange("p (c x) -> p c x", c=8)
            b_pp = P.buf("pp")
            for k in range(4):
                P.dma("sp", pp[:, :, k], I["lru_conv_w"][l, k, :].rearrange("(c p) -> p c", p=128), [], [b_pp], **NC_)
            for col, nm in [(4, "lru_conv_b"), (5, "lru_ba"), (6, "lru_bx"), (7, "lru_lam"), (10, "state_lru_h")]:
                P.dma("sp", pp[:, :, col], I[nm][l, :].rearrange("(c p) -> p c", p=128), [], [b_pp], **NC_)
            P.op("act", [b_pp], [b_pp], lambda e: e.activation(out=pp[:, :, 8], in_=pp[:, :, 7], func=AF.Exp, scale=-1.0))
            P.op("dve", [b_pp], [b_pp], lambda e: e.tensor_scalar_add(pp[:, :, 8], pp[:, :, 8], 1.0))
            P.op("act", [b_pp], [b_pp], lambda e: e.activation(out=pp[:, :, 9], in_=pp[:, :, 8], func=AF.Ln))
            P.op("dve", [b_pp], [b_pp], lambda e: e.tensor_scalar_mul(pp[:, :, 8], pp[:, :, 9], -8.0))
            P.op("dve", [b_pp], [b_pp], lambda e: e.tensor_scalar_mul(pp[:, :, 9], pp[:, :, 9], -16.0))
            wa_bf = ar.bf16(8 * 128).rearrange("p (n j) -> p n j", n=8)
            wx_bf = ar.bf16(8 * 128).rearrange("p (n j) -> p n j", n=8)
            b_wg = P.buf("wg")
            P.dma("pool", wa_bf, I["lru_wa"][l].rearrange("n k j -> k n j"), [], [b_wg])
            P.dma("pool", wx_bf, I["lru_wx"][l].rearrange("n k j -> k n j"), [], [b_wg])
            TP = T + 6
            segs = [(0, S, "prompt"), (S + 3, R, "sample")]
            names = ["xpad", "xc", "r", "i", "a", "s2", "h", "lg", "t"]
            lsets = []
            for _i in range(2):
                lsets.append(({nm: ar.f32(TP) for nm in names}, {nm: P.buf("L" + nm) for nm in names},
                              ar.bf16(TP), P.buf("Lxcb"), ar.bf16(T), P.buf("Lob")))

            def l_chunk(c, tl, tb, xcb, b_xcb, ob, b_ob):
                    tl = dict(tl); tb = dict(tb)
                    tl["u"] = tl["i"]; tb["u"] = tb["i"]
                    row = 3072 + c * 128
                    P.op("dve", [], [tb["xpad"]], lambda e: e.memset(tl["xpad"][:, 0:3], 0.0))
                    P.dma("pool", tl["xpad"][:, 3:3 + S], pj[row:row + 128, 0:S], [], [tb["xpad"]])
                    P.dma("pool", tl["xpad"][:, S + 6:S + 6 + R], pj[row:row + 128, S:S + R], [], [tb["xpad"]])
                    P.dma("sp", tl["xpad"][:, S + 3:S + 6], I["state_lru_conv"][l, :, c * 128:(c + 1) * 128].rearrange("k c -> c k"),
                          [], [tb["xpad"]], **NC_)
                    P.dma("pool", tl["lg"][:, 0:T], pj[4096 + c * 128:4096 + (c + 1) * 128, :], [], [tb["lg"]])
                    for (off, n, nm) in segs:
                        xp_ = tl["xpad"]
                        P.op("dve", [tb["xpad"], b_pp], [tb["xc"]],
                             lambda e, off=off, n=n, c=c, xp_=xp_: e.tensor_scalar(tl["xc"][:, off:off + n], xp_[:, off + 3:off + 3 + n],
                                                                              pp[:, c, 3:4], pp[:, c, 4:5], ALU.mult, ALU.add))
                        for k in (2, 1, 0):
                            P.op("dve", [tb["xpad"], b_pp, tb["xc"]], [tb["xc"]],
                                 lambda e, off=off, n=n, c=c, k=k, xp_=xp_: e.scalar_tensor_tensor(
                                     tl["xc"][:, off:off + n], xp_[:, off + k:off + k + n], pp[:, c, k:k + 1],
                                     tl["xc"][:, off:off + n], ALU.mult, ALU.add))
                        P.dma("sp", O["lru_conv_" + nm][l, :, c * 128:(c + 1) * 128].rearrange("k c -> c k"),
                              xp_[:, off + n:off + n + 3], [tb["xpad"]], [], **NC_)
                        P.op("act", [tb["xc"]], [b_xcb],
                             lambda e, off=off, n=n: e.copy(out=xcb[:, off:off + n], in_=tl["xc"][:, off:off + n]))
                        for c0 in range(0, n, 512):
                            m = min(512, n - c0)
                            for (wt, bcol, dst) in ((wa_bf, 5, "r"), (wx_bf, 6, "i")):
                                pb = next_bank()
                                P.op("pe", [b_wg, b_xcb], [pbuf[pb]],
                                     lambda e, pb=pb, wt=wt, c=c, off=off, c0=c0, m=m: e.matmul(
                                         pbank[pb][:, 0:m], wt[:, c, :], xcb[:, off + c0:off + c0 + m], start=True, stop=True))
                                P.op("act", [pbuf[pb], b_pp], [tb[dst]],
                                     lambda e, pb=pb, dst=dst, bcol=bcol, c=c, off=off, c0=c0, m=m: e.activation(
                                         out=tl[dst][:, off + c0:off + c0 + m], in_=pbank[pb][:, 0:m], func=AF.Sigmoid,
                                         bias=pp[:, c, bcol:bcol + 1]))
                        sl = slice(off, off + n)
                        P.op("act", [tb["r"], b_pp], [tb["a"]],
                             lambda e, sl=sl, c=c: e.activation(out=tl["a"][:, sl], in_=tl["r"][:, sl], func=AF.Exp, scale=pp[:, c, 8:9]))
                        P.op("act", [tb["r"], b_pp], [tb["s2"]],
                             lambda e, sl=sl, c=c: e.activation(out=tl["s2"][:, sl], in_=tl["r"][:, sl], func=AF.Exp, scale=pp[:, c, 9:10]))
                        P.op("dve", [tb["s2"]], [tb["s2"]],
                             lambda e, sl=sl: e.tensor_scalar(tl["s2"][:, sl], tl["s2"][:, sl], -1.0, 1.0, ALU.mult, ALU.add))
                        P.op("act", [tb["s2"]], [tb["s2"]],
                             lambda e, sl=sl: e.activation(out=tl["s2"][:, sl], in_=tl["s2"][:, sl], func=AF.Sqrt))
                        P.op("dve", [tb["i"], tb["xc"]], [tb["u"]],
                             lambda e, sl=sl: e.tensor_tensor(tl["u"][:, sl], tl["i"][:, sl], tl["xc"][:, sl], ALU.mult))
                        P.op("dve", [tb["u"], tb["s2"]], [tb["u"]],
                             lambda e, sl=sl: e.tensor_tensor(tl["u"][:, sl], tl["u"][:, sl], tl["s2"][:, sl], ALU.mult))
                        init = 0.0 if nm == "prompt" else pp[:, c, 10:11]
                        P.op("dve", [tb["a"], tb["u"], b_pp], [tb["h"]],
                             lambda e, sl=sl, init=init: e.tensor_tensor_scan(tl["h"][:, sl], tl["a"][:, sl], tl["u"][:, sl],
                                                                              init, ALU.mult, ALU.add))
                        P.dma("sp", O["lru_h_" + nm][l, c * 128:(c + 1) * 128].rearrange("(p o) -> p o", o=1),
                              tl["h"][:, off + n - 1:off + n], [tb["h"]], [])
                    P.op("act", [tb["lg"]], [tb["t"]], lambda e: e.activation(out=tl["t"][:, 0:T], in_=tl["lg"][:, 0:T], func=AF.Square))
                    P.op("dve", [tb["t"]], [tb["t"]],
                         lambda e: e.tensor_scalar(tl["t"][:, 0:T], tl["t"][:, 0:T], 0.044715, 1.0, ALU.mult, ALU.add))
                    P.op("dve", [tb["t"], tb["lg"]], [tb["t"]],
                         lambda e: e.tensor_tensor(tl["t"][:, 0:T], tl["t"][:, 0:T], tl["lg"][:, 0:T], ALU.mult))
                    P.op("act", [tb["t"]], [tb["t"]],
                         lambda e: e.activation(out=tl["t"][:, 0:T], in_=tl["t"][:, 0:T], func=AF.Sigmoid, scale=1.5957691216))
                    P.op("dve", [tb["t"], tb["lg"]], [tb["t"]],
                         lambda e: e.tensor_tensor(tl["t"][:, 0:T], tl["t"][:, 0:T], tl["lg"][:, 0:T], ALU.mult))
                    P.op("dve", [tb["t"], tb["h"]], [b_ob],
                         lambda e: e.tensor_tensor(ob[:, 0:S], tl["t"][:, 0:S], tl["h"][:, 0:S], ALU.mult))
                    P.op("dve", [tb["t"], tb["h"]], [b_ob],
                         lambda e: e.tensor_tensor(ob[:, S:T], tl["t"][:, S:T], tl["h"][:, S + 3:S + 3 + R], ALU.mult))
                    P.dma("sp", brT[1, c * 128:(c + 1) * 128, :], ob, [b_ob], [])
            for c in range(8):
                l_chunk(c, *lsets[c % 2])
            P.barrier()

        if cfg.get("do_S", True):
            ar.reset()
            NC_ = {"allow_slow_non_contiguous": True}
            TP = T + 6
            b_sc = P.buf("Sconst")
            negm = ar.bf16(128)
            P.dma("pool", negm, I["negmask"], [], [b_sc])
            selh = ar.f32(16 * 128).rearrange("p (h m) -> p h m", h=16)
            P.dma("sp", selh[0:16], I["selh"].rearrange("k (h m) -> k h m", h=16), [], [b_sc])
            sel128 = ar.f32(128)
            selR = ar.f32(128)
            P.dma("sp", sel128, I["sel128"], [], [b_sc])
            P.dma("sp", selR, I["selR"], [], [b_sc])
            DB = ar.f32(16)
            P.dma("sp", DB, I["ssd_d"][l:l + 1, :].partition_broadcast(128), [], [b_sc])
            gnB = ar.f32(MIX)
            P.dma("sp", gnB, I["ssd_norm"][l:l + 1, :].partition_broadcast(128), [], [b_sc])
            cp = ar.f32(12 * 8).rearrange("p (c x) -> p c x", c=12)
            for k in range(4):
                P.dma("sp", cp[:, :, k], I["ssd_conv_w"][l, k, :].rearrange("(c p) -> p c", p=128), [], [b_sc], **NC_)
            P.dma("sp", cp[:, :, 4], I["ssd_conv_b"][l, :].rearrange("(c p) -> p c", p=128), [], [b_sc], **NC_)
            hp_ = ar.f32(4)
            P.dma("sp", hp_[0:16, 0:1], I["ssd_dt_bias"][l, :].rearrange("(p o) -> p o", o=1), [], [b_sc])
            P.dma("sp", hp_[0:16, 1:2], I["ssd_a_log"][l, :].rearrange("(p o) -> p o", o=1), [], [b_sc])
            P.op("act", [b_sc], [b_sc], lambda e: e.activation(out=hp_[0:16, 2:3], in_=hp_[0:16, 1:2], func=AF.Exp))
            P.op("dve", [b_sc], [b_sc], lambda e: e.tensor_scalar_mul(hp_[0:16, 2:3], hp_[0:16, 2:3], -1.0))
            xcT = ar.bf16(12 * T).rearrange("p (c t) -> p c t", c=12)
            b_xcT = P.bufs_n("xcT", 12)
            mark = ar.off
            segs = [(0, S, "prompt"), (S + 3, R, "sample")]
            cw = [dict(xpad=ar.f32(TP), xc=ar.f32(TP), b_xpad=P.buf("Sxpad"), b_xc=P.buf("Sxc")) for _ in range(2)]
            for c in range(12):
                w = cw[c % 2]
                row = 6144 + c * 128
                P.op("dve", [], [w["b_xpad"]], lambda e, w=w: e.memset(w["xpad"][:, 0:3], 0.0))
                P.dma("pool", w["xpad"][:, 3:3 + S], pj[row:row + 128, 0:S], [], [w["b_xpad"]])
                P.dma("pool", w["xpad"][:, S + 6:S + 6 + R], pj[row:row + 128, S:S + R], [], [w["b_xpad"]])
                P.dma("sp", w["xpad"][:, S + 3:S + 6], I["state_ssd_conv"][l, :, c * 128:(c + 1) * 128].rearrange("k c -> c k"),
                      [], [w["b_xpad"]], **NC_)
                for (off, n, nm) in segs:
                    P.op("dve", [w["b_xpad"], b_sc], [w["b_xc"]],
                         lambda e, off=off, n=n, c=c, w=w: e.tensor_scalar(w["xc"][:, off:off + n], w["xpad"][:, off + 3:off + 3 + n],
                                                                        cp[:, c, 3:4], cp[:, c, 4:5], ALU.mult, ALU.add))
                    for k in (2, 1, 0):
                        P.op("dve", [w["b_xpad"], b_sc, w["b_xc"]], [w["b_xc"]],
                             lambda e, off=off, n=n, c=c, k=k, w=w: e.scalar_tensor_tensor(
                                 w["xc"][:, off:off + n], w["xpad"][:, off + k:off + k + n], cp[:, c, k:k + 1],
                                 w["xc"][:, off:off + n], ALU.mult, ALU.add))
                    P.dma("sp", O["ssd_conv_" + nm][l, :, c * 128:(c + 1) * 128].rearrange("k c -> c k"),
                          w["xpad"][:, off + n:off + n + 3], [w["b_xpad"]], [], **NC_)
                    t0 = 0 if nm == "prompt" else S
                    P.op("act", [w["b_xc"]], [b_xcT[c]],
                         lambda e, off=off, n=n, c=c, w=w, t0=t0: e.activation(out=xcT[:, c, t0:t0 + n], in_=w["xc"][:, off:off + n],
                                                                            func=AF.Silu))
            P.barrier()
            ar.off = mark
            dts = ar.f32(T); dta = ar.f32(T); cumT = ar.f32(T); rmask = ar.f32(T)
            b_dt = P.buf("Sdt")
            P.dma("pool", dts[0:16, :], pj[IN_COLS:IN_COLS + 16, :], [], [b_dt])
            P.op("act", [b_dt, b_sc], [b_dt], lambda e: e.activation(out=dts[0:16, :], in_=dts[0:16, :], func=AF.Exp, bias=hp_[0:16, 0:1]))
            P.op("dve", [b_dt], [b_dt], lambda e: e.tensor_scalar_add(dts[0:16, :], dts[0:16, :], 1.0))
            P.op("act", [b_dt], [b_dt], lambda e: e.activation(out=dts[0:16, :], in_=dts[0:16, :], func=AF.Ln))
            P.op("dve", [b_dt, b_sc], [b_dt], lambda e: e.tensor_scalar(dta[0:16, :], dts[0:16, :], hp_[0:16, 2:3], None, ALU.mult))
            P.op("dve", [], [b_dt], lambda e: e.memset(rmask[0:16, :], 1.0))
            P.op("dve", [b_dt], [b_dt],
                 lambda e: e.memset(rmask[0:16, 0:S].rearrange("p (b q) -> p b q", q=128)[:, :, 0:1], 0.0))
            P.op("dve", [b_dt], [b_dt], lambda e: e.memset(rmask[0:16, S:S + 1], 0.0))
            P.op("dve", [b_dt], [b_dt],
                 lambda e: e.tensor_tensor_scan(cumT[0:16, :], rmask[0:16, :], dta[0:16, :], 0.0, ALU.mult, ALU.add))
            def A2(fn, nm):
                return {"ap": [fn(), fn()], "b": P.bufs_n(nm, 2)}
            xs_tok = A2(lambda: ar.bf16(1024), "xs_tok")
            B_tok = A2(lambda: ar.bf16(256), "B_tok")
            xw_tok = A2(lambda: ar.bf16(1024), "xw_tok")
            dc = A2(lambda: ar.f32(96), "dc")
            cdB = A2(lambda: ar.f32(16), "cdB")
            CBs = A2(lambda: ar.f32(256), "CBs")
            Et = A2(lambda: ar.f32(128), "Et")
            MT = A2(lambda: ar.bf16(128), "MT")
            yi = A2(lambda: ar.f32(1024), "yi")
            yt = A2(lambda: ar.f32(1024), "yt")
            zt = A2(lambda: ar.bf16(1024), "zt")
            sz = A2(lambda: ar.f32(1024), "sz")
            tmp = A2(lambda: ar.f32(1024), "tmp")
            ss = A2(lambda: ar.f32(8), "ss")
            oc = A2(lambda: ar.bf16(1024), "oc")
            ocT = A2(lambda: ar.bf16(1024).rearrange("p (c t) -> p c t", c=8), "ocT")
            hT = ar.f32(1024); hTb = ar.bf16(1024)
            b_hT = P.buf("hT"); b_hTb = P.buf("hTb")
            stg = ar.f32(1024).rearrange("p (c n) -> p c n", c=8)
            b_stg = P.buf("stg")

            def ssd_block(bi, t0, n, sel):
                j = bi % 2
                def nb2():
                    state['pbi'] += 1
                    return state['pbi'] % 2
                pb = nb2(); pv = pbank[pb].bitcast(BF16)
                P.op("pe", [b_xcT[c] for c in range(8)] + [b_const], [pbuf[pb]],
                     lambda e, pv=pv: [e.transpose(pv[0:n, c * 128:(c + 1) * 128], xcT[:, c, t0:t0 + n], ident_bf) for c in range(8)][-1])
                P.op("act", [pbuf[pb]], [xs_tok["b"][j]], lambda e, pv=pv: e.copy(out=xs_tok["ap"][j][0:n, :], in_=pv[0:n, :]))
                pb = nb2(); pv2 = pbank[pb].bitcast(BF16)
                P.op("pe", [b_xcT[8], b_xcT[9], b_const], [pbuf[pb]],
                     lambda e, pv2=pv2: [e.transpose(pv2[0:n, c * 128:(c + 1) * 128], xcT[:, 8 + c, t0:t0 + n], ident_bf) for c in range(2)][-1])
                P.op("dve", [pbuf[pb]], [B_tok["b"][j]], lambda e, pv2=pv2: e.tensor_copy(B_tok["ap"][j][0:n, :], pv2[0:n, 0:256]))
                pb = nb2()
                P.op("pe", [b_dt, b_const], [pbuf[pb]],
                     lambda e, pb=pb: [e.transpose(pbank[pb][0:n, 0:16], dts[0:16, t0:t0 + n], ident_f[0:16, 0:16]),
                                       e.transpose(pbank[pb][0:n, 16:32], cumT[0:16, t0:t0 + n], ident_f[0:16, 0:16])][-1])
                D_ = dc["ap"][j]; bD = dc["b"][j]
                P.op("act", [pbuf[pb]], [bD], lambda e, pb=pb: e.copy(out=D_[0:n, 0:32], in_=pbank[pb][0:n, 0:32]))
                P.op("dve", [bD], [bD], lambda e: e.tensor_scalar_mul(D_[0:n, 32:48], D_[0:n, 16:32], -1.0))
                pb = nb2()
                P.op("pe", [bD, b_sc], [pbuf[pb]],
                     lambda e, pb=pb: e.matmul(pbank[pb][:, 0:16], sel[0:n, :], D_[0:n, 16:32], start=True, stop=True))
                P.op("act", [pbuf[pb]], [bD], lambda e, pb=pb: e.copy(out=D_[:, 48:64], in_=pbank[pb][:, 0:16]))
                P.op("act", [bD], [cdB["b"][j]], lambda e: e.activation(out=cdB["ap"][j], in_=D_[:, 48:64], func=AF.Exp))
                P.op("dve", [bD], [bD], lambda e: e.tensor_tensor(D_[0:n, 64:80], D_[0:n, 48:64], D_[0:n, 16:32], ALU.subtract))
                P.op("act", [bD], [bD], lambda e: e.activation(out=D_[0:n, 64:80], in_=D_[0:n, 64:80], func=AF.Exp))
                P.op("dve", [bD], [bD], lambda e: e.tensor_tensor(D_[0:n, 64:80], D_[0:n, 64:80], D_[0:n, 0:16], ALU.mult))
                P.op("act", [bD], [bD], lambda e: e.activation(out=D_[0:n, 80:96], in_=D_[0:n, 16:32], func=AF.Exp))
                P.op("dve", [xs_tok["b"][j], bD], [xw_tok["b"][j]],
                     lambda e: e.tensor_tensor(xw_tok["ap"][j][0:n, :].rearrange("p (h q) -> p h q", h=16),
                                               xs_tok["ap"][j][0:n, :].rearrange("p (h q) -> p h q", h=16),
                                               D_[0:n, 64:80].unsqueeze(2).to_broadcast([n, 16, 64]), ALU.mult))
                for g in range(2):
                    pb = nb2()
                    P.op("pe", [b_xcT[8 + g], b_xcT[10 + g]], [pbuf[pb]],
                         lambda e, pb=pb, g=g: e.matmul(pbank[pb][0:n, 0:n], xcT[:, 8 + g, t0:t0 + n], xcT[:, 10 + g, t0:t0 + n],
                                                        start=True, stop=True))
                    P.op("act", [pbuf[pb]], [CBs["b"][j]],
                         lambda e, pb=pb, g=g: e.copy(out=CBs["ap"][j][0:n, g * 128:g * 128 + n], in_=pbank[pb][0:n, 0:n]))
                pbi = [2, 3]
                for g in range(2):
                    P.op("pe", [b_xcT[10 + g], b_hTb], [pbuf[pbi[g]]],
                         lambda e, g=g: e.matmul(pbank[pbi[g]][0:n, :], xcT[:, 10 + g, t0:t0 + n], hTb[:, g * 512:(g + 1) * 512],
                                                 start=True, stop=True))
                    P.op("dve", [pbuf[pbi[g]], bD], [yi["b"][j]],
                         lambda e, g=g: e.tensor_tensor(yi["ap"][j][0:n, g * 512:(g + 1) * 512].rearrange("p (h q) -> p h q", h=8),
                                                        pbank[pbi[g]][0:n, :].rearrange("p (h q) -> p h q", h=8),
                                                        D_[0:n, 80 + g * 8:88 + g * 8].unsqueeze(2).to_broadcast([n, 8, 64]), ALU.mult))
                pby = [4, 5]
                for h in range(16):
                    g = h // 8
                    jj = h % 2
                    pb = nb2()
                    P.op("pe", [b_dt, b_sc, b_const], [pbuf[pb]],
                         lambda e, pb=pb, h=h: [e.matmul(pbank[pb][0:n, 0:n], selh[0:16, h, 0:n], cumT[0:16, t0:t0 + n], start=True, stop=False),
                                                e.matmul(pbank[pb][0:n, 0:n], ident_bf[0:n, 0:n], negm[0:n, 0:n], start=False, stop=True)][-1])
                    P.op("act", [pbuf[pb], bD], [Et["b"][jj]],
                         lambda e, pb=pb, h=h, jj=jj: e.activation(out=Et["ap"][jj][0:n, 0:n], in_=pbank[pb][0:n, 0:n], func=AF.Exp,
                                                                   bias=D_[0:n, 32 + h:33 + h]))
                    P.op("dve", [Et["b"][jj], bD, CBs["b"][j]], [MT["b"][jj]],
                         lambda e, h=h, jj=jj, g=g: e.scalar_tensor_tensor(MT["ap"][jj][0:n, 0:n], Et["ap"][jj][0:n, 0:n], D_[0:n, h:h + 1],
                                                                          CBs["ap"][j][0:n, g * 128:g * 128 + n], ALU.mult, ALU.mult))
                    P.op("pe", [MT["b"][jj], xs_tok["b"][j]], [pbuf[pby[g]]],
                         lambda e, h=h, jj=jj, g=g: e.matmul(pbank[pby[g]][0:n, (h % 8) * 64:(h % 8 + 1) * 64], MT["ap"][jj][0:n, 0:n],
                                                             xs_tok["ap"][j][0:n, h * 64:(h + 1) * 64], start=True, stop=True))
                pbs = [6, 7]
                for g in range(2):
                    P.op("pe", [B_tok["b"][j], xw_tok["b"][j]], [pbuf[pbs[g]]],
                         lambda e, g=g: e.matmul(pbank[pbs[g]][:, :], B_tok["ap"][j][0:n, g * 128:(g + 1) * 128],
                                                 xw_tok["ap"][j][0:n, g * 512:(g + 1) * 512], start=True, stop=True))
                P.op("dve", [b_hT, cdB["b"][j]], [b_hT],
                     lambda e: e.tensor_tensor(hT.rearrange("p (h q) -> p h q", h=16), hT.rearrange("p (h q) -> p h q", h=16),
                                               cdB["ap"][j].unsqueeze(2).to_broadcast([128, 16, 64]), ALU.mult))
                for g in range(2):
                    P.op("dve", [b_hT, pbuf[pbs[g]]], [b_hT],
                         lambda e, g=g: e.tensor_tensor(hT[:, g * 512:(g + 1) * 512], hT[:, g * 512:(g + 1) * 512], pbank[pbs[g]][:, :], ALU.add))
                P.op("act", [b_hT], [b_hTb], lambda e: e.copy(out=hTb, in_=hT))
                Y = yt["ap"][j]; bY = yt["b"][j]
                for g in range(2):
                    P.op("dve", [pbuf[pby[g]], yi["b"][j]], [bY],
                         lambda e, g=g: e.tensor_tensor(Y[0:n, g * 512:(g + 1) * 512], pbank[pby[g]][0:n, :],
                                                        yi["ap"][j][0:n, g * 512:(g + 1) * 512], ALU.add))
                P.op("dve", [xs_tok["b"][j], b_sc], [tmp["b"][j]],
                     lambda e: e.tensor_tensor(tmp["ap"][j][0:n, :].rearrange("p (h q) -> p h q", h=16),
                                               xs_tok["ap"][j][0:n, :].rearrange("p (h q) -> p h q", h=16),
                                               DB[0:n, :].unsqueeze(2).to_broadcast([n, 16, 64]), ALU.mult))
                P.op("dve", [bY, tmp["b"][j]], [bY], lambda e: e.tensor_tensor(Y[0:n, :], Y[0:n, :], tmp["ap"][j][0:n, :], ALU.add))
                P.dma("sp", zt["ap"][j][0:n, :], ztok[t0:t0 + n, :], [], [zt["b"][j]])
                P.op("act", [zt["b"][j]], [sz["b"][j]], lambda e: e.activation(out=sz["ap"][j][0:n, :], in_=zt["ap"][j][0:n, :], func=AF.Silu))
                P.op("dve", [bY, sz["b"][j]], [bY], lambda e: e.tensor_tensor(Y[0:n, :], Y[0:n, :], sz["ap"][j][0:n, :], ALU.mult))
                SS = ss["ap"][j]; bS = ss["b"][j]
                for g in range(2):
                    P.op("act", [bY], [tmp["b"][j], bS],
                         lambda e, g=g: e.activation(out=tmp["ap"][j][0:n, g * 512:(g + 1) * 512], in_=Y[0:n, g * 512:(g + 1) * 512],
                                                     func=AF.Square, accum_out=SS[0:n, g:g + 1]))
                P.op("dve", [bS], [bS], lambda e: e.tensor_scalar(SS[0:n, 2:4], SS[0:n, 0:2], 1.0 / 512, EPS, ALU.mult, ALU.add))
                P.op("act", [bS], [bS], lambda e: e.activation(out=SS[0:n, 4:6], in_=SS[0:n, 2:4], func=AF.Sqrt))
                P.op("dve", [bS], [bS], lambda e: e.reciprocal(SS[0:n, 6:8], SS[0:n, 4:6]))
                for g in range(2):
                    P.op("dve", [bS, bY, b_sc], [oc["b"][j]],
                         lambda e, g=g: e.scalar_tensor_tensor(oc["ap"][j][0:n, g * 512:(g + 1) * 512], Y[0:n, g * 512:(g + 1) * 512],
                                                              SS[0:n, 6 + g:7 + g], gnB[0:n, g * 512:(g + 1) * 512], ALU.mult, ALU.mult))
                pb = nb2(); pv3 = pbank[pb].bitcast(BF16)
                P.op("pe", [oc["b"][j], b_const], [pbuf[pb]],
                     lambda e, pv3=pv3: [e.transpose(pv3[:, c * 128:c * 128 + n], oc["ap"][j][0:n, c * 128:(c + 1) * 128], ident_bf[0:n, 0:n])
                                         for c in range(8)][-1])
                P.op("act", [pbuf[pb]], [ocT["b"][j]],
                     lambda e, pv3=pv3: e.copy(out=ocT["ap"][j][:, :, 0:n], in_=pv3.rearrange("p (c t) -> p c t", c=8)[:, :, 0:n]))
                P.dma("sp", brT[2, :, t0:t0 + n].rearrange("(c p) t -> p c t", p=128), ocT["ap"][j][:, :, 0:n], [ocT["b"][j]], [])

            def state_out(dst):
                for half in range(2):
                    pb = next_bank()
                    P.op("pe", [b_hT, b_const], [pbuf[pb]],
                         lambda e, pb=pb, half=half: [e.transpose(pbank[pb][:, c * 128:(c + 1) * 128], hT[:, (half * 4 + c) * 128:(half * 4 + c + 1) * 128], ident_f)
                                                      for c in range(4)][-1])
                    P.op("act", [pbuf[pb]], [b_stg],
                         lambda e, pb=pb, half=half: e.copy(out=stg[:, half * 4:(half + 1) * 4, :], in_=pbank[pb].rearrange("p (c n) -> p c n", c=4)))
                P.dma("sp", dst.rearrange("h p n -> (h p) n").rearrange("(j q) n -> q j n", q=128), stg, [b_stg], [])

            P.op("dve", [], [b_hT], lambda e: e.memset(hT, 0.0))
            P.op("act", [b_hT], [b_hTb], lambda e: e.copy(out=hTb, in_=hT))
            for bi in range(NB):
                ssd_block(bi, bi * 128, 128, sel128)
            state_out(O["ssd_state_prompt"][l])
            P.dma("sp", stg, I["state_ssd"][l].rearrange("h p n -> (h p) n").rearrange("(j q) n -> q j n", q=128), [], [b_stg])
            for half in range(2):
                pb = next_bank()
                P.op("pe", [b_stg, b_const], [pbuf[pb]],
                     lambda e, pb=pb, half=half: [e.transpose(pbank[pb][:, c * 128:(c + 1) * 128], stg[:, half * 4 + c, :], ident_f)
                                                  for c in range(4)][-1])
                P.op("act", [pbuf[pb]], [b_hT], lambda e, pb=pb, half=half: e.copy(out=hT[:, half * 512:(half + 1) * 512], in_=pbank[pb]))
            P.op("act", [b_hT], [b_hTb], lambda e: e.copy(out=hTb, in_=hT))
            ssd_block(NB, S, R, selR)
            state_out(O["ssd_state_sample"][l])
            P.barrier()

        NC_ = {"allow_slow_non_contiguous": True}
        GT = cfg.get("group_tokens", 1024)
        groups = []
        for t0 in range(0, S, GT):
            groups.append((t0, min(GT, S - t0)))
        groups[-1] = (groups[-1][0], groups[-1][1] + R)

        def A1(fn, nm):
            a = fn()
            b = P.buf(nm)
            return {"ap": [a, a], "b": [b, b]}

        def out_proj_residual(wsrc, nkc, xT, x_bufs, g0, gn, wbf, rs, ot, ncolblk=512, wcnt=[0]):
            blocks = split_range(g0, gn, 128)
            for cb in range(D // ncolblk):
                wap, wbuf = load_w2(wsrc, cb * ncolblk, ncolblk, nkc, wbf)

                def evac(ps_ap, ps_buf, t0, n, cb=cb):
                    jj = wcnt[0] % 2
                    wcnt[0] += 1
                    P.dma("sp", rs["ap"][jj][0:n, 0:ncolblk], res_src(l, t0, n, cb * ncolblk, ncolblk), [], [rs["b"][jj]])
                    P.op("dve", [ps_buf, rs["b"][jj]], [ot["b"][jj]],
                         lambda e: e.tensor_tensor(ot["ap"][jj][0:n, 0:ncolblk], ps_ap[0:n, 0:ncolblk], rs["ap"][jj][0:n, 0:ncolblk], ALU.add))
                    P.dma("sp", res[t0:t0 + n, cb * ncolblk:(cb + 1) * ncolblk], ot["ap"][jj][0:n, 0:ncolblk], [ot["b"][jj]], [])
                gemm_tm2(wap, wbuf, ncolblk, xT, x_bufs, nkc, g0, blocks, evac)

        def A2(fn, nm):
            return {"ap": [fn(), fn()], "b": P.bufs_n(nm, 2)}

        if cfg.get("do_M", True):
            def m_group(g0, gn):
                ar.reset()
                chunks = split_range(g0, gn, 512)
                oT = [ar.bf16(8 * gn).rearrange("p (c t) -> p c t", c=8) for _ in range(3)]
                b_oT = P.bufs_n("MoT", 3)
                for n_ in range(3):
                    P.dma("sp", oT[n_], brT[n_, :, g0:g0 + gn].rearrange("(c p) t -> p c t", p=128), [], [b_oT[n_]])
                mgT = ar.bf16(KC * gn).rearrange("p (c t) -> p c t", c=KC)
                b_mg = P.buf("mgT")
                acc = ar.f32(4 * gn).rearrange("p (s t) -> p s t", s=4)
                b_acc = P.buf("Macc")
                tmpm = A2(lambda: ar.f32(512), "Mtmp")
                sg = A2(lambda: ar.bf16(4 * gn).rearrange("p (s t) -> p s t", s=4), "Msg")
                wbf = AN(lambda: ar.bf16(KC * 512).rearrange("p (k n) -> p k n", k=KC), "Mwbf", 3)
                rs = A2(lambda: ar.f32(512), "Mrs")
                ot = A2(lambda: ar.f32(512), "Mot")
                cnt = [0]
                for cb in range(4):
                    for n_ in range(3):
                        j = cnt[0] % 2
                        cnt[0] += 1
                        wap, wbuf = load_w2(I["w_branch"][l, n_], cb * 512, 512, 8, wbf)
                        r0 = 7696 + n_ * 2048 + cb * 512
                        P.dma("sp", sg["ap"][j], pj[r0:r0 + 512, g0:g0 + gn].rearrange("(s p) t -> p s t", p=128), [], [sg["b"][j]])

                        def evac(ps_ap, ps_buf, sub, t0, n, m, n_=n_, j=j):
                            lo = t0 - g0
                            if n_ == 0:
                                P.op("dve", [ps_buf, sg["b"][j]], [b_acc],
                                     lambda e: e.tensor_tensor(acc[:, sub, lo:lo + n], ps_ap[:, 0:n], sg["ap"][j][:, sub, lo:lo + n], ALU.mult))
                            else:
                                jj = cnt[0] % 2
                                cnt[0] += 1
                                P.op("dve", [ps_buf, sg["b"][j]], [tmpm["b"][jj]],
                                     lambda e: e.tensor_tensor(tmpm["ap"][jj][:, 0:n], ps_ap[:, 0:n], sg["ap"][j][:, sub, lo:lo + n], ALU.mult))
                                P.op("dve", [tmpm["b"][jj], b_acc], [b_acc],
                                     lambda e: e.tensor_tensor(acc[:, sub, lo:lo + n], acc[:, sub, lo:lo + n], tmpm["ap"][jj][:, 0:n], ALU.add))
                        gemm_fm2(wap, wbuf, 512, oT[n_], [b_oT[n_]], 8, g0, chunks, evac)
                    P.op("act", [b_acc], [b_mg], lambda e, cb=cb: e.copy(out=mgT[:, cb * 4:(cb + 1) * 4, :], in_=acc))
                out_proj_residual(I["w_out"][l], KC, mgT, [b_mg], g0, gn, wbf, rs, ot)
                P.barrier()
            for (g0_, gn_) in groups:
                m_group(g0_, gn_)
            state["res_live"] = True

        if cfg.get("do_X", True):
            ar.reset()
            mkT = [ar.bf16(KC * NMEM).rearrange("p (c m) -> p c m", c=KC) for _ in range(2)]
            mvt = [ar.bf16(2 * D).rearrange("p (b d) -> p b d", b=2) for _ in range(2)]
            b_mk = P.bufs_n("mkT", 2)
            b_mv = P.bufs_n("mvt", 2)
            ones_bf = ar.bf16(128)
            b_on = P.buf("onesbf")
            P.op("dve", [], [b_on], lambda e: e.memset(ones_bf, 1.0))
            markX = ar.off
            mnT = ar.bf16(KC * NMEM).rearrange("p (k t) -> p k t", k=KC)
            b_mn = P.bufs_n("mnT", 2)
            gB = ar.f32(D); gB_buf = P.buf("gBx")
            work = (A2(lambda: ar.f32(D), "xt"), A2(lambda: ar.bf16(D), "xb"), A2(lambda: ar.f32(D), "sq"), A2(lambda: ar.f32(4), "st"))
            phase_norm([(0, 128, I["mem_prompt"][0:128, :]), (128, 128, I["mem_prompt"][128:256, :])],
                       I["norm_mem"][l:l + 1, :], mnT, b_mn, gB, gB_buf, work)
            wbf = AN(lambda: ar.bf16(KC * 512).rearrange("p (k n) -> p k n", k=KC), "Xwbf", 3)
            otmx = A2(lambda: ar.f32(512), "Xotmx")
            cstage = ar.bf16(2 * D).rearrange("p (b d) -> p b d", b=2)
            b_cst = P.buf("cst")
            cnt = [0]
            mblocks = [(0, 128), (128, 128)]
            for cb in range(4):
                wap, wbuf = load_w2(I["w_xk"][l], cb * 512, 512, KC, wbf)

                def evk(ps_ap, ps_buf, sub, t0, n, m, cb=cb):
                    P.op("act", [ps_buf], [b_mk[0]], lambda e: e.copy(out=mkT[0][:, cb * 4 + sub, 0:NMEM], in_=ps_ap[:, 0:NMEM]))
                gemm_fm2(wap, wbuf, 512, mnT, b_mn, KC, 0, [(0, NMEM)], evk)

                def evk2(ps_ap, ps_buf, t0, n, cb=cb):
                    jj = cnt[0] % 2; cnt[0] += 1
                    P.op("act", [ps_buf], [otmx["b"][jj]], lambda e: e.copy(out=otmx["ap"][jj], in_=ps_ap))
                    P.dma("sp", O["mem_k_prompt"][l, t0:t0 + n, cb * 512:(cb + 1) * 512], otmx["ap"][jj], [otmx["b"][jj]], [])
                gemm_tm2(wap, wbuf, 512, mnT, b_mn, KC, 0, mblocks, evk2)
            for cb in range(4):
                wap, wbuf = load_w2(I["w_xv"][l], cb * 512, 512, KC, wbf)

                def evv(ps_ap, ps_buf, t0, n, cb=cb):
                    jj = cnt[0] % 2; cnt[0] += 1
                    P.op("act", [ps_buf], [otmx["b"][jj]], lambda e: e.copy(out=otmx["ap"][jj], in_=ps_ap))
                    P.dma("sp", O["mem_v_prompt"][l, t0:t0 + n, cb * 512:(cb + 1) * 512], otmx["ap"][jj], [otmx["b"][jj]], [])
                    P.op("dve", [otmx["b"][jj]], [b_mv[0]],
                         lambda e: e.tensor_copy(mvt[0][:, t0 // 128, cb * 512:(cb + 1) * 512], otmx["ap"][jj]))
                gemm_tm2(wap, wbuf, 512, mnT, b_mn, KC, 0, mblocks, evv)
            P.dma("pool", cstage, I["cache_mem_k"][l].rearrange("(b p) d -> p b d", p=128), [], [b_cst])
            P.dma("pool", mvt[1], I["cache_mem_v"][l].rearrange("(b p) d -> p b d", p=128), [], [b_mv[1]])
            for mb in range(2):
                for half in range(2):
                    pb = next_bank(); pv = pbank[pb].bitcast(BF16)
                    P.op("pe", [b_cst, b_const], [pbuf[pb]],
                         lambda e, pv=pv, mb=mb, half=half: [e.transpose(pv[:, c * 128:(c + 1) * 128], cstage[:, mb, (half * 8 + c) * 128:(half * 8 + c + 1) * 128], ident_bf)
                                                             for c in range(8)][-1])
                    P.op("dve", [pbuf[pb]], [b_mk[1]],
                         lambda e, pv=pv, mb=mb, half=half: e.tensor_copy(mkT[1][:, half * 8:(half + 1) * 8, mb * 128:(mb + 1) * 128],
                                                                          pv.rearrange("p (c m) -> p c m", c=8)))
            P.barrier()
            xscale = 512 ** -0.5
            def x_group(g0, gn):
                ar.off = markX
                blocks = split_range(g0, gn, 128)
                chunks = split_range(g0, gn, 512)
                xn2T = ar.bf16(KC * gn).rearrange("p (k t) -> p k t", k=KC)
                b_xn2 = P.bufs_n("xn2T", len(blocks))
                oTx = ar.bf16(KC * gn).rearrange("p (k t) -> p k t", k=KC)
                b_oTx = P.buf("oTx")
                markG = ar.off
                gB = ar.f32(D); gB_buf = P.buf("gBx2")
                work = (A2(lambda: ar.f32(D), "xt"), A2(lambda: ar.bf16(D), "xb"), A2(lambda: ar.f32(D), "sq"), A2(lambda: ar.f32(4), "st"))
                phase_norm([(t0 - g0, n, res_src(l, t0, n)) for (t0, n) in blocks], I["norm_xattn"][l:l + 1, :], xn2T, b_xn2, gB, gB_buf, work)
                P.barrier()
                ar.off = markG
                wbf = AN(lambda: ar.bf16(KC * 512).rearrange("p (k n) -> p k n", k=KC), "Xwbf", 3)
                qTh = A2(lambda: ar.bf16(4 * gn).rearrange("p (s t) -> p s t", s=4), "qTh")
                eT = A2(lambda: ar.bf16(2 * 512).rearrange("p (b t) -> p b t", b=2), "eT")
                rsm = A2(lambda: ar.f32(512), "rsm")
                rs = A2(lambda: ar.f32(512), "Xrs")
                ot = A2(lambda: ar.f32(512), "Xot")
                cnt = [0]
                for h in range(4):
                    j = h % 2
                    wap, wbuf = load_w2(I["w_xq"][l], h * 512, 512, KC, wbf)

                    def evq(ps_ap, ps_buf, sub, t0, n, m, j=j):
                        lo = t0 - g0
                        eng = "act" if sub % 2 == 0 else "dve"
                        if eng == "act":
                            P.op("act", [ps_buf], [qTh["b"][j]], lambda e: e.copy(out=qTh["ap"][j][:, sub, lo:lo + n], in_=ps_ap[:, 0:n]))
                        else:
                            P.op("dve", [ps_buf], [qTh["b"][j]], lambda e: e.tensor_copy(qTh["ap"][j][:, sub, lo:lo + n], ps_ap[:, 0:n]))
                    gemm_fm2(wap, wbuf, 512, xn2T, b_xn2, KC, g0, chunks, evq)
                    for (t0, n) in chunks:
                        lo = t0 - g0
                        mi = 0 if t0 < S else 1
                        jj = cnt[0] % 2; cnt[0] += 1
                        for mb in range(2):
                            pb = next_bank()
                            P.op("pe", [b_mk[mi], qTh["b"][j]], [pbuf[pb]],
                                 lambda e, pb=pb, mb=mb, mi=mi, lo=lo, n=n, h=h, j=j: [e.matmul(pbank[pb][:, 0:n], mkT[mi][:, h * 4 + dc, mb * 128:(mb + 1) * 128],
                                                                                      qTh["ap"][j][:, dc, lo:lo + n], start=(dc == 0), stop=(dc == 3))
                                                                             for dc in range(4)][-1])
                            P.op("act", [pbuf[pb]], [eT["b"][jj]],
                                 lambda e, pb=pb, mb=mb, n=n, jj=jj: e.activation(out=eT["ap"][jj][:, mb, 0:n], in_=pbank[pb][:, 0:n], func=AF.Exp, scale=xscale))
                        pb = next_bank()
                        P.op("pe", [b_on, eT["b"][jj]], [pbuf[pb]],
                             lambda e, pb=pb, n=n, jj=jj: [e.matmul(pbank[pb][:, 0:n], ones_bf, eT["ap"][jj][:, mb, 0:n], start=(mb == 0), stop=(mb == 1))
                                                           for mb in range(2)][-1])
                        P.op("dve", [pbuf[pb]], [rsm["b"][jj]], lambda e, pb=pb, n=n, jj=jj: e.reciprocal(rsm["ap"][jj][:, 0:n], pbank[pb][:, 0:n]))
                        for dc in range(4):
                            pb = next_bank()
                            P.op("pe", [b_mv[mi], eT["b"][jj]], [pbuf[pb]],
                                 lambda e, pb=pb, n=n, jj=jj, dc=dc, mi=mi, h=h: [e.matmul(pbank[pb][:, 0:n], mvt[mi][:, mb, h * 512 + dc * 128:h * 512 + (dc + 1) * 128],
                                                                                         eT["ap"][jj][:, mb, 0:n], start=(mb == 0), stop=(mb == 1))
                                                                                for mb in range(2)][-1])
                            P.op("dve", [pbuf[pb], rsm["b"][jj]], [b_oTx],
                                 lambda e, pb=pb, n=n, jj=jj, dc=dc, lo=lo, h=h: e.tensor_tensor(oTx[:, h * 4 + dc, lo:lo + n], pbank[pb][:, 0:n],
                                                                                                rsm["ap"][jj][:, 0:n], ALU.mult))
                out_proj_residual(I["w_xo"][l], KC, oTx, [b_oTx], g0, gn, wbf, rs, ot)
                P.barrier()
            for (g0_, gn_) in groups:
                x_group(g0_, gn_)

        if cfg.get("do_F", True):
            ar.reset()
            NP = DFF // 256
            xn3T = ar.bf16(KC * T).rearrange("p (k t) -> p k t", k=KC)
            b_xn3 = P.bufs_n("xn3T", len(tblocks))
            markF = ar.off
            gB = ar.f32(D); gB_buf = P.buf("gBf")
            work = (A2(lambda: ar.f32(D), "xt"), A2(lambda: ar.bf16(D), "xb"), A2(lambda: ar.f32(D), "sq"), A2(lambda: ar.f32(4), "st"))
            phase_norm([(t0, n, res_src(l, t0, n)) for (t0, n) in tblocks], I["norm_ffn"][l:l + 1, :], xn3T, b_xn3, gB, gB_buf, work)
            P.barrier()
            ar.off = markF
            fp = ar.f32(88 * 6).rearrange("p (c x) -> p c x", c=88)
            b_fp = P.buf("fp")
            for k in range(3):
                P.dma("sp", fp[:, :, k], I["ffn_conv_w"][l, k, :].rearrange("(c p) -> p c", p=128), [], [b_fp], **NC_)
            P.dma("sp", fp[:, :, 3], I["ffn_conv_b"][l, :].rearrange("(c p) -> p c", p=128), [], [b_fp], **NC_)
            for k in range(2):
                P.dma("sp", fp[:, :, 4 + k], I["state_ffn_conv"][l, k, :].rearrange("(c p) -> p c", p=128), [], [b_fp], **NC_)
            TP2 = T + 4
            wbf = AN(lambda: ar.bf16(KC * 256).rearrange("p (k n) -> p k n", k=KC), "Fwbf", 4)
            ugs = [[ar.bf16(2 * TP2).rearrange("p (s t) -> p s t", s=2) for _ in range(2)] for _ in range(2)]
            b_ugs = [P.bufs_n("ug", 2) for _ in range(2)]
            cg = A2(lambda: ar.f32(T), "cg")
            cv = A2(lambda: ar.f32(T), "cv")
            actT = A1(lambda: ar.bf16(2 * T).rearrange("p (s t) -> p s t", s=2), "actT")
            segs = [(0, S, 0, "prompt"), (S + 2, R, S, "sample")]
            cnt = [0]
            wh = {}

            def issue_w(pr_, half_):
                wh[(pr_, half_)] = load_w2(I["w_up"][l], half_ * DFF + pr_ * 256, 256, KC, wbf)
            issue_w(0, 0)
            issue_w(0, 1)
            for pr in range(NP):
                ug = ugs[pr % 2]
                b_ug = b_ugs[pr % 2]
                if pr + 1 < NP:
                    issue_w(pr + 1, 0)
                    issue_w(pr + 1, 1)
                for half in range(2):
                    j = cnt[0] % 2; cnt[0] += 1
                    c0 = half * DFF + pr * 256
                    wap, wbuf = wh[(pr, half)]
                    U = ug[half]
                    P.op("dve", [], [b_ug[half]], lambda e, U=U: e.memset(U[:, :, 0:2], 0.0))
                    for sub in range(2):
                        ch = c0 // 128 + sub
                        P.op("act", [b_fp], [b_ug[half]], lambda e, U=U, sub=sub, ch=ch: e.copy(out=U[:, sub, S + 2:S + 4], in_=fp[:, ch, 4:6]))

                    def evu(ps_ap, ps_buf, sub, t0, n, m, U=U, half=half):
                        off = t0 + 2 if t0 < S else t0 + 4
                        if sub % 2 == 0:
                            P.op("act", [ps_buf], [b_ug[half]], lambda e: e.copy(out=U[:, sub, off:off + n], in_=ps_ap[:, 0:n]))
                        else:
                            P.op("dve", [ps_buf], [b_ug[half]], lambda e: e.tensor_copy(U[:, sub, off:off + n], ps_ap[:, 0:n]))
                    gemm_fm2(wap, wbuf, 256, xn3T, b_xn3, KC, 0, tchunks, evu)
                    for sub in range(2):
                        ch = c0 // 128 + sub
                        for (off, n, t0, nm) in segs:
                            P.dma("pool", O["ffn_conv_" + nm][l, :, ch * 128:(ch + 1) * 128].rearrange("k c -> c k"),
                                  U[:, sub, off + n:off + n + 2], [b_ug[half]], [], **NC_)
                ja = pr % 2
                for sub in range(2):
                    jj = cnt[0] % 2; cnt[0] += 1
                    for half, dstt in ((0, cg), (1, cv)):
                        U = ug[half]
                        ch = (half * DFF + pr * 256) // 128 + sub
                        for (off, n, t0, nm) in segs:
                            P.op("act", [b_ug[half], b_fp], [dstt["b"][jj]],
                                 lambda e, U=U, ch=ch, off=off, n=n, t0=t0, dstt=dstt, jj=jj, sub=sub: e.activation(
                                     out=dstt["ap"][jj][:, t0:t0 + n], in_=U[:, sub, off + 2:off + 2 + n], func=AF.Identity,
                                     scale=fp[:, ch, 2:3], bias=fp[:, ch, 3:4]))
                            for k in (1, 0):
                                P.op("dve", [b_ug[half], b_fp, dstt["b"][jj]], [dstt["b"][jj]],
                                     lambda e, U=U, ch=ch, off=off, n=n, t0=t0, dstt=dstt, jj=jj, sub=sub, k=k: e.scalar_tensor_tensor(
                                         dstt["ap"][jj][:, t0:t0 + n], U[:, sub, off + k:off + k + n], fp[:, ch, k:k + 1],
                                         dstt["ap"][jj][:, t0:t0 + n], ALU.mult, ALU.add))
                    P.op("act", [cg["b"][jj]], [cg["b"][jj]], lambda e, jj=jj: e.activation(out=cg["ap"][jj], in_=cg["ap"][jj], func=AF.Silu))
                    P.op("pool", [cg["b"][jj], cv["b"][jj]], [actT["b"][ja]],
                         lambda e, jj=jj, sub=sub, ja=ja: e.tensor_tensor(actT["ap"][ja][:, sub, :], cg["ap"][jj], cv["ap"][jj], ALU.mult))
                P.dma("sp", actd[pr * 256:(pr + 1) * 256, :].rearrange("(s p) t -> p s t", p=128), actT["ap"][ja], [actT["b"][ja]], [])
            P.barrier()
            fgroups = list(groups)
            KD = DFF // 128
            def f_group(g0, gn):
                ar.reset()
                aT = ar.bf16(KD * gn).rearrange("p (k t) -> p k t", k=KD)
                b_aT = P.buf("aT")
                P.dma("sp", aT, actd[:, g0:g0 + gn].rearrange("(k p) t -> p k t", p=128), [], [b_aT])
                wbf = AN(lambda: ar.bf16(KD * 256).rearrange("p (k n) -> p k n", k=KD), "Dwbf", 3)
                rs = A2(lambda: ar.f32(512), "Drs")
                ot = A2(lambda: ar.f32(512), "Dot")
                out_proj_residual(I["w_down"][l], KD, aT, [b_aT], g0, gn, wbf, rs, ot, ncolblk=256)
                P.barrier()
            for (g0_, gn_) in fgroups:
                f_group(g0_, gn_)

    if cfg.get("do_final", True):
        ar.reset()
        gB = ar.f32(D); gB_buf = P.buf("gBfin")
        work = (A2(lambda: ar.f32(D), "xt"), A2(lambda: ar.bf16(D), "xb"), A2(lambda: ar.f32(D), "sq"), A2(lambda: ar.f32(4), "st"))
        blks = []
        for (t0, n) in tblocks:
            dst = O["y_prompt"][t0:t0 + n, :] if t0 < S else O["y_sample"][0:n, :]
            blks.append((t0, n, res[t0:t0 + n, :], dst))
        phase_norm(blks, I["norm_final"][0:1, :], None, None, gB, gB_buf, work)

    P.emit()
    es.close()
    return nc


def _consts(R):
    c = {}
    c["ident"] = np.eye(128, dtype=np.float32)
    c["maskL"] = np.tril(np.ones((128, 128), np.float32), -1)
    c["negmask"] = np.where(np.arange(128)[:, None] > np.arange(128)[None, :], -30000.0, 0.0).astype(np.float32)
    sh = np.zeros((16, 16, 128), np.float32)
    for h in range(16):
        sh[h, h, :] = 1.0
    c["selh"] = sh.reshape(16, 16 * 128)
    s1 = np.zeros((128, 128), np.float32)
    s1[127, :] = 1.0
    c["sel128"] = s1
    s2 = np.zeros((128, 128), np.float32)
    s2[R - 1, :] = 1.0
    c["selR"] = s2
    return c


_WEIGHTS = ["norm_mix", "w_in", "lru_conv_w", "lru_conv_b", "lru_wa", "lru_ba", "lru_wx", "lru_bx", "lru_lam",
            "ssd_conv_w", "ssd_conv_b", "ssd_dt_bias", "ssd_a_log", "ssd_d", "ssd_norm", "w_branch", "w_out",
            "norm_xattn", "norm_mem", "w_xq", "w_xk", "w_xv", "w_xo", "norm_ffn", "w_up", "ffn_conv_w",
            "ffn_conv_b", "w_down"]


def make_in_map(inp, pb, sb, R):
    f = lambda a: np.ascontiguousarray(np.asarray(a, dtype=np.float32))
    m = dict(_consts(R))
    for k in _WEIGHTS:
        m[k] = f(inp[k])
    m["norm_final"] = f(inp["norm_final"]).reshape(1, D)
    m["x_prompt"] = f(inp["x_prompt"][pb])
    m["x_sample"] = f(inp["x_sample"][sb])
    m["mem_prompt"] = f(inp["mem_prompt"][pb])
    m["cache_sb_k"] = f(inp["cache_sb_k"][:, sb])
    m["cache_sb_v"] = f(inp["cache_sb_v"][:, sb])
    m["cache_mem_k"] = f(inp["cache_mem_k"][:, sb]).reshape(DEPTH, NMEM, D)
    m["cache_mem_v"] = f(inp["cache_mem_v"][:, sb]).reshape(DEPTH, NMEM, D)
    m["state_lru_conv"] = f(inp["state_lru_conv"][:, sb])
    m["state_lru_h"] = f(inp["state_lru_h"][:, sb])
    m["state_ssd_conv"] = f(inp["state_ssd_conv"][:, sb])
    m["state_ssd"] = f(inp["state_ssd"][:, sb])
    m["state_ffn_conv"] = f(inp["state_ffn_conv"][:, sb])
    return m


_PROMPT_OUT = ["sb_k_prompt", "sb_v_prompt", "lru_conv_prompt", "lru_h_prompt", "ssd_conv_prompt", "ssd_state_prompt",
               "ffn_conv_prompt", "mem_k_prompt", "mem_v_prompt"]
_SAMPLE_OUT = ["sb_k_sample", "sb_v_sample", "lru_conv_sample", "lru_h_sample", "ssd_conv_sample", "ssd_state_sample",
               "ffn_conv_sample"]


def assemble(results, pcores, scores, S, R):
    def stk(name, cores, reshape=None):
        a = np.stack([np.asarray(results[c][name], dtype=np.float32) for c in cores], axis=0)
        return a
    outs = []
    outs.append(stk("y_prompt", pcores))
    outs.append(stk("y_sample", scores))
    for nm in _PROMPT_OUT:
        a = np.moveaxis(stk(nm, pcores), 0, 1)
        if nm.startswith("sb_"):
            a = a.reshape(a.shape[0], a.shape[1], S, NH, 128)
        if nm.startswith("mem_"):
            a = a.reshape(a.shape[0], a.shape[1], NMEM, 4, 512)
        outs.append(np.ascontiguousarray(a))
    for nm in _SAMPLE_OUT:
        a = np.moveaxis(stk(nm, scores), 0, 1)
        if nm.startswith("sb_"):
            a = a.reshape(a.shape[0], a.shape[1], R, NH, 128)
        outs.append(np.ascontiguousarray(a))
    return tuple(outs)


def kernel(**inputs):
    B, S = inputs["x_prompt"].shape[0], inputs["x_prompt"].shape[1]
    SBn, R = inputs["x_sample"].shape[0], inputs["x_sample"].shape[1]
    PAST = inputs["cache_sb_k"].shape[2]
    n = 8
    nc = build(dict(S=S, R=R, PAST=PAST, depth=DEPTH))
    in_maps = [make_in_map(inputs, c % B, c % SBn, R) for c in range(n)]
    res = run_bass_kernel_spmd(nc, in_maps, core_ids=list(range(n)))
    return assemble(res.results, list(range(B)), list(range(SBn)), S, R)
```

```python
import contextlib
import numpy as np
import concourse.bass as bass
import concourse.mybir as mybir
from concourse.bass_utils import run_bass_kernel_spmd

F32 = mybir.dt.float32
BF16 = mybir.dt.bfloat16
AF = mybir.ActivationFunctionType
ALU = mybir.AluOpType
AX = mybir.AxisListType

D = 2048
KC = D // 128
MIX = 1024
NH = 8
DFF = 5632
IN_COLS = 13840
NMEM = 256
EPS = 1e-6
DEPTH = 2


class Buf:
    __slots__ = ("name", "w", "r", "serial")

    def __init__(self, name):
        self.name = name
        self.serial = name.startswith("ps")
        self.w = None
        self.r = []


class Prog:
    ENG = ("pe", "act", "dve", "pool", "sp")

    def __init__(self, nc, es, n_dma_sems=60):
        self.nc = nc
        self.es = es
        self.lists = {e: [] for e in self.ENG}
        self.sems = []
        self.tot = []
        self.isdma = []
        self.eng_sem = {}
        for e in ("pe", "act", "dve", "pool"):
            self.eng_sem[e] = self._new_sem("e_" + e, False)
        self.dma_pool = [self._new_sem("d%d" % i, True) for i in range(n_dma_sems)]
        self.dma_next = 0
        self.sw_pool = [self._new_sem("w%d" % i, True) for i in range(20)]
        self.wsems = [self._new_sem("ws%d" % i, True) for i in range(4)]
        self.sw_next = 0
        self.waited = {e: {} for e in self.ENG}
        self.bufs = []
        self.same_engine_sync = True

    def _new_sem(self, name, isdma):
        s = self.es.enter_context(self.nc.semaphore(name))
        self.sems.append(s)
        self.tot.append(0)
        self.isdma.append(isdma)
        return len(self.sems) - 1

    def buf(self, name):
        b = Buf(name)
        self.bufs.append(b)
        return b

    def bufs_n(self, name, n):
        return [self.buf("%s%d" % (name, i)) for i in range(n)]

    def dma_sem(self, q="sp"):
        if q == "pool":
            s = self.sw_pool[self.sw_next % len(self.sw_pool)]
            self.sw_next += 1
            return s
        s = self.dma_pool[self.dma_next % len(self.dma_pool)]
        self.dma_next += 1
        return s

    def _deps(self, eng, reads, writes):
        deps = []
        for b in reads:
            if b.w is not None:
                deps.append(b.w)
            if b.serial:
                deps.extend(b.r)
        for b in writes:
            if b.w is not None:
                deps.append(b.w)
            deps.extend(b.r)
        waits = []
        wd = self.waited[eng]
        for (s, v) in deps:
            if self.isdma[s]:
                v = self.tot[s]
            elif (not self.same_engine_sync or eng == "pe") and self.eng_sem.get(eng) == s:
                continue
            if wd.get(s, 0) >= v:
                continue
            wd[s] = v
            waits.append((s, v))
        return waits

    def _commit(self, ev, reads, writes):
        for b in reads:
            b.r.append(ev)
        for b in writes:
            b.w = ev
            b.r = []

    def op(self, eng, reads, writes, fn):
        waits = self._deps(eng, reads, writes)
        s = self.eng_sem[eng]
        self.tot[s] += 1
        ev = (s, self.tot[s])
        self.lists[eng].append((waits, fn, s, 1))
        self._commit(ev, reads, writes)

    def dma(self, q, out, in_, reads, writes, sem=None, **kw):
        waits = self._deps(q, reads, writes)
        s = self.dma_sem(q) if sem is None else sem
        self.tot[s] += 16
        ev = (s, self.tot[s])

        def fn(e, out=out, in_=in_, kw=kw):
            return e.dma_start(out=out, in_=in_, **kw)
        self.lists[q].append((waits, fn, s, 16))
        self._commit(ev, reads, writes)

    def barrier(self):
        for e in self.ENG:
            wd = self.waited[e]
            waits = []
            for s in range(len(self.sems)):
                v = self.tot[s]
                if v > 0 and wd.get(s, 0) < v:
                    wd[s] = v
                    waits.append((s, v))
            if waits:
                self.lists[e].append((waits, None, None, 0))
        for b in self.bufs:
            b.w = None
            b.r = []

    def emit(self, final_wait_eng="sp"):
        nc = self.nc
        self.barrier()
        engmap = {"pe": "tensor", "act": "scalar", "dve": "vector", "pool": "gpsimd", "sp": "sync"}
        with nc.Block() as block:
            for e in self.ENG:
                lst = self.lists[e]

                def body(eng, lst=lst):
                    for (waits, fn, s, amt) in lst:
                        for (ws, wv) in waits:
                            eng.wait_ge(self.sems[ws], wv)
                        if fn is not None:
                            ins = fn(eng)
                            ins.then_inc(self.sems[s], amt)
                getattr(block, engmap[e])(body)


class Arena:
    def __init__(self, nc, es, name, words):
        self.t = es.enter_context(nc.sbuf_tensor(name, [128, words], F32))
        self.words = words
        self.off = 0
        self.base = 0

    def reset(self):
        self.off = self.base

    def pin(self):
        self.base = self.off

    def f32(self, n):
        assert self.off + n <= self.words, ("arena overflow", self.off, n, self.words)
        ap = self.t[:, self.off:self.off + n]
        self.off += n
        return ap

    def bf16(self, n):
        w = (n + 1) // 2
        ap = self.f32(w).bitcast(BF16)
        return ap[:, 0:n]


def build(cfg):
    S = cfg["S"]
    R = cfg["R"]
    PAST = cfg["PAST"]
    debug = cfg.get("debug", False)
    T = S + R
    NB = S // 128
    tblocks = [(i * 128, 128) for i in range(NB)] + [(S, R)]
    tchunks = [(i * 512, min(512, S - i * 512)) for i in range((S + 511) // 512)] + [(S, R)]

    nc = bass.Bass("TRN2", target_bir_lowering=False)
    es = contextlib.ExitStack()
    P = Prog(nc, es)

    def din(name, shape, dt=F32):
        return nc.dram_tensor(name, list(shape), dt, kind="ExternalInput").ap()

    def dout(name, shape, dt=F32):
        return nc.dram_tensor(name, list(shape), dt, kind="ExternalOutput").ap()

    def dscr(name, shape, dt=BF16):
        return nc.dram_tensor(name, list(shape), dt, kind=("ExternalOutput" if debug else "Internal")).ap()

    I = {}
    I["x_prompt"] = din("x_prompt", [S, D])
    I["x_sample"] = din("x_sample", [R, D])
    I["norm_mix"] = din("norm_mix", [DEPTH, D])
    I["w_in"] = din("w_in", [DEPTH, D, IN_COLS])
    I["ident"] = din("ident", [128, 128])
    I["maskL"] = din("maskL", [128, 128])
    for nm, shp in [("w_branch", [DEPTH, 3, MIX, D]), ("w_out", [DEPTH, D, D]), ("norm_xattn", [DEPTH, D]), ("norm_mem", [DEPTH, D]),
                    ("w_xq", [DEPTH, D, D]), ("w_xk", [DEPTH, D, D]), ("w_xv", [DEPTH, D, D]), ("w_xo", [DEPTH, D, D]),
                    ("norm_ffn", [DEPTH, D]), ("w_up", [DEPTH, D, 2 * DFF]), ("ffn_conv_w", [DEPTH, 3, 2 * DFF]),
                    ("ffn_conv_b", [DEPTH, 2 * DFF]), ("w_down", [DEPTH, DFF, D]), ("norm_final", [1, D]),
                    ("mem_prompt", [NMEM, D]), ("cache_mem_k", [DEPTH, NMEM, D]), ("cache_mem_v", [DEPTH, NMEM, D]),
                    ("state_ffn_conv", [DEPTH, 2, 2 * DFF])]:
        I[nm] = din(nm, shp)
    for nm, shp in [("ssd_conv_w", [DEPTH, 4, 1536]), ("ssd_conv_b", [DEPTH, 1536]), ("ssd_dt_bias", [DEPTH, 16]),
                    ("ssd_a_log", [DEPTH, 16]), ("ssd_d", [DEPTH, 16]), ("ssd_norm", [DEPTH, MIX]),
                    ("state_ssd_conv", [DEPTH, 3, 1536]), ("state_ssd", [DEPTH, 16, 64, 128]),
                    ("negmask", [128, 128]), ("selh", [16, 16 * 128]), ("sel128", [128, 128]), ("selR", [128, 128])]:
        I[nm] = din(nm, shp)
    for nm, shp in [("lru_conv_w", [DEPTH, 4, MIX]), ("lru_conv_b", [DEPTH, MIX]), ("lru_wa", [DEPTH, 8, 128, 128]),
                    ("lru_ba", [DEPTH, MIX]), ("lru_wx", [DEPTH, 8, 128, 128]), ("lru_bx", [DEPTH, MIX]),
                    ("lru_lam", [DEPTH, MIX]), ("state_lru_conv", [DEPTH, 3, MIX]), ("state_lru_h", [DEPTH, MIX])]:
        I[nm] = din(nm, shp)
    I["cache_sb_k"] = din("cache_sb_k", [DEPTH, PAST, NH, 128])
    I["cache_sb_v"] = din("cache_sb_v", [DEPTH, PAST, NH, 128])

    O = {}
    O["sb_k_prompt"] = dout("sb_k_prompt", [DEPTH, S, MIX])
    O["sb_v_prompt"] = dout("sb_v_prompt", [DEPTH, S, MIX])
    O["sb_k_sample"] = dout("sb_k_sample", [DEPTH, R, MIX])
    O["sb_v_sample"] = dout("sb_v_sample", [DEPTH, R, MIX])

    for nm, shp in [("lru_conv_prompt", [DEPTH, 3, MIX]), ("lru_h_prompt", [DEPTH, MIX]),
                    ("lru_conv_sample", [DEPTH, 3, MIX]), ("lru_h_sample", [DEPTH, MIX])]:
        O[nm] = dout(nm, shp)
    for nm, shp in [("ssd_conv_prompt", [DEPTH, 3, 1536]), ("ssd_state_prompt", [DEPTH, 16, 64, 128]),
                    ("ssd_conv_sample", [DEPTH, 3, 1536]), ("ssd_state_sample", [DEPTH, 16, 64, 128])]:
        O[nm] = dout(nm, shp)
    for nm, shp in [("y_prompt", [S, D]), ("y_sample", [R, D]), ("ffn_conv_prompt", [DEPTH, 2, 2 * DFF]),
                    ("ffn_conv_sample", [DEPTH, 2, 2 * DFF]), ("mem_k_prompt", [DEPTH, NMEM, D]), ("mem_v_prompt", [DEPTH, NMEM, D])]:
        O[nm] = dout(nm, shp)
    actd = dscr("actd", [DFF, T])
    pj = dscr("pj", [IN_COLS + 112, T])
    vtok = dscr("vtok", [T, MIX])
    ztok = dscr("ztok", [T, MIX])
    brT = dscr("brT", [3, MIX, T])

    ar = Arena(nc, es, "arena", 46 * 1024)
    psum = es.enter_context(nc.psum_tensor("psum", [128, 4096], F32))
    pbank = [psum[:, i * 512:(i + 1) * 512] for i in range(8)]
    pbuf = P.bufs_n("ps", 8)
    P.pin_bufs = list(pbuf)

    ident_bf = ar.bf16(128)
    ident_f = ar.f32(128)
    b_const = P.buf("const")
    P.dma("pool", ident_bf, I["ident"], [], [b_const])
    P.dma("sp", ident_f, I["ident"], [], [b_const])
    maskL = ar.f32(128)
    P.dma("sp", maskL, I["maskL"], [], [b_const])
    ar.pin()
    P.barrier()
    pbuf = P.bufs_n("ps", 8)

    state = {"pbi": 0}
    res = dscr("res", [T, D], F32)

    def res_src(l, t0, n, c0=0, ncol=D):
        if l == 0 and not state.get("res_live"):
            if t0 < S:
                return I["x_prompt"][t0:t0 + n, c0:c0 + ncol]
            return I["x_sample"][t0 - S:t0 - S + n, c0:c0 + ncol]
        return res[t0:t0 + n, c0:c0 + ncol]

    def split_range(g0, gn, step):
        out = []
        t = g0
        while t < g0 + gn:
            lim = S if t < S else g0 + gn
            n = min(step, lim - t, g0 + gn - t)
            out.append((t, n))
            t += n
        return out

    def gemm_fm2(w_ap, w_buf, ncols, xT, x_bufs, nk, g0, chunks, evac):
        nsub = (ncols + 127) // 128
        for (t0, n) in chunks:
            for sub in range(nsub):
                m = min(128, ncols - sub * 128)
                pb = next_bank()

                def mm(e, sub=sub, m=m, t0=t0, n=n, pb=pb):
                    ins = None
                    for k in range(nk):
                        ins = e.matmul(pbank[pb][0:m, 0:n], w_ap[:, k, sub * 128:sub * 128 + m],
                                       xT[:, k, t0 - g0:t0 - g0 + n], start=(k == 0), stop=(k == nk - 1))
                    return ins
                P.op("pe", [w_buf] + x_bufs, [pbuf[pb]], mm)
                evac(pbank[pb], pbuf[pb], sub, t0, n, m)

    def gemm_tm2(w_ap, w_buf, ncols, xT, x_bufs, nk, g0, blocks, evac):
        for (t0, n) in blocks:
            pb = next_bank()

            def mm(e, t0=t0, n=n, pb=pb):
                ins = None
                for k in range(nk):
                    ins = e.matmul(pbank[pb][0:n, 0:ncols], xT[:, k, t0 - g0:t0 - g0 + n], w_ap[:, k, 0:ncols],
                                   start=(k == 0), stop=(k == nk - 1))
                return ins
            P.op("pe", [w_buf] + x_bufs, [pbuf[pb]], mm)
            evac(pbank[pb], pbuf[pb], t0, n)

    def next_bank():
        i = state["pbi"] % 8
        state["pbi"] += 1
        return i

    def phase_norm(blocks, gain_ap, xnT, xnT_bufs, gB, gB_buf, work, final_dst=None):
        P.dma("sp", gB, gain_ap.partition_broadcast(128), [], [gB_buf])
        xt, xb, sq, st = work
        for bi, blk in enumerate(blocks):
            t0, n, src = blk[0], blk[1], blk[2]
            j = bi % 2
            P.dma("sp", xt["ap"][j][0:n, :], src, [], [xt["b"][j]])
            P.op("act", [xt["b"][j]], [sq["b"][j], st["b"][j]],
                 lambda e, j=j, n=n: e.activation(out=sq["ap"][j][0:n, :], in_=xt["ap"][j][0:n, :],
                                                  func=AF.Square, accum_out=st["ap"][j][0:n, 0:1]))
            P.op("dve", [st["b"][j]], [st["b"][j]],
                 lambda e, j=j, n=n: e.tensor_scalar(st["ap"][j][0:n, 1:2], st["ap"][j][0:n, 0:1],
                                                     1.0 / D, EPS, ALU.mult, ALU.add))
            P.op("act", [st["b"][j]], [st["b"][j]],
                 lambda e, j=j, n=n: e.activation(out=st["ap"][j][0:n, 2:3], in_=st["ap"][j][0:n, 1:2],
                                                  func=AF.Sqrt))
            P.op("dve", [st["b"][j]], [st["b"][j]],
                 lambda e, j=j, n=n: e.reciprocal(st["ap"][j][0:n, 3:4], st["ap"][j][0:n, 2:3]))
            if len(blk) > 3:
                P.op("dve", [st["b"][j], xt["b"][j], gB_buf], [sq["b"][j]],
                     lambda e, j=j, n=n: e.scalar_tensor_tensor(sq["ap"][j][0:n, :], xt["ap"][j][0:n, :],
                                                                st["ap"][j][0:n, 3:4], gB[0:n, :],
                                                                ALU.mult, ALU.mult))
                P.dma("sp", blk[3], sq["ap"][j][0:n, :], [sq["b"][j]], [])
                continue
            P.op("dve", [st["b"][j], xt["b"][j], gB_buf], [xb["b"][j]],
                 lambda e, j=j, n=n: e.scalar_tensor_tensor(xb["ap"][j][0:n, :], xt["ap"][j][0:n, :],
                                                            st["ap"][j][0:n, 3:4], gB[0:n, :],
                                                            ALU.mult, ALU.mult))
            for half in range(2):
                pb = next_bank()
                pv = pbank[pb].bitcast(BF16)

                def tr(e, j=j, n=n, half=half, pv=pv):
                    ins = None
                    for c in range(8):
                        k = half * 8 + c
                        ins = e.transpose(pv[:, c * 128:c * 128 + n], xb["ap"][j][0:n, k * 128:(k + 1) * 128],
                                          ident_bf[0:n, 0:n])
                    return ins
                P.op("pe", [xb["b"][j], b_const], [pbuf[pb]], tr)
                eng = "act" if half == 0 else "dve"

                def ev(e, n=n, half=half, pv=pv, t0=t0, eng=eng):
                    src = pv.rearrange("p (c t) -> p c t", c=8)[:, :, 0:n]
                    dst = xnT[:, half * 8:(half + 1) * 8, t0:t0 + n]
                    if eng == "act":
                        return e.copy(out=dst, in_=src)
                    return e.tensor_copy(dst, src)
                P.op(eng, [pbuf[pb]], [xnT_bufs[bi]], ev)

    wrot = {"n": 0}

    def load_w2(wsrc, c0, ncols, nkc, wbf):
        nb = len(wbf["ap"])
        j = wrot["n"] % nb
        wrot["n"] += 1
        P.dma("pool", wbf["ap"][j][:, 0:nkc, 0:ncols], wsrc.rearrange("(k p) n -> p k n", p=128)[:, :, c0:c0 + ncols],
              [], [wbf["b"][j]], sem=P.wsems[j])
        return wbf["ap"][j], wbf["b"][j]

    def AN(fn, nm, n):
        return {"ap": [fn() for _ in range(n)], "b": P.bufs_n(nm, n)}

    def gemm_fm(wbf_ap, wbf_buf, ncols, xT, x_bufs, nk, evac):
        nsub = (ncols + 127) // 128
        for ti, (t0, n) in enumerate(tchunks):
            for sub in range(nsub):
                m = min(128, ncols - sub * 128)
                pb = next_bank()

                def mm(e, sub=sub, m=m, t0=t0, n=n, pb=pb):
                    ins = None
                    for k in range(nk):
                        ins = e.matmul(pbank[pb][0:m, 0:n], wbf_ap[:, k, sub * 128:sub * 128 + m],
                                       xT[:, k, t0:t0 + n], start=(k == 0), stop=(k == nk - 1))
                    return ins
                P.op("pe", [wbf_buf] + x_bufs, [pbuf[pb]], mm)
                evac(pbank[pb], pbuf[pb], sub, ti, t0, n, m)

    def gemm_tm(wbf_ap, wbf_buf, ncols, xT, x_bufs, nk, evac):
        for bi, (t0, n) in enumerate(tblocks):
            pb = next_bank()

            def mm(e, t0=t0, n=n, pb=pb):
                ins = None
                for k in range(nk):
                    ins = e.matmul(pbank[pb][0:n, 0:ncols], xT[:, k, t0:t0 + n], wbf_ap[:, k, 0:ncols],
                                   start=(k == 0), stop=(k == nk - 1))
                return ins
            P.op("pe", [wbf_buf, x_bufs[bi]], [pbuf[pb]], mm)
            evac(pbank[pb], pbuf[pb], bi, t0, n)

    def dbl(alloc, name):
        aps = [alloc(), alloc()]
        return {"ap": aps, "b": P.bufs_n(name, 2)}

    for l in range(cfg["depth"]):
        ar.reset()
        xnT = ar.bf16(KC * T).rearrange("p (k t) -> p k t", k=KC)
        xnT_bufs = P.bufs_n("xnT", len(tblocks))
        mark = ar.off
        gB = ar.f32(D)
        gB_buf = P.buf("gB")
        work = (dbl(lambda: ar.f32(D), "xt"), dbl(lambda: ar.bf16(D), "xb"),
                dbl(lambda: ar.f32(D), "sq"), dbl(lambda: ar.f32(4), "st"))
        phase_norm([(t0, n, res_src(l, t0, n)) for (t0, n) in tblocks], I["norm_mix"][l:l + 1, :], xnT, xnT_bufs, gB, gB_buf, work)
        P.barrier()
        pbuf = P.bufs_n("ps", 8)
        xnT_bufs = P.bufs_n("xnT", len(tblocks))
        ar.off = mark
        wbf = AN(lambda: ar.bf16(KC * 512).rearrange("p (k n) -> p k n", k=KC), "wbf", 3)
        ofm = dbl(lambda: ar.bf16(4 * 512).rearrange("p (s n) -> p s n", s=4), "ofm")
        otm = dbl(lambda: ar.f32(512), "otm")
        otb = dbl(lambda: ar.bf16(512), "otb")
        cnt = {"fm": 0, "tm": 0, "dt": 0}
        dtr = dbl(lambda: ar.bf16(512), "dtr")
        ncb = (IN_COLS + 511) // 512
        for cb in range(cfg.get("ncb", ncb)):
            c0 = cb * 512
            ncols = min(512, IN_COLS - c0)
            wap, wbuf = load_w2(I["w_in"][l], c0, ncols, KC, wbf)
            is_gate = c0 >= 7680
            tm_kind = None
            if 1024 <= c0 < 2048:
                tm_kind = "k"
            elif 2048 <= c0 < 3072:
                tm_kind = "v"
            elif 5120 <= c0 < 6144:
                tm_kind = "z"
            if tm_kind != "v" and tm_kind != "z":
                nsub = (ncols + 127) // 128

                def evac(ps_ap, ps_buf, sub, ti, t0, n, m, c0=c0, nsub=nsub, is_gate=is_gate):
                    jj = cnt["fm"] % 2
                    if c0 == 7680 and sub == 0:
                        jd = cnt["dt"] % 2
                        cnt["dt"] += 1
                        P.op("dve", [ps_buf], [dtr["b"][jd]],
                             lambda e: e.tensor_copy(dtr["ap"][jd][0:16, 0:n], ps_ap[0:16, 0:n]))
                        P.dma("sp", pj[IN_COLS:IN_COLS + 16, t0:t0 + n], dtr["ap"][jd][0:16, 0:n], [dtr["b"][jd]], [])
                    eng = "act" if (sub % 2 == 0 or is_gate) else "dve"
                    dst = ofm["ap"][jj][0:m, sub, 0:n]
                    wr = [ofm["b"][jj]]
                    if eng == "act":
                        P.op("act", [ps_buf], wr,
                             lambda e: e.activation(out=dst, in_=ps_ap[0:m, 0:n],
                                                    func=(AF.Sigmoid if is_gate else AF.Copy)))
                    else:
                        P.op("dve", [ps_buf], wr, lambda e: e.tensor_copy(dst, ps_ap[0:m, 0:n]))
                    if sub == nsub - 1:
                        if m == 128:
                            P.dma("sp", pj[c0:c0 + nsub * 128, t0:t0 + n].rearrange("(s p) t -> p s t", p=128),
                                  ofm["ap"][jj][:, 0:nsub, 0:n], [ofm["b"][jj]], [])
                        else:
                            if nsub > 1:
                                P.dma("sp", pj[c0:c0 + (nsub - 1) * 128, t0:t0 + n].rearrange("(s p) t -> p s t", p=128),
                                      ofm["ap"][jj][:, 0:nsub - 1, 0:n], [ofm["b"][jj]], [])
                            P.dma("sp", pj[c0 + (nsub - 1) * 128:c0 + (nsub - 1) * 128 + m, t0:t0 + n],
                                  ofm["ap"][jj][0:m, nsub - 1, 0:n], [ofm["b"][jj]], [])
                        cnt["fm"] += 1
                gemm_fm(wap, wbuf, ncols, xnT, xnT_bufs, KC, evac)
            if tm_kind is not None:
                def evac_t(ps_ap, ps_buf, bi, t0, n, c0=c0, tm_kind=tm_kind):
                    jj = cnt["tm"] % 2
                    cnt["tm"] += 1
                    import os
                    kd = os.environ.get("KDBG", "")
                    if tm_kind in ("k", "v") and "noact" not in kd:
                        P.op("act", [ps_buf], [otm["b"][jj]],
                             lambda e: e.copy(out=otm["ap"][jj][0:n, :], in_=ps_ap[0:n, 0:512]))
                        cc = c0 - (1024 if tm_kind == "k" else 2048)
                        if t0 < S:
                            dst = O["sb_%s_prompt" % tm_kind][l, t0:t0 + n, cc:cc + 512]
                        else:
                            dst = O["sb_%s_sample" % tm_kind][l, 0:n, cc:cc + 512]
                        P.dma("sp", dst, otm["ap"][jj][0:n, :], [otm["b"][jj]], [])
                    if tm_kind in ("v", "z") and "nodve" not in kd:
                        if tm_kind == "v":
                            P.op("dve", [otm["b"][jj]], [otb["b"][jj]],
                                 lambda e: e.tensor_copy(otb["ap"][jj][0:n, :], otm["ap"][jj][0:n, :]))
                        else:
                            P.op("dve", [ps_buf], [otb["b"][jj]],
                                 lambda e: e.tensor_copy(otb["ap"][jj][0:n, :], ps_ap[0:n, 0:512]))
                        cc = c0 - (2048 if tm_kind == "v" else 5120)
                        dd = vtok if tm_kind == "v" else ztok
                        P.dma("sp", dd[t0:t0 + n, cc:cc + 512], otb["ap"][jj][0:n, :], [otb["b"][jj]], [])
                gemm_tm(wap, wbuf, 512, xnT, xnT_bufs, KC, evac_t)
        P.barrier()

        if cfg.get("do_A", True):
            ar.reset()
            PB = PAST // 128
            NK = max(S, PAST + R)
            NKB = (NK + 127) // 128
            ones = ar.f32(NK)
            b_ones = P.buf("ones")
            P.op("dve", [], [b_ones], lambda e: e.memset(ones, 1.0))
            hd = []
            for j in range(2):
                hd.append(dict(
                    qT=ar.bf16(T), kT=ar.bf16(S), kTs=ar.bf16(PAST + R),
                    vh=ar.bf16(NB * 128).rearrange("p (b d) -> p b d", b=NB),
                    vs=ar.bf16((PB + 1) * 128).rearrange("p (b d) -> p b d", b=PB + 1),
                    ck=ar.bf16(PB * 128).rearrange("p (b d) -> p b d", b=PB),
                    oT=ar.bf16(T),
                    b_q=P.buf("hq"), b_k=P.buf("hk"), b_ks=P.buf("hks"), b_v=P.buf("hv"), b_vs=P.buf("hvs"),
                    b_ck=P.buf("hck"), b_o=P.buf("ho")))
            wk = []
            NWK = 3
            for j in range(NWK):
                d_ = dict(E2=ar.f32(NK), c=ar.f32(NK), rec=ar.f32(NK),
                          A=ar.bf16(NK), AT=ar.bf16(NKB * 128).rearrange("p (b q) -> p b q", b=NKB),
                          nt=ar.f32(2),
                          b_E2=P.buf("wE2"), b_c=P.buf("wc"),
                          b_rec=P.buf("wrec"), b_A=P.buf("wA"), b_AT=P.buf("wAT"), b_nt=P.buf("wnt"))
                d_["W"] = d_["E2"]
                d_["b_W"] = d_["b_E2"]
                wk.append(d_)
            acnt = {"n": 0}
            scale = 128 ** -0.5

            def attend(qT_ap, b_q, nq, kT_ap, b_k, nk, vblocks, b_v, oT_ap, b_o):
                w = wk[acnt["n"] % NWK]
                acnt["n"] += 1
                nd = nq
                for c0 in range(0, nk, 512):
                    n = min(512, nk - c0)
                    pb = next_bank()
                    P.op("pe", [b_q, b_k], [pbuf[pb]],
                         lambda e, pb=pb, c0=c0, n=n: e.matmul(pbank[pb][0:nq, 0:n], qT_ap, kT_ap[:, c0:c0 + n],
                                                               start=True, stop=True))
                    P.op("act", [pbuf[pb]], [w["b_E2"]],
                         lambda e, pb=pb, c0=c0, n=n: e.activation(out=w["E2"][0:nq, c0:c0 + n], in_=pbank[pb][0:nq, 0:n],
                                                                   func=AF.Exp, scale=scale))
                    P.op("act", [pbuf[pb]], [w["b_rec"]],
                         lambda e, pb=pb, c0=c0, n=n: e.activation(out=w["rec"][0:nq, c0:c0 + n], in_=pbank[pb][0:nq, 0:n],
                                                                   func=AF.Exp, scale=-scale))
                P.op("act", [w["b_E2"]], [w["b_W"]],
                     lambda e: e.activation(out=w["W"][0:nq, 0:nk], in_=w["E2"][0:nq, 0:nk], func=AF.Ln, bias=1.0))
                P.op("act", [w["b_rec"]], [w["b_rec"]],
                     lambda e: e.activation(out=w["rec"][0:nq, 0:nk], in_=w["rec"][0:nq, 0:nk], func=AF.Ln, bias=1.0))
                fns = [lambda: attend_s1b(w, nq, nk, nd), lambda: attend_s2(w, nq, vblocks, b_v, oT_ap, b_o)]
                for item in reversed(list(inflight)):
                    item.pop(0)()
                    if not item:
                        inflight.remove(item)
                inflight.append(fns)

            inflight = []

            def attend_flush():
                while inflight:
                    for item in list(inflight):
                        item.pop(0)()
                        if not item:
                            inflight.remove(item)

            def attend_s1b(w, nq, nk, nd):
                P.op("dve", [w["b_W"], b_const], [w["b_W"]],
                     lambda e: e.tensor_tensor(w["W"][0:nq, nk - nd:nk], w["W"][0:nq, nk - nd:nk],
                                               maskL[0:nq, 0:nd], ALU.mult))
                P.op("dve", [w["b_W"], b_ones], [w["b_c"]],
                     lambda e: e.tensor_tensor_scan(w["c"][0:nq, 0:nk], ones[0:nq, 0:nk], w["W"][0:nq, 0:nk],
                                                    0.0, ALU.mult, ALU.add))
                P.op("dve", [w["b_c"]], [w["b_nt"]],
                     lambda e: e.tensor_scalar_mul(w["nt"][0:nq, 0:1], w["c"][0:nq, nk - 1:nk], -1.0))
                P.op("dve", [w["b_c"], w["b_rec"]], [w["b_c"]],
                     lambda e: e.tensor_tensor(w["c"][0:nq, 0:nk], w["c"][0:nq, 0:nk], w["rec"][0:nq, 0:nk], ALU.subtract))
                P.op("act", [w["b_c"], w["b_nt"]], [w["b_A"]],
                     lambda e: e.activation(out=w["A"][0:nq, 0:nk], in_=w["c"][0:nq, 0:nk], func=AF.Exp,
                                            bias=w["nt"][0:nq, 0:1]))
                P.op("dve", [w["b_A"], b_const], [w["b_A"]],
                     lambda e: e.tensor_tensor(w["A"][0:nq, nk - nd:nk], w["A"][0:nq, nk - nd:nk],
                                               maskL[0:nq, 0:nd], ALU.mult))

            def attend_s2(w, nq, vblocks, b_v, oT_ap, b_o):
                nblk = len(vblocks)
                groups = []
                for i in range(nblk):
                    if groups and len(groups[-1]) < 8 and vblocks[groups[-1][0]][2] == vblocks[i][2]:
                        groups[-1].append(i)
                    else:
                        groups.append([i])
                for gi, grp in enumerate(groups):
                    g0, gn = grp[0], len(grp)
                    pb = next_bank()
                    pv = pbank[pb].bitcast(BF16)

                    def tr(e, g0=g0, gn=gn, pv=pv):
                        ins = None
                        for i in range(gn):
                            k0, kn = vblocks[g0 + i][1], vblocks[g0 + i][2]
                            ins = e.transpose(pv[0:kn, i * 128:i * 128 + nq], w["A"][0:nq, k0:k0 + kn],
                                              ident_bf[0:nq, 0:nq])
                        return ins
                    P.op("pe", [w["b_A"], b_const], [pbuf[pb]], tr)
                    kmax = max(vblocks[g0 + i][2] for i in range(gn))
                    eng = "dve" if gi % 2 == 0 else "act"

                    def ev(e, g0=g0, gn=gn, pv=pv, eng=eng, kmax=kmax):
                        src = pv.rearrange("p (b q) -> p b q", b=8)[0:kmax, 0:gn, 0:nq]
                        dst = w["AT"][0:kmax, g0:g0 + gn, 0:nq]
                        if eng == "act":
                            return e.copy(out=dst, in_=src)
                        return e.tensor_copy(dst, src)
                    P.op(eng, [pbuf[pb]], [w["b_AT"]], ev)
                pb = next_bank()

                def av(e, pb=pb):
                    ins = None
                    for i, (vap, k0, kn) in enumerate(vblocks):
                        ins = e.matmul(pbank[pb][:, 0:nq], vap, w["AT"][0:kn, i, 0:nq],
                                       start=(i == 0), stop=(i == nblk - 1))
                    return ins
                P.op("pe", [b_v, w["b_AT"]], [pbuf[pb]], av)
                P.op("dve", [pbuf[pb]], [b_o], lambda e, pb=pb: e.tensor_copy(oT_ap, pbank[pb][:, 0:nq]))

            for h in range(NH):
                H = hd[h % 2]
                P.dma("sp", H["qT"], pj[h * 128:(h + 1) * 128, :], [], [H["b_q"]])
                P.dma("sp", H["kT"], pj[1024 + h * 128:1024 + (h + 1) * 128, 0:S], [], [H["b_k"]])
                P.dma("sp", H["vh"], vtok[0:S, h * 128:(h + 1) * 128].rearrange("(b p) d -> p b d", p=128),
                      [], [H["b_v"]])
                P.dma("pool", H["ck"], I["cache_sb_k"][l, :, h, :].rearrange("(b p) d -> p b d", p=128),
                      [], [H["b_ck"]])
                P.dma("pool", H["vs"][:, 0:PB, :], I["cache_sb_v"][l, :, h, :].rearrange("(b p) d -> p b d", p=128),
                      [], [H["b_vs"]])
                P.dma("sp", H["vs"][0:R, PB, :], vtok[S:S + R, h * 128:(h + 1) * 128], [], [H["b_vs"]])
                P.dma("sp", H["kTs"][:, PAST:PAST + R], pj[1024 + h * 128:1024 + (h + 1) * 128, S:S + R],
                      [], [H["b_ks"]])
                for g0 in range(0, PB, 8):
                    gn = min(8, PB - g0)
                    pb = next_bank()
                    pv = pbank[pb].bitcast(BF16)

                    def trk(e, g0=g0, gn=gn, pv=pv, H=H):
                        ins = None
                        for i in range(gn):
                            ins = e.transpose(pv[:, i * 128:(i + 1) * 128], H["ck"][:, g0 + i, :], ident_bf)
                        return ins
                    P.op("pe", [H["b_ck"], b_const], [pbuf[pb]], trk)
                    P.op("dve", [pbuf[pb]], [H["b_ks"]],
                         lambda e, g0=g0, gn=gn, pv=pv, H=H: e.tensor_copy(H["kTs"][:, g0 * 128:(g0 + gn) * 128],
                                                                           pv[:, 0:gn * 128]))
                for qb in range(NB):
                    vbl = [(H["vh"][:, i, :], i * 128, 128) for i in range(qb + 1)]
                    attend(H["qT"][:, qb * 128:(qb + 1) * 128], H["b_q"], 128, H["kT"], H["b_k"], (qb + 1) * 128,
                           vbl, H["b_v"], H["oT"][:, qb * 128:(qb + 1) * 128], H["b_o"])
                vbl = [(H["vs"][:, i, :], i * 128, 128) for i in range(PB)] + [(H["vs"][0:R, PB, :], PAST, R)]
                attend(H["qT"][:, S:S + R], H["b_q"], R, H["kTs"], H["b_ks"], PAST + R,
                       vbl, H["b_vs"], H["oT"][:, S:S + R], H["b_o"])
                attend_flush()
                P.dma("sp", brT[0, h * 128:(h + 1) * 128, :], H["oT"], [H["b_o"]], [])
            P.barrier()

        if cfg.get("do_L", True):
            ar.reset()
            NC_ = {"allow_slow_non_contiguous": True}
            pp = ar.f32(8 * 16).rearrange("p (c x) -> p c x", c=8)
            b_pp = P.buf("pp")
            for k in range(4):
                P.dma("sp", pp[:, :, k], I["lru_conv_w"][l, k, :].rearrange("(c p) -> p c", p=128), [], [b_pp], **NC_)
            for col, nm in [(4, "lru_conv_b"), (5, "lru_ba"), (6, "lru_bx"), (7, "lru_lam"), (10, "state_lru_h")]:
                P.dma("sp", pp[:, :, col], I[nm][l, :].rearrange("(c p) -> p c", p=128), [], [b_pp], **NC_)
            P.op("act", [b_pp], [b_pp], lambda e: e.activation(out=pp[:, :, 8], in_=pp[:, :, 7], func=AF.Exp, scale=-1.0))
            P.op("dve", [b_pp], [b_pp], lambda e: e.tensor_scalar_add(pp[:, :, 8], pp[:, :, 8], 1.0))
            P.op("act", [b_pp], [b_pp], lambda e: e.activation(out=pp[:, :, 9], in_=pp[:, :, 8], func=AF.Ln))
            P.op("dve", [b_pp], [b_pp], lambda e: e.tensor_scalar_mul(pp[:, :, 8], pp[:, :, 9], -8.0))
            P.op("dve", [b_pp], [b_pp], lambda e: e.tensor_scalar_mul(pp[:, :, 9], pp[:, :, 9], -16.0))
            wa_bf = ar.bf16(8 * 128).rearrange("p (n j) -> p n j", n=8)
            wx_bf = ar.bf16(8 * 128).rearrange("p (n j) -> p n j", n=8)
            b_wg = P.buf("wg")
            P.dma("pool", wa_bf, I["lru_wa"][l].rearrange("n k j -> k n j"), [], [b_wg])
            P.dma("pool", wx_bf, I["lru_wx"][l].rearrange("n k j -> k n j"), [], [b_wg])
            TP = T + 6
            segs = [(0, S, "prompt"), (S + 3, R, "sample")]
            names = ["xpad", "xc", "r", "i", "a", "s2", "h", "lg", "t"]
            lsets = []
            for _i in range(2):
                lsets.append(({nm: ar.f32(TP) for nm in names}, {nm: P.buf("L" + nm) for nm in names},
                              ar.bf16(TP), P.buf("Lxcb"), ar.bf16(T), P.buf("Lob")))

            def l_chunk(c, tl, tb, xcb, b_xcb, ob, b_ob):
                    tl = dict(tl); tb = dict(tb)
                    tl["u"] = tl["i"]; tb["u"] = tb["i"]
                    row = 3072 + c * 128
                    P.op("dve", [], [tb["xpad"]], lambda e: e.memset(tl["xpad"][:, 0:3], 0.0))
                    P.dma("pool", tl["xpad"][:, 3:3 + S], pj[row:row + 128, 0:S], [], [tb["xpad"]])
                    P.dma("pool", tl["xpad"][:, S + 6:S + 6 + R], pj[row:row + 128, S:S + R], [], [tb["xpad"]])
                    P.dma("sp", tl["xpad"][:, S + 3:S + 6], I["state_lru_conv"][l, :, c * 128:(c + 1) * 128].rearrange("k c -> c k"),
                          [], [tb["xpad"]], **NC_)
                    P.dma("pool", tl["lg"][:, 0:T], pj[4096 + c * 128:4096 + (c + 1) * 128, :], [], [tb["lg"]])
                    for (off, n, nm) in segs:
                        xp_ = tl["xpad"]
                        P.op("dve", [tb["xpad"], b_pp], [tb["xc"]],
                             lambda e, off=off, n=n, c=c, xp_=xp_: e.tensor_scalar(tl["xc"][:, off:off + n], xp_[:, off + 3:off + 3 + n],
                                                                              pp[:, c, 3:4], pp[:, c, 4:5], ALU.mult, ALU.add))
                        for k in (2, 1, 0):
                            P.op("dve", [tb["xpad"], b_pp, tb["xc"]], [tb["xc"]],
                                 lambda e, off=off, n=n, c=c, k=k, xp_=xp_: e.scalar_tensor_tensor(
                                     tl["xc"][:, off:off + n], xp_[:, off + k:off + k + n], pp[:, c, k:k + 1],
                                     tl["xc"][:, off:off + n], ALU.mult, ALU.add))
                        P.dma("sp", O["lru_conv_" + nm][l, :, c * 128:(c + 1) * 128].rearrange("k c -> c k"),
                              xp_[:, off + n:off + n + 3], [tb["xpad"]], [], **NC_)
                        P.op("act", [tb["xc"]], [b_xcb],
                             lambda e, off=off, n=n: e.copy(out=xcb[:, off:off + n], in_=tl["xc"][:, off:off + n]))
                        for c0 in range(0, n, 512):
                            m = min(512, n - c0)
                            for (wt, bcol, dst) in ((wa_bf, 5, "r"), (wx_bf, 6, "i")):
                                pb = next_bank()
                                P.op("pe", [b_wg, b_xcb], [pbuf[pb]],
                                     lambda e, pb=pb, wt=wt, c=c, off=off, c0=c0, m=m: e.matmul(
                                         pbank[pb][:, 0:m], wt[:, c, :], xcb[:, off + c0:off + c0 + m], start=True, stop=True))
                                P.op("act", [pbuf[pb], b_pp], [tb[dst]],
                                     lambda e, pb=pb, dst=dst, bcol=bcol, c=c, off=off, c0=c0, m=m: e.activation(
                                         out=tl[dst][:, off + c0:off + c0 + m], in_=pbank[pb][:, 0:m], func=AF.Sigmoid,
                                         bias=pp[:, c, bcol:bcol + 1]))
                        sl = slice(off, off + n)
                        P.op("act", [tb["r"], b_pp], [tb["a"]],
                             lambda e, sl=sl, c=c: e.activation(out=tl["a"][:, sl], in_=tl["r"][:, sl], func=AF.Exp, scale=pp[:, c, 8:9]))
                        P.op("act", [tb["r"], b_pp], [tb["s2"]],
                             lambda e, sl=sl, c=c: e.activation(out=tl["s2"][:, sl], in_=tl["r"][:, sl], func=AF.Exp, scale=pp[:, c, 9:10]))
                        P.op("dve", [tb["s2"]], [tb["s2"]],
                             lambda e, sl=sl: e.tensor_scalar(tl["s2"][:, sl], tl["s2"][:, sl], -1.0, 1.0, ALU.mult, ALU.add))
                        P.op("act", [tb["s2"]], [tb["s2"]],
                             lambda e, sl=sl: e.activation(out=tl["s2"][:, sl], in_=tl["s2"][:, sl], func=AF.Sqrt))
                        P.op("dve", [tb["i"], tb["xc"]], [tb["u"]],
                             lambda e, sl=sl: e.tensor_tensor(tl["u"][:, sl], tl["i"][:, sl], tl["xc"][:, sl], ALU.mult))
                        P.op("dve", [tb["u"], tb["s2"]], [tb["u"]],
                             lambda e, sl=sl: e.tensor_tensor(tl["u"][:, sl], tl["u"][:, sl], tl["s2"][:, sl], ALU.mult))
                        init = 0.0 if nm == "prompt" else pp[:, c, 10:11]
                        P.op("dve", [tb["a"], tb["u"], b_pp], [tb["h"]],
                             lambda e, sl=sl, init=init: e.tensor_tensor_scan(tl["h"][:, sl], tl["a"][:, sl], tl["u"][:, sl],
                                                                              init, ALU.mult, ALU.add))
                        P.dma("sp", O["lru_h_" + nm][l, c * 128:(c + 1) * 128].rearrange("(p o) -> p o", o=1),
                              tl["h"][:, off + n - 1:off + n], [tb["h"]], [])
                    P.op("act", [tb["lg"]], [tb["t"]], lambda e: e.activation(out=tl["t"][:, 0:T], in_=tl["lg"][:, 0:T], func=AF.Square))
                    P.op("dve", [tb["t"]], [tb["t"]],
                         lambda e: e.tensor_scalar(tl["t"][:, 0:T], tl["t"][:, 0:T], 0.044715, 1.0, ALU.mult, ALU.add))
                    P.op("dve", [tb["t"], tb["lg"]], [tb["t"]],
                         lambda e: e.tensor_tensor(tl["t"][:, 0:T], tl["t"][:, 0:T], tl["lg"][:, 0:T], ALU.mult))
                    P.op("act", [tb["t"]], [tb["t"]],
                         lambda e: e.activation(out=tl["t"][:, 0:T], in_=tl["t"][:, 0:T], func=AF.Sigmoid, scale=1.5957691216))
                    P.op("dve", [tb["t"], tb["lg"]], [tb["t"]],
                         lambda e: e.tensor_tensor(tl["t"][:, 0:T], tl["t"][:, 0:T], tl["lg"][:, 0:T], ALU.mult))
                    P.op("dve", [tb["t"], tb["h"]], [b_ob],
                         lambda e: e.tensor_tensor(ob[:, 0:S], tl["t"][:, 0:S], tl["h"][:, 0:S], ALU.mult))
                    P.op("dve", [tb["t"], tb["h"]], [b_ob],
                         lambda e: e.tensor_tensor(ob[:, S:T], tl["t"][:, S:T], tl["h"][:, S + 3:S + 3 + R], ALU.mult))
                    P.dma("sp", brT[1, c * 128:(c + 1) * 128, :], ob, [b_ob], [])
            for c in range(8):
                l_chunk(c, *lsets[c % 2])
            P.barrier()

        if cfg.get("do_S", True):
            ar.reset()
            NC_ = {"allow_slow_non_contiguous": True}
            TP = T + 6
            b_sc = P.buf("Sconst")
            negm = ar.bf16(128)
            P.dma("pool", negm, I["negmask"], [], [b_sc])
            selh = ar.f32(16 * 128).rearrange("p (h m) -> p h m", h=16)
            P.dma("sp", selh[0:16], I["selh"].rearrange("k (h m) -> k h m", h=16), [], [b_sc])
            sel128 = ar.f32(128)
            selR = ar.f32(128)
            P.dma("sp", sel128, I["sel128"], [], [b_sc])
            P.dma("sp", selR, I["selR"], [], [b_sc])
            DB = ar.f32(16)
            P.dma("sp", DB, I["ssd_d"][l:l + 1, :].partition_broadcast(128), [], [b_sc])
            gnB = ar.f32(MIX)
            P.dma("sp", gnB, I["ssd_norm"][l:l + 1, :].partition_broadcast(128), [], [b_sc])
            cp = ar.f32(12 * 8).rearrange("p (c x) -> p c x", c=12)
            for k in range(4):
                P.dma("sp", cp[:, :, k], I["ssd_conv_w"][l, k, :].rearrange("(c p) -> p c", p=128), [], [b_sc], **NC_)
            P.dma("sp", cp[:, :, 4], I["ssd_conv_b"][l, :].rearrange("(c p) -> p c", p=128), [], [b_sc], **NC_)
            hp_ = ar.f32(4)
            P.dma("sp", hp_[0:16, 0:1], I["ssd_dt_bias"][l, :].rearrange("(p o) -> p o", o=1), [], [b_sc])
            P.dma("sp", hp_[0:16, 1:2], I["ssd_a_log"][l, :].rearrange("(p o) -> p o", o=1), [], [b_sc])
            P.op("act", [b_sc], [b_sc], lambda e: e.activation(out=hp_[0:16, 2:3], in_=hp_[0:16, 1:2], func=AF.Exp))
            P.op("dve", [b_sc], [b_sc], lambda e: e.tensor_scalar_mul(hp_[0:16, 2:3], hp_[0:16, 2:3], -1.0))
            xcT = ar.bf16(12 * T).rearrange("p (c t) -> p c t", c=12)
            b_xcT = P.bufs_n("xcT", 12)
            mark = ar.off
            segs = [(0, S, "prompt"), (S + 3, R, "sample")]
            cw = [dict(xpad=ar.f32(TP), xc=ar.f32(TP), b_xpad=P.buf("Sxpad"), b_xc=P.buf("Sxc")) for _ in range(2)]
            for c in range(12):
                w = cw[c % 2]
                row = 6144 + c * 128
                P.op("dve", [], [w["b_xpad"]], lambda e, w=w: e.memset(w["xpad"][:, 0:3], 0.0))
                P.dma("pool", w["xpad"][:, 3:3 + S], pj[row:row + 128, 0:S], [], [w["b_xpad"]])
                P.dma("pool", w["xpad"][:, S + 6:S + 6 + R], pj[row:row + 128, S:S + R], [], [w["b_xpad"]])
                P.dma("sp", w["xpad"][:, S + 3:S + 6], I["state_ssd_conv"][l, :, c * 128:(c + 1) * 128].rearrange("k c -> c k"),
                      [], [w["b_xpad"]], **NC_)
                for (off, n, nm) in segs:
                    P.op("dve", [w["b_xpad"], b_sc], [w["b_xc"]],
                         lambda e, off=off, n=n, c=c, w=w: e.tensor_scalar(w["xc"][:, off:off + n], w["xpad"][:, off + 3:off + 3 + n],
                                                                        cp[:, c, 3:4], cp[:, c, 4:5], ALU.mult, ALU.add))
                    for k in (2, 1, 0):
                        P.op("dve", [w["b_xpad"], b_sc, w["b_xc"]], [w["b_xc"]],
                             lambda e, off=off, n=n, c=c, k=k, w=w: e.scalar_tensor_tensor(
                                 w["xc"][:, off:off + n], w["xpad"][:, off + k:off + k + n], cp[:, c, k:k + 1],
                                 w["xc"][:, off:off + n], ALU.mult, ALU.add))
                    P.dma("sp", O["ssd_conv_" + nm][l, :, c * 128:(c + 1) * 128].rearrange("k c -> c k"),
                          w["xpad"][:, off + n:off + n + 3], [w["b_xpad"]], [], **NC_)
                    t0 = 0 if nm == "prompt" else S
                    P.op("act", [w["b_xc"]], [b_xcT[c]],
                         lambda e, off=off, n=n, c=c, w=w, t0=t0: e.activation(out=xcT[:, c, t0:t0 + n], in_=w["xc"][:, off:off + n],
                                                                            func=AF.Silu))
            P.barrier()
            ar.off = mark
            dts = ar.f32(T); dta = ar.f32(T); cumT = ar.f32(T); rmask = ar.f32(T)
            b_dt = P.buf("Sdt")
            P.dma("pool", dts[0:16, :], pj[IN_COLS:IN_COLS + 16, :], [], [b_dt])
            P.op("act", [b_dt, b_sc], [b_dt], lambda e: e.activation(out=dts[0:16, :], in_=dts[0:16, :], func=AF.Exp, bias=hp_[0:16, 0:1]))
            P.op("dve", [b_dt], [b_dt], lambda e: e.tensor_scalar_add(dts[0:16, :], dts[0:16, :], 1.0))
            P.op("act", [b_dt], [b_dt], lambda e: e.activation(out=dts[0:16, :], in_=dts[0:16, :], func=AF.Ln))
            P.op("dve", [b_dt, b_sc], [b_dt], lambda e: e.tensor_scalar(dta[0:16, :], dts[0:16, :], hp_[0:16, 2:3], None, ALU.mult))
            P.op("dve", [], [b_dt], lambda e: e.memset(rmask[0:16, :], 1.0))
            P.op("dve", [b_dt], [b_dt],
                 lambda e: e.memset(rmask[0:16, 0:S].rearrange("p (b q) -> p b q", q=128)[:, :, 0:1], 0.0))
            P.op("dve", [b_dt], [b_dt], lambda e: e.memset(rmask[0:16, S:S + 1], 0.0))
            P.op("dve", [b_dt], [b_dt],
                 lambda e: e.tensor_tensor_scan(cumT[0:16, :], rmask[0:16, :], dta[0:16, :], 0.0, ALU.mult, ALU.add))
            def A2(fn, nm):
                return {"ap": [fn(), fn()], "b": P.bufs_n(nm, 2)}
            xs_tok = A2(lambda: ar.bf16(1024), "xs_tok")
            B_tok = A2(lambda: ar.bf16(256), "B_tok")
            xw_tok = A2(lambda: ar.bf16(1024), "xw_tok")
            dc = A2(lambda: ar.f32(96), "dc")
            cdB = A2(lambda: ar.f32(16), "cdB")
            CBs = A2(lambda: ar.f32(256), "CBs")
            Et = A2(lambda: ar.f32(128), "Et")
            MT = A2(lambda: ar.bf16(128), "MT")
            yi = A2(lambda: ar.f32(1024), "yi")
            yt = A2(lambda: ar.f32(1024), "yt")
            zt = A2(lambda: ar.bf16(1024), "zt")
            sz = A2(lambda: ar.f32(1024), "sz")
            tmp = A2(lambda: ar.f32(1024), "tmp")
            ss = A2(lambda: ar.f32(8), "ss")
            oc = A2(lambda: ar.bf16(1024), "oc")
            ocT = A2(lambda: ar.bf16(1024).rearrange("p (c t) -> p c t", c=8), "ocT")
            hT = ar.f32(1024); hTb = ar.bf16(1024)
            b_hT = P.buf("hT"); b_hTb = P.buf("hTb")
            stg = ar.f32(1024).rearrange("p (c n) -> p c n", c=8)
            b_stg = P.buf("stg")

            def ssd_block(bi, t0, n, sel):
                j = bi % 2
                def nb2():
                    state['pbi'] += 1
                    return state['pbi'] % 2
                pb = nb2(); pv = pbank[pb].bitcast(BF16)
                P.op("pe", [b_xcT[c] for c in range(8)] + [b_const], [pbuf[pb]],
                     lambda e, pv=pv: [e.transpose(pv[0:n, c * 128:(c + 1) * 128], xcT[:, c, t0:t0 + n], ident_bf) for c in range(8)][-1])
                P.op("act", [pbuf[pb]], [xs_tok["b"][j]], lambda e, pv=pv: e.copy(out=xs_tok["ap"][j][0:n, :], in_=pv[0:n, :]))
                pb = nb2(); pv2 = pbank[pb].bitcast(BF16)
                P.op("pe", [b_xcT[8], b_xcT[9], b_const], [pbuf[pb]],
                     lambda e, pv2=pv2: [e.transpose(pv2[0:n, c * 128:(c + 1) * 128], xcT[:, 8 + c, t0:t0 + n], ident_bf) for c in range(2)][-1])
                P.op("dve", [pbuf[pb]], [B_tok["b"][j]], lambda e, pv2=pv2: e.tensor_copy(B_tok["ap"][j][0:n, :], pv2[0:n, 0:256]))
                pb = nb2()
                P.op("pe", [b_dt, b_const], [pbuf[pb]],
                     lambda e, pb=pb: [e.transpose(pbank[pb][0:n, 0:16], dts[0:16, t0:t0 + n], ident_f[0:16, 0:16]),
                                       e.transpose(pbank[pb][0:n, 16:32], cumT[0:16, t0:t0 + n], ident_f[0:16, 0:16])][-1])
                D_ = dc["ap"][j]; bD = dc["b"][j]
                P.op("act", [pbuf[pb]], [bD], lambda e, pb=pb: e.copy(out=D_[0:n, 0:32], in_=pbank[pb][0:n, 0:32]))
                P.op("dve", [bD], [bD], lambda e: e.tensor_scalar_mul(D_[0:n, 32:48], D_[0:n, 16:32], -1.0))
                pb = nb2()
                P.op("pe", [bD, b_sc], [pbuf[pb]],
                     lambda e, pb=pb: e.matmul(pbank[pb][:, 0:16], sel[0:n, :], D_[0:n, 16:32], start=True, stop=True))
                P.op("act", [pbuf[pb]], [bD], lambda e, pb=pb: e.copy(out=D_[:, 48:64], in_=pbank[pb][:, 0:16]))
                P.op("act", [bD], [cdB["b"][j]], lambda e: e.activation(out=cdB["ap"][j], in_=D_[:, 48:64], func=AF.Exp))
                P.op("dve", [bD], [bD], lambda e: e.tensor_tensor(D_[0:n, 64:80], D_[0:n, 48:64], D_[0:n, 16:32], ALU.subtract))
                P.op("act", [bD], [bD], lambda e: e.activation(out=D_[0:n, 64:80], in_=D_[0:n, 64:80], func=AF.Exp))
                P.op("dve", [bD], [bD], lambda e: e.tensor_tensor(D_[0:n, 64:80], D_[0:n, 64:80], D_[0:n, 0:16], ALU.mult))
                P.op("act", [bD], [bD], lambda e: e.activation(out=D_[0:n, 80:96], in_=D_[0:n, 16:32], func=AF.Exp))
                P.op("dve", [xs_tok["b"][j], bD], [xw_tok["b"][j]],
                     lambda e: e.tensor_tensor(xw_tok["ap"][j][0:n, :].rearrange("p (h q) -> p h q", h=16),
                                               xs_tok["ap"][j][0:n, :].rearrange("p (h q) -> p h q", h=16),
                                               D_[0:n, 64:80].unsqueeze(2).to_broadcast([n, 16, 64]), ALU.mult))
                for g in range(2):
                    pb = nb2()
                    P.op("pe", [b_xcT[8 + g], b_xcT[10 + g]], [pbuf[pb]],
                         lambda e, pb=pb, g=g: e.matmul(pbank[pb][0:n, 0:n], xcT[:, 8 + g, t0:t0 + n], xcT[:, 10 + g, t0:t0 + n],
                                                        start=True, stop=True))
                    P.op("act", [pbuf[pb]], [CBs["b"][j]],
                         lambda e, pb=pb, g=g: e.copy(out=CBs["ap"][j][0:n, g * 128:g * 128 + n], in_=pbank[pb][0:n, 0:n]))
                pbi = [2, 3]
                for g in range(2):
                    P.op("pe", [b_xcT[10 + g], b_hTb], [pbuf[pbi[g]]],
                         lambda e, g=g: e.matmul(pbank[pbi[g]][0:n, :], xcT[:, 10 + g, t0:t0 + n], hTb[:, g * 512:(g + 1) * 512],
                                                 start=True, stop=True))
                    P.op("dve", [pbuf[pbi[g]], bD], [yi["b"][j]],
                         lambda e, g=g: e.tensor_tensor(yi["ap"][j][0:n, g * 512:(g + 1) * 512].rearrange("p (h q) -> p h q", h=8),
                                                        pbank[pbi[g]][0:n, :].rearrange("p (h q) -> p h q", h=8),
                                                        D_[0:n, 80 + g * 8:88 + g * 8].unsqueeze(2).to_broadcast([n, 8, 64]), ALU.mult))
                pby = [4, 5]
                for h in range(16):
                    g = h // 8
                    jj = h % 2
                    pb = nb2()
                    P.op("pe", [b_dt, b_sc, b_const], [pbuf[pb]],
                         lambda e, pb=pb, h=h: [e.matmul(pbank[pb][0:n, 0:n], selh[0:16, h, 0:n], cumT[0:16, t0:t0 + n], start=True, stop=False),
                                                e.matmul(pbank[pb][0:n, 0:n], ident_bf[0:n, 0:n], negm[0:n, 0:n], start=False, stop=True)][-1])
                    P.op("act", [pbuf[pb], bD], [Et["b"][jj]],
                         lambda e, pb=pb, h=h, jj=jj: e.activation(out=Et["ap"][jj][0:n, 0:n], in_=pbank[pb][0:n, 0:n], func=AF.Exp,
                                                                   bias=D_[0:n, 32 + h:33 + h]))
                    P.op("dve", [Et["b"][jj], bD, CBs["b"][j]], [MT["b"][jj]],
                         lambda e, h=h, jj=jj, g=g: e.scalar_tensor_tensor(MT["ap"][jj][0:n, 0:n], Et["ap"][jj][0:n, 0:n], D_[0:n, h:h + 1],
                                                                          CBs["ap"][j][0:n, g * 128:g * 128 + n], ALU.mult, ALU.mult))
                    P.op("pe", [MT["b"][jj], xs_tok["b"][j]], [pbuf[pby[g]]],
                         lambda e, h=h, jj=jj, g=g: e.matmul(pbank[pby[g]][0:n, (h % 8) * 64:(h % 8 + 1) * 64], MT["ap"][jj][0:n, 0:n],
                                                             xs_tok["ap"][j][0:n, h * 64:(h + 1) * 64], start=True, stop=True))
                pbs = [6, 7]
                for g in range(2):
                    P.op("pe", [B_tok["b"][j], xw_tok["b"][j]], [pbuf[pbs[g]]],
                         lambda e, g=g: e.matmul(pbank[pbs[g]][:, :], B_tok["ap"][j][0:n, g * 128:(g + 1) * 128],
                                                 xw_tok["ap"][j][0:n, g * 512:(g + 1) * 512], start=True, stop=True))
                P.op("dve", [b_hT, cdB["b"][j]], [b_hT],
                     lambda e: e.tensor_tensor(hT.rearrange("p (h q) -> p h q", h=16), hT.rearrange("p (h q) -> p h q", h=16),
                                               cdB["ap"][j].unsqueeze(2).to_broadcast([128, 16, 64]), ALU.mult))
                for g in range(2):
                    P.op("dve", [b_hT, pbuf[pbs[g]]], [b_hT],
                         lambda e, g=g: e.tensor_tensor(hT[:, g * 512:(g + 1) * 512], hT[:, g * 512:(g + 1) * 512], pbank[pbs[g]][:, :], ALU.add))
                P.op("act", [b_hT], [b_hTb], lambda e: e.copy(out=hTb, in_=hT))
                Y = yt["ap"][j]; bY = yt["b"][j]
                for g in range(2):
                    P.op("dve", [pbuf[pby[g]], yi["b"][j]], [bY],
                         lambda e, g=g: e.tensor_tensor(Y[0:n, g * 512:(g + 1) * 512], pbank[pby[g]][0:n, :],
                                                        yi["ap"][j][0:n, g * 512:(g + 1) * 512], ALU.add))
                P.op("dve", [xs_tok["b"][j], b_sc], [tmp["b"][j]],
                     lambda e: e.tensor_tensor(tmp["ap"][j][0:n, :].rearrange("p (h q) -> p h q", h=16),
                                               xs_tok["ap"][j][0:n, :].rearrange("p (h q) -> p h q", h=16),
                                               DB[0:n, :].unsqueeze(2).to_broadcast([n, 16, 64]), ALU.mult))
                P.op("dve", [bY, tmp["b"][j]], [bY], lambda e: e.tensor_tensor(Y[0:n, :], Y[0:n, :], tmp["ap"][j][0:n, :], ALU.add))
                P.dma("sp", zt["ap"][j][0:n, :], ztok[t0:t0 + n, :], [], [zt["b"][j]])
                P.op("act", [zt["b"][j]], [sz["b"][j]], lambda e: e.activation(out=sz["ap"][j][0:n, :], in_=zt["ap"][j][0:n, :], func=AF.Silu))
                P.op("dve", [bY, sz["b"][j]], [bY], lambda e: e.tensor_tensor(Y[0:n, :], Y[0:n, :], sz["ap"][j][0:n, :], ALU.mult))
                SS = ss["ap"][j]; bS = ss["b"][j]
                for g in range(2):
                    P.op("act", [bY], [tmp["b"][j], bS],
                         lambda e, g=g: e.activation(out=tmp["ap"][j][0:n, g * 512:(g + 1) * 512], in_=Y[0:n, g * 512:(g + 1) * 512],
                                                     func=AF.Square, accum_out=SS[0:n, g:g + 1]))
                P.op("dve", [bS], [bS], lambda e: e.tensor_scalar(SS[0:n, 2:4], SS[0:n, 0:2], 1.0 / 512, EPS, ALU.mult, ALU.add))
                P.op("act", [bS], [bS], lambda e: e.activation(out=SS[0:n, 4:6], in_=SS[0:n, 2:4], func=AF.Sqrt))
                P.op("dve", [bS], [bS], lambda e: e.reciprocal(SS[0:n, 6:8], SS[0:n, 4:6]))
                for g in range(2):
                    P.op("dve", [bS, bY, b_sc], [oc["b"][j]],
                         lambda e, g=g: e.scalar_tensor_tensor(oc["ap"][j][0:n, g * 512:(g + 1) * 512], Y[0:n, g * 512:(g + 1) * 512],
                                                              SS[0:n, 6 + g:7 + g], gnB[0:n, g * 512:(g + 1) * 512], ALU.mult, ALU.mult))
                pb = nb2(); pv3 = pbank[pb].bitcast(BF16)
                P.op("pe", [oc["b"][j], b_const], [pbuf[pb]],
                     lambda e, pv3=pv3: [e.transpose(pv3[:, c * 128:c * 128 + n], oc["ap"][j][0:n, c * 128:(c + 1) * 128], ident_bf[0:n, 0:n])
                                         for c in range(8)][-1])
                P.op("act", [pbuf[pb]], [ocT["b"][j]],
                     lambda e, pv3=pv3: e.copy(out=ocT["ap"][j][:, :, 0:n], in_=pv3.rearrange("p (c t) -> p c t", c=8)[:, :, 0:n]))
                P.dma("sp", brT[2, :, t0:t0 + n].rearrange("(c p) t -> p c t", p=128), ocT["ap"][j][:, :, 0:n], [ocT["b"][j]], [])

            def state_out(dst):
                for half in range(2):
                    pb = next_bank()
                    P.op("pe", [b_hT, b_const], [pbuf[pb]],
                         lambda e, pb=pb, half=half: [e.transpose(pbank[pb][:, c * 128:(c + 1) * 128], hT[:, (half * 4 + c) * 128:(half * 4 + c + 1) * 128], ident_f)
                                                      for c in range(4)][-1])
                    P.op("act", [pbuf[pb]], [b_stg],
                         lambda e, pb=pb, half=half: e.copy(out=stg[:, half * 4:(half + 1) * 4, :], in_=pbank[pb].rearrange("p (c n) -> p c n", c=4)))
                P.dma("sp", dst.rearrange("h p n -> (h p) n").rearrange("(j q) n -> q j n", q=128), stg, [b_stg], [])

            P.op("dve", [], [b_hT], lambda e: e.memset(hT, 0.0))
            P.op("act", [b_hT], [b_hTb], lambda e: e.copy(out=hTb, in_=hT))
            for bi in range(NB):
                ssd_block(bi, bi * 128, 128, sel128)
            state_out(O["ssd_state_prompt"][l])
            P.dma("sp", stg, I["state_ssd"][l].rearrange("h p n -> (h p) n").rearrange("(j q) n -> q j n", q=128), [], [b_stg])
            for half in range(2):
                pb = next_bank()
                P.op("pe", [b_stg, b_const], [pbuf[pb]],
                     lambda e, pb=pb, half=half: [e.transpose(pbank[pb][:, c * 128:(c + 1) * 128], stg[:, half * 4 + c, :], ident_f)
                                                  for c in range(4)][-1])
                P.op("act", [pbuf[pb]], [b_hT], lambda e, pb=pb, half=half: e.copy(out=hT[:, half * 512:(half + 1) * 512], in_=pbank[pb]))
            P.op("act", [b_hT], [b_hTb], lambda e: e.copy(out=hTb, in_=hT))
            ssd_block(NB, S, R, selR)
            state_out(O["ssd_state_sample"][l])
            P.barrier()

        NC_ = {"allow_slow_non_contiguous": True}
        GT = cfg.get("group_tokens", 1024)
        groups = []
        for t0 in range(0, S, GT):
            groups.append((t0, min(GT, S - t0)))
        groups[-1] = (groups[-1][0], groups[-1][1] + R)

        def A1(fn, nm):
            a = fn()
            b = P.buf(nm)
            return {"ap": [a, a], "b": [b, b]}

        def out_proj_residual(wsrc, nkc, xT, x_bufs, g0, gn, wbf, rs, ot, ncolblk=512, wcnt=[0]):
            blocks = split_range(g0, gn, 128)
            for cb in range(D // ncolblk):
                wap, wbuf = load_w2(wsrc, cb * ncolblk, ncolblk, nkc, wbf)

                def evac(ps_ap, ps_buf, t0, n, cb=cb):
                    jj = wcnt[0] % 2
                    wcnt[0] += 1
                    P.dma("sp", rs["ap"][jj][0:n, 0:ncolblk], res_src(l, t0, n, cb * ncolblk, ncolblk), [], [rs["b"][jj]])
                    P.op("dve", [ps_buf, rs["b"][jj]], [ot["b"][jj]],
                         lambda e: e.tensor_tensor(ot["ap"][jj][0:n, 0:ncolblk], ps_ap[0:n, 0:ncolblk], rs["ap"][jj][0:n, 0:ncolblk], ALU.add))
                    P.dma("sp", res[t0:t0 + n, cb * ncolblk:(cb + 1) * ncolblk], ot["ap"][jj][0:n, 0:ncolblk], [ot["b"][jj]], [])
                gemm_tm2(wap, wbuf, ncolblk, xT, x_bufs, nkc, g0, blocks, evac)

        def A2(fn, nm):
            return {"ap": [fn(), fn()], "b": P.bufs_n(nm, 2)}

        if cfg.get("do_M", True):
            def m_group(g0, gn):
                ar.reset()
                chunks = split_range(g0, gn, 512)
                oT = [ar.bf16(8 * gn).rearrange("p (c t) -> p c t", c=8) for _ in range(3)]
                b_oT = P.bufs_n("MoT", 3)
                for n_ in range(3):
                    P.dma("sp", oT[n_], brT[n_, :, g0:g0 + gn].rearrange("(c p) t -> p c t", p=128), [], [b_oT[n_]])
                mgT = ar.bf16(KC * gn).rearrange("p (c t) -> p c t", c=KC)
                b_mg = P.buf("mgT")
                acc = ar.f32(4 * gn).rearrange("p (s t) -> p s t", s=4)
                b_acc = P.buf("Macc")
                tmpm = A2(lambda: ar.f32(512), "Mtmp")
                sg = A2(lambda: ar.bf16(4 * gn).rearrange("p (s t) -> p s t", s=4), "Msg")
                wbf = AN(lambda: ar.bf16(KC * 512).rearrange("p (k n) -> p k n", k=KC), "Mwbf", 3)
                rs = A2(lambda: ar.f32(512), "Mrs")
                ot = A2(lambda: ar.f32(512), "Mot")
                cnt = [0]
                for cb in range(4):
                    for n_ in range(3):
                        j = cnt[0] % 2
                        cnt[0] += 1
                        wap, wbuf = load_w2(I["w_branch"][l, n_], cb * 512, 512, 8, wbf)
                        r0 = 7696 + n_ * 2048 + cb * 512
                        P.dma("sp", sg["ap"][j], pj[r0:r0 + 512, g0:g0 + gn].rearrange("(s p) t -> p s t", p=128), [], [sg["b"][j]])

                        def evac(ps_ap, ps_buf, sub, t0, n, m, n_=n_, j=j):
                            lo = t0 - g0
                            if n_ == 0:
                                P.op("dve", [ps_buf, sg["b"][j]], [b_acc],
                                     lambda e: e.tensor_tensor(acc[:, sub, lo:lo + n], ps_ap[:, 0:n], sg["ap"][j][:, sub, lo:lo + n], ALU.mult))
                            else:
                                jj = cnt[0] % 2
                                cnt[0] += 1
                                P.op("dve", [ps_buf, sg["b"][j]], [tmpm["b"][jj]],
                                     lambda e: e.tensor_tensor(tmpm["ap"][jj][:, 0:n], ps_ap[:, 0:n], sg["ap"][j][:, sub, lo:lo + n], ALU.mult))
                                P.op("dve", [tmpm["b"][jj], b_acc], [b_acc],
                                     lambda e: e.tensor_tensor(acc[:, sub, lo:lo + n], acc[:, sub, lo:lo + n], tmpm["ap"][jj][:, 0:n], ALU.add))
                        gemm_fm2(wap, wbuf, 512, oT[n_], [b_oT[n_]], 8, g0, chunks, evac)
                    P.op("act", [b_acc], [b_mg], lambda e, cb=cb: e.copy(out=mgT[:, cb * 4:(cb + 1) * 4, :], in_=acc))
                out_proj_residual(I["w_out"][l], KC, mgT, [b_mg], g0, gn, wbf, rs, ot)
                P.barrier()
            for (g0_, gn_) in groups:
                m_group(g0_, gn_)
            state["res_live"] = True

        if cfg.get("do_X", True):
            ar.reset()
            mkT = [ar.bf16(KC * NMEM).rearrange("p (c m) -> p c m", c=KC) for _ in range(2)]
            mvt = [ar.bf16(2 * D).rearrange("p (b d) -> p b d", b=2) for _ in range(2)]
            b_mk = P.bufs_n("mkT", 2)
            b_mv = P.bufs_n("mvt", 2)
            ones_bf = ar.bf16(128)
            b_on = P.buf("onesbf")
            P.op("dve", [], [b_on], lambda e: e.memset(ones_bf, 1.0))
            markX = ar.off
            mnT = ar.bf16(KC * NMEM).rearrange("p (k t) -> p k t", k=KC)
            b_mn = P.bufs_n("mnT", 2)
            gB = ar.f32(D); gB_buf = P.buf("gBx")
            work = (A2(lambda: ar.f32(D), "xt"), A2(lambda: ar.bf16(D), "xb"), A2(lambda: ar.f32(D), "sq"), A2(lambda: ar.f32(4), "st"))
            phase_norm([(0, 128, I["mem_prompt"][0:128, :]), (128, 128, I["mem_prompt"][128:256, :])],
                       I["norm_mem"][l:l + 1, :], mnT, b_mn, gB, gB_buf, work)
            wbf = AN(lambda: ar.bf16(KC * 512).rearrange("p (k n) -> p k n", k=KC), "Xwbf", 3)
            otmx = A2(lambda: ar.f32(512), "Xotmx")
            cstage = ar.bf16(2 * D).rearrange("p (b d) -> p b d", b=2)
            b_cst = P.buf("cst")
            cnt = [0]
            mblocks = [(0, 128), (128, 128)]
            for cb in range(4):
                wap, wbuf = load_w2(I["w_xk"][l], cb * 512, 512, KC, wbf)

                def evk(ps_ap, ps_buf, sub, t0, n, m, cb=cb):
                    P.op("act", [ps_buf], [b_mk[0]], lambda e: e.copy(out=mkT[0][:, cb * 4 + sub, 0:NMEM], in_=ps_ap[:, 0:NMEM]))
                gemm_fm2(wap, wbuf, 512, mnT, b_mn, KC, 0, [(0, NMEM)], evk)

                def evk2(ps_ap, ps_buf, t0, n, cb=cb):
                    jj = cnt[0] % 2; cnt[0] += 1
                    P.op("act", [ps_buf], [otmx["b"][jj]], lambda e: e.copy(out=otmx["ap"][jj], in_=ps_ap))
                    P.dma("sp", O["mem_k_prompt"][l, t0:t0 + n, cb * 512:(cb + 1) * 512], otmx["ap"][jj], [otmx["b"][jj]], [])
                gemm_tm2(wap, wbuf, 512, mnT, b_mn, KC, 0, mblocks, evk2)
            for cb in range(4):
                wap, wbuf = load_w2(I["w_xv"][l], cb * 512, 512, KC, wbf)

                def evv(ps_ap, ps_buf, t0, n, cb=cb):
                    jj = cnt[0] % 2; cnt[0] += 1
                    P.op("act", [ps_buf], [otmx["b"][jj]], lambda e: e.copy(out=otmx["ap"][jj], in_=ps_ap))
                    P.dma("sp", O["mem_v_prompt"][l, t0:t0 + n, cb * 512:(cb + 1) * 512], otmx["ap"][jj], [otmx["b"][jj]], [])
                    P.op("dve", [otmx["b"][jj]], [b_mv[0]],
                         lambda e: e.tensor_copy(mvt[0][:, t0 // 128, cb * 512:(cb + 1) * 512], otmx["ap"][jj]))
                gemm_tm2(wap, wbuf, 512, mnT, b_mn, KC, 0, mblocks, evv)
            P.dma("pool", cstage, I["cache_mem_k"][l].rearrange("(b p) d -> p b d", p=128), [], [b_cst])
            P.dma("pool", mvt[1], I["cache_mem_v"][l].rearrange("(b p) d -> p b d", p=128), [], [b_mv[1]])
            for mb in range(2):
                for half in range(2):
                    pb = next_bank(); pv = pbank[pb].bitcast(BF16)
                    P.op("pe", [b_cst, b_const], [pbuf[pb]],
                         lambda e, pv=pv, mb=mb, half=half: [e.transpose(pv[:, c * 128:(c + 1) * 128], cstage[:, mb, (half * 8 + c) * 128:(half * 8 + c + 1) * 128], ident_bf)
                                                             for c in range(8)][-1])
                    P.op("dve", [pbuf[pb]], [b_mk[1]],
                         lambda e, pv=pv, mb=mb, half=half: e.tensor_copy(mkT[1][:, half * 8:(half + 1) * 8, mb * 128:(mb + 1) * 128],
                                                                          pv.rearrange("p (c m) -> p c m", c=8)))
            P.barrier()
            xscale = 512 ** -0.5
            def x_group(g0, gn):
                ar.off = markX
                blocks = split_range(g0, gn, 128)
                chunks = split_range(g0, gn, 512)
                xn2T = ar.bf16(KC * gn).rearrange("p (k t) -> p k t", k=KC)
                b_xn2 = P.bufs_n("xn2T", len(blocks))
                oTx = ar.bf16(KC * gn).rearrange("p (k t) -> p k t", k=KC)
                b_oTx = P.buf("oTx")
                markG = ar.off
                gB = ar.f32(D); gB_buf = P.buf("gBx2")
                work = (A2(lambda: ar.f32(D), "xt"), A2(lambda: ar.bf16(D), "xb"), A2(lambda: ar.f32(D), "sq"), A2(lambda: ar.f32(4), "st"))
                phase_norm([(t0 - g0, n, res_src(l, t0, n)) for (t0, n) in blocks], I["norm_xattn"][l:l + 1, :], xn2T, b_xn2, gB, gB_buf, work)
                P.barrier()
                ar.off = markG
                wbf = AN(lambda: ar.bf16(KC * 512).rearrange("p (k n) -> p k n", k=KC), "Xwbf", 3)
                qTh = A2(lambda: ar.bf16(4 * gn).rearrange("p (s t) -> p s t", s=4), "qTh")
                eT = A2(lambda: ar.bf16(2 * 512).rearrange("p (b t) -> p b t", b=2), "eT")
                rsm = A2(lambda: ar.f32(512), "rsm")
                rs = A2(lambda: ar.f32(512), "Xrs")
                ot = A2(lambda: ar.f32(512), "Xot")
                cnt = [0]
                for h in range(4):
                    j = h % 2
                    wap, wbuf = load_w2(I["w_xq"][l], h * 512, 512, KC, wbf)

                    def evq(ps_ap, ps_buf, sub, t0, n, m, j=j):
                        lo = t0 - g0
                        eng = "act" if sub % 2 == 0 else "dve"
                        if eng == "act":
                            P.op("act", [ps_buf], [qTh["b"][j]], lambda e: e.copy(out=qTh["ap"][j][:, sub, lo:lo + n], in_=ps_ap[:, 0:n]))
                        else:
                            P.op("dve", [ps_buf], [qTh["b"][j]], lambda e: e.tensor_copy(qTh["ap"][j][:, sub, lo:lo + n], ps_ap[:, 0:n]))
                    gemm_fm2(wap, wbuf, 512, xn2T, b_xn2, KC, g0, chunks, evq)
                    for (t0, n) in chunks:
                        lo = t0 - g0
                        mi = 0 if t0 < S else 1
                        jj = cnt[0] % 2; cnt[0] += 1
                        for mb in range(2):
                            pb = next_bank()
                            P.op("pe", [b_mk[mi], qTh["b"][j]], [pbuf[pb]],
                                 lambda e, pb=pb, mb=mb, mi=mi, lo=lo, n=n, h=h, j=j: [e.matmul(pbank[pb][:, 0:n], mkT[mi][:, h * 4 + dc, mb * 128:(mb + 1) * 128],
                                                                                      qTh["ap"][j][:, dc, lo:lo + n], start=(dc == 0), stop=(dc == 3))
                                                                             for dc in range(4)][-1])
                            P.op("act", [pbuf[pb]], [eT["b"][jj]],
                                 lambda e, pb=pb, mb=mb, n=n, jj=jj: e.activation(out=eT["ap"][jj][:, mb, 0:n], in_=pbank[pb][:, 0:n], func=AF.Exp, scale=xscale))
                        pb = next_bank()
                        P.op("pe", [b_on, eT["b"][jj]], [pbuf[pb]],
                             lambda e, pb=pb, n=n, jj=jj: [e.matmul(pbank[pb][:, 0:n], ones_bf, eT["ap"][jj][:, mb, 0:n], start=(mb == 0), stop=(mb == 1))
                                                           for mb in range(2)][-1])
                        P.op("dve", [pbuf[pb]], [rsm["b"][jj]], lambda e, pb=pb, n=n, jj=jj: e.reciprocal(rsm["ap"][jj][:, 0:n], pbank[pb][:, 0:n]))
                        for dc in range(4):
                            pb = next_bank()
                            P.op("pe", [b_mv[mi], eT["b"][jj]], [pbuf[pb]],
                                 lambda e, pb=pb, n=n, jj=jj, dc=dc, mi=mi, h=h: [e.matmul(pbank[pb][:, 0:n], mvt[mi][:, mb, h * 512 + dc * 128:h * 512 + (dc + 1) * 128],
                                                                                         eT["ap"][jj][:, mb, 0:n], start=(mb == 0), stop=(mb == 1))
                                                                                for mb in range(2)][-1])
                            P.op("dve", [pbuf[pb], rsm["b"][jj]], [b_oTx],
                                 lambda e, pb=pb, n=n, jj=jj, dc=dc, lo=lo, h=h: e.tensor_tensor(oTx[:, h * 4 + dc, lo:lo + n], pbank[pb][:, 0:n],
                                                                                                rsm["ap"][jj][:, 0:n], ALU.mult))
                out_proj_residual(I["w_xo"][l], KC, oTx, [b_oTx], g0, gn, wbf, rs, ot)
                P.barrier()
            for (g0_, gn_) in groups:
                x_group(g0_, gn_)

        if cfg.get("do_F", True):
            ar.reset()
            NP = DFF // 256
            xn3T = ar.bf16(KC * T).rearrange("p (k t) -> p k t", k=KC)
            b_xn3 = P.bufs_n("xn3T", len(tblocks))
            markF = ar.off
            gB = ar.f32(D); gB_buf = P.buf("gBf")
            work = (A2(lambda: ar.f32(D), "xt"), A2(lambda: ar.bf16(D), "xb"), A2(lambda: ar.f32(D), "sq"), A2(lambda: ar.f32(4), "st"))
            phase_norm([(t0, n, res_src(l, t0, n)) for (t0, n) in tblocks], I["norm_ffn"][l:l + 1, :], xn3T, b_xn3, gB, gB_buf, work)
            P.barrier()
            ar.off = markF
            fp = ar.f32(88 * 6).rearrange("p (c x) -> p c x", c=88)
            b_fp = P.buf("fp")
            for k in range(3):
                P.dma("sp", fp[:, :, k], I["ffn_conv_w"][l, k, :].rearrange("(c p) -> p c", p=128), [], [b_fp], **NC_)
            P.dma("sp", fp[:, :, 3], I["ffn_conv_b"][l, :].rearrange("(c p) -> p c", p=128), [], [b_fp], **NC_)
            for k in range(2):
                P.dma("sp", fp[:, :, 4 + k], I["state_ffn_conv"][l, k, :].rearrange("(c p) -> p c", p=128), [], [b_fp], **NC_)
            TP2 = T + 4
            wbf = AN(lambda: ar.bf16(KC * 256).rearrange("p (k n) -> p k n", k=KC), "Fwbf", 4)
            ugs = [[ar.bf16(2 * TP2).rearrange("p (s t) -> p s t", s=2) for _ in range(2)] for _ in range(2)]
            b_ugs = [P.bufs_n("ug", 2) for _ in range(2)]
            cg = A2(lambda: ar.f32(T), "cg")
            cv = A2(lambda: ar.f32(T), "cv")
            actT = A1(lambda: ar.bf16(2 * T).rearrange("p (s t) -> p s t", s=2), "actT")
            segs = [(0, S, 0, "prompt"), (S + 2, R, S, "sample")]
            cnt = [0]
            wh = {}

            def issue_w(pr_, half_):
                wh[(pr_, half_)] = load_w2(I["w_up"][l], half_ * DFF + pr_ * 256, 256, KC, wbf)
            issue_w(0, 0)
            issue_w(0, 1)
            for pr in range(NP):
                ug = ugs[pr % 2]
                b_ug = b_ugs[pr % 2]
                if pr + 1 < NP:
                    issue_w(pr + 1, 0)
                    issue_w(pr + 1, 1)
                for half in range(2):
                    j = cnt[0] % 2; cnt[0] += 1
                    c0 = half * DFF + pr * 256
                    wap, wbuf = wh[(pr, half)]
                    U = ug[half]
                    P.op("dve", [], [b_ug[half]], lambda e, U=U: e.memset(U[:, :, 0:2], 0.0))
                    for sub in range(2):
                        ch = c0 // 128 + sub
                        P.op("act", [b_fp], [b_ug[half]], lambda e, U=U, sub=sub, ch=ch: e.copy(out=U[:, sub, S + 2:S + 4], in_=fp[:, ch, 4:6]))

                    def evu(ps_ap, ps_buf, sub, t0, n, m, U=U, half=half):
                        off = t0 + 2 if t0 < S else t0 + 4
                        if sub % 2 == 0:
                            P.op("act", [ps_buf], [b_ug[half]], lambda e: e.copy(out=U[:, sub, off:off + n], in_=ps_ap[:, 0:n]))
                        else:
                            P.op("dve", [ps_buf], [b_ug[half]], lambda e: e.tensor_copy(U[:, sub, off:off + n], ps_ap[:, 0:n]))
                    gemm_fm2(wap, wbuf, 256, xn3T, b_xn3, KC, 0, tchunks, evu)
                    for sub in range(2):
                        ch = c0 // 128 + sub
                        for (off, n, t0, nm) in segs:
                            P.dma("pool", O["ffn_conv_" + nm][l, :, ch * 128:(ch + 1) * 128].rearrange("k c -> c k"),
                                  U[:, sub, off + n:off + n + 2], [b_ug[half]], [], **NC_)
                ja = pr % 2
                for sub in range(2):
                    jj = cnt[0] % 2; cnt[0] += 1
                    for half, dstt in ((0, cg), (1, cv)):
                        U = ug[half]
                        ch = (half * DFF + pr * 256) // 128 + sub
                        for (off, n, t0, nm) in segs:
                            P.op("act", [b_ug[half], b_fp], [dstt["b"][jj]],
                                 lambda e, U=U, ch=ch, off=off, n=n, t0=t0, dstt=dstt, jj=jj, sub=sub: e.activation(
                                     out=dstt["ap"][jj][:, t0:t0 + n], in_=U[:, sub, off + 2:off + 2 + n], func=AF.Identity,
                                     scale=fp[:, ch, 2:3], bias=fp[:, ch, 3:4]))
                            for k in (1, 0):
                                P.op("dve", [b_ug[half], b_fp, dstt["b"][jj]], [dstt["b"][jj]],
                                     lambda e, U=U, ch=ch, off=off, n=n, t0=t0, dstt=dstt, jj=jj, sub=sub, k=k: e.scalar_tensor_tensor(
                                         dstt["ap"][jj][:, t0:t0 + n], U[:, sub, off + k:off + k + n], fp[:, ch, k:k + 1],
                                         dstt["ap"][jj][:, t0:t0 + n], ALU.mult, ALU.add))
                    P.op("act", [cg["b"][jj]], [cg["b"][jj]], lambda e, jj=jj: e.activation(out=cg["ap"][jj], in_=cg["ap"][jj], func=AF.Silu))
                    P.op("pool", [cg["b"][jj], cv["b"][jj]], [actT["b"][ja]],
                         lambda e, jj=jj, sub=sub, ja=ja: e.tensor_tensor(actT["ap"][ja][:, sub, :], cg["ap"][jj], cv["ap"][jj], ALU.mult))
                P.dma("sp", actd[pr * 256:(pr + 1) * 256, :].rearrange("(s p) t -> p s t", p=128), actT["ap"][ja], [actT["b"][ja]], [])
            P.barrier()
            fgroups = list(groups)
            KD = DFF // 128
            def f_group(g0, gn):
                ar.reset()
                aT = ar.bf16(KD * gn).rearrange("p (k t) -> p k t", k=KD)
                b_aT = P.buf("aT")
                P.dma("sp", aT, actd[:, g0:g0 + gn].rearrange("(k p) t -> p k t", p=128), [], [b_aT])
                wbf = AN(lambda: ar.bf16(KD * 256).rearrange("p (k n) -> p k n", k=KD), "Dwbf", 3)
                rs = A2(lambda: ar.f32(512), "Drs")
                ot = A2(lambda: ar.f32(512), "Dot")
                out_proj_residual(I["w_down"][l], KD, aT, [b_aT], g0, gn, wbf, rs, ot, ncolblk=256)
                P.barrier()
            for (g0_, gn_) in fgroups:
                f_group(g0_, gn_)

    if cfg.get("do_final", True):
        ar.reset()
        gB = ar.f32(D); gB_buf = P.buf("gBfin")
        work = (A2(lambda: ar.f32(D), "xt"), A2(lambda: ar.bf16(D), "xb"), A2(lambda: ar.f32(D), "sq"), A2(lambda: ar.f32(4), "st"))
        blks = []
        for (t0, n) in tblocks:
            dst = O["y_prompt"][t0:t0 + n, :] if t0 < S else O["y_sample"][0:n, :]
            blks.append((t0, n, res[t0:t0 + n, :], dst))
        phase_norm(blks, I["norm_final"][0:1, :], None, None, gB, gB_buf, work)

    P.emit()
    es.close()
    return nc


def _consts(R):
    c = {}
    c["ident"] = np.eye(128, dtype=np.float32)
    c["maskL"] = np.tril(np.ones((128, 128), np.float32), -1)
    c["negmask"] = np.where(np.arange(128)[:, None] > np.arange(128)[None, :], -30000.0, 0.0).astype(np.float32)
    sh = np.zeros((16, 16, 128), np.float32)
    for h in range(16):
        sh[h, h, :] = 1.0
    c["selh"] = sh.reshape(16, 16 * 128)
    s1 = np.zeros((128, 128), np.float32)
    s1[127, :] = 1.0
    c["sel128"] = s1
    s2 = np.zeros((128, 128), np.float32)
    s2[R - 1, :] = 1.0
    c["selR"] = s2
    return c


_WEIGHTS = ["norm_mix", "w_in", "lru_conv_w", "lru_conv_b", "lru_wa", "lru_ba", "lru_wx", "lru_bx", "lru_lam",
            "ssd_conv_w", "ssd_conv_b", "ssd_dt_bias", "ssd_a_log", "ssd_d", "ssd_norm", "w_branch", "w_out",
            "norm_xattn", "norm_mem", "w_xq", "w_xk", "w_xv", "w_xo", "norm_ffn", "w_up", "ffn_conv_w",
            "ffn_conv_b", "w_down"]


def make_in_map(inp, pb, sb, R):
    f = lambda a: np.ascontiguousarray(np.asarray(a, dtype=np.float32))
    m = dict(_consts(R))
    for k in _WEIGHTS:
        m[k] = f(inp[k])
    m["norm_final"] = f(inp["norm_final"]).reshape(1, D)
    m["x_prompt"] = f(inp["x_prompt"][pb])
    m["x_sample"] = f(inp["x_sample"][sb])
    m["mem_prompt"] = f(inp["mem_prompt"][pb])
    m["cache_sb_k"] = f(inp["cache_sb_k"][:, sb])
    m["cache_sb_v"] = f(inp["cache_sb_v"][:, sb])
    m["cache_mem_k"] = f(inp["cache_mem_k"][:, sb]).reshape(DEPTH, NMEM, D)
    m["cache_mem_v"] = f(inp["cache_mem_v"][:, sb]).reshape(DEPTH, NMEM, D)
    m["state_lru_conv"] = f(inp["state_lru_conv"][:, sb])
    m["state_lru_h"] = f(inp["state_lru_h"][:, sb])
    m["state_ssd_conv"] = f(inp["state_ssd_conv"][:, sb])
    m["state_ssd"] = f(inp["state_ssd"][:, sb])
    m["state_ffn_conv"] = f(inp["state_ffn_conv"][:, sb])
    return m


_PROMPT_OUT = ["sb_k_prompt", "sb_v_prompt", "lru_conv_prompt", "lru_h_prompt", "ssd_conv_prompt", "ssd_state_prompt",
               "ffn_conv_prompt", "mem_k_prompt", "mem_v_prompt"]
_SAMPLE_OUT = ["sb_k_sample", "sb_v_sample", "lru_conv_sample", "lru_h_sample", "ssd_conv_sample", "ssd_state_sample",
               "ffn_conv_sample"]


def assemble(results, pcores, scores, S, R):
    def stk(name, cores, reshape=None):
        a = np.stack([np.asarray(results[c][name], dtype=np.float32) for c in cores], axis=0)
        return a
    outs = []
    outs.append(stk("y_prompt", pcores))
    outs.append(stk("y_sample", scores))
    for nm in _PROMPT_OUT:
        a = np.moveaxis(stk(nm, pcores), 0, 1)
        if nm.startswith("sb_"):
            a = a.reshape(a.shape[0], a.shape[1], S, NH, 128)
        if nm.startswith("mem_"):
            a = a.reshape(a.shape[0], a.shape[1], NMEM, 4, 512)
        outs.append(np.ascontiguousarray(a))
    for nm in _SAMPLE_OUT:
        a = np.moveaxis(stk(nm, scores), 0, 1)
        if nm.startswith("sb_"):
            a = a.reshape(a.shape[0], a.shape[1], R, NH, 128)
        outs.append(np.ascontiguousarray(a))
    return tuple(outs)


def kernel(**inputs):
    B, S = inputs["x_prompt"].shape[0], inputs["x_prompt"].shape[1]
    SBn, R = inputs["x_sample"].shape[0], inputs["x_sample"].shape[1]
    PAST = inputs["cache_sb_k"].shape[2]
    n = 8
    nc = build(dict(S=S, R=R, PAST=PAST, depth=DEPTH))
    in_maps = [make_in_map(inputs, c % B, c % SBn, R) for c in range(n)]
    res = run_bass_kernel_spmd(nc, in_maps, core_ids=list(range(n)))
    return assemble(res.results, list(range(B)), list(range(SBn)), S, R)
```
